# Optimizing a Trainium2 kernel written in Bass

```python
import jax
import jax.numpy as jnp
from jax import lax
import numpy as np


D_MODEL = 2048
BATCH = 8
SEQ = 2048
DEPTH = 2

RMS_EPS = 1e-6
PLE_DIM = 256
SB_HEADS = 8
SB_HEAD_DIM = 128
SB_BLOCK = 128
SB_WIDTH = SB_HEADS * SB_HEAD_DIM
HG_HEADS = 8
HG_KEY_DIM = 128
HG_VAL_DIM = 128
HG_CHUNK = 64
HG_KEY_WIDTH = HG_HEADS * HG_KEY_DIM
HG_VAL_WIDTH = HG_HEADS * HG_VAL_DIM
EVEN_SPLITS = [SB_WIDTH, SB_WIDTH, SB_WIDTH, HG_KEY_WIDTH, HG_KEY_WIDTH, HG_VAL_WIDTH, HG_VAL_WIDTH]
EVEN_IN_WIDTH = sum(EVEN_SPLITS)
EVEN_OUT_WIDTH = SB_WIDTH + HG_VAL_WIDTH
LRU_WIDTH = ((4 * D_MODEL // 3 + 255) // 256) * 256
RG_BLOCK_WIDTH = 256
RG_BLOCKS = LRU_WIDTH // RG_BLOCK_WIDTH
CONV_WIDTH = 4
RG_C = 8.0
D_FF = ((8 * D_MODEL // 3 + 255) // 256) * 256
N_EVEN = (DEPTH + 1) // 2
N_ODD = DEPTH // 2

kernel_name = "hybrid_stickbreak_hgrn2_rglru_block"


def rmsnorm(x, g):
    xf = x.astype(jnp.float32)
    y = xf * lax.rsqrt(jnp.mean(xf * xf, axis=-1, keepdims=True) + RMS_EPS)
    return (y * g.astype(jnp.float32)).astype(x.dtype)


def split_heads(a, n_heads, head_dim):
    b, s, _ = a.shape
    return a.reshape(b, s, n_heads, head_dim).transpose(0, 2, 1, 3)


def merge_heads(a):
    b, h, s, d = a.shape
    return a.transpose(0, 2, 1, 3).reshape(b, s, h * d)


def stick_breaking_attention(q, k, v):
    seq = q.shape[2]
    qf = q.astype(jnp.float32) * (SB_HEAD_DIM ** -0.5)
    kf = k.astype(jnp.float32)
    vf = v.astype(jnp.float32)
    outs = []
    for blk in range(seq // SB_BLOCK):
        q0 = blk * SB_BLOCK
        q1 = q0 + SB_BLOCK
        z = jnp.einsum('bhtd,bhsd->bhts', qf[:, :, q0:q1], kf[:, :, :q1])
        mask = jnp.arange(q1)[None, :] < (q0 + jnp.arange(SB_BLOCK))[:, None]
        log_beta = jax.nn.log_sigmoid(z)
        log_one_minus = jnp.where(mask, log_beta - z, 0.0)
        log_remain = lax.cumsum(log_one_minus, axis=3, reverse=True) - log_one_minus
        w = jnp.where(mask, jnp.exp(log_beta + log_remain), 0.0)
        outs.append(jnp.einsum('bhts,bhsd->bhtd', w, vf[:, :, :q1]))
    return jnp.concatenate(outs, axis=2).astype(q.dtype)


def hgrn2_chunkwise(q, k, v, log_f):
    b, h, s, dk = q.shape
    dv = v.shape[-1]
    n_chunks = s // HG_CHUNK

    def to_chunks(a):
        return a.reshape(b, h, n_chunks, HG_CHUNK, a.shape[-1]).transpose(2, 0, 1, 3, 4)

    causal = (jnp.arange(HG_CHUNK)[:, None] >= jnp.arange(HG_CHUNK)[None, :])[:, :, None]

    def step(state, inp):
        qi, ki, vi, gi = inp
        cum = jnp.cumsum(gi, axis=2)
        o_inter = jnp.einsum('bhtk,bhkv->bhtv', qi * jnp.exp(cum), state)
        diff = cum[:, :, :, None, :] - cum[:, :, None, :, :]
        decay = jnp.exp(jnp.where(causal, diff, -jnp.inf))
        scores = jnp.einsum('bhtk,bhsk,bhtsk->bhts', qi, ki, decay)
        o_intra = jnp.einsum('bhts,bhsv->bhtv', scores, vi)
        last = cum[:, :, -1:, :]
        k_dec = ki * jnp.exp(last - cum)
        new_state = state * jnp.exp(last[:, :, 0, :, None]) + jnp.einsum('bhsk,bhsv->bhkv', k_dec, vi)
        return new_state, o_inter + o_intra

    state0 = jnp.zeros((b, h, dk, dv), jnp.float32)
    _, o = lax.scan(step, state0, (to_chunks(q), to_chunks(k), to_chunks(v), to_chunks(log_f)))
    return o.transpose(1, 2, 0, 3, 4).reshape(b, h, s, dv)


def hgrn2(hq, hf, hi, hg, lb, norm_g):
    lbf = lb.astype(jnp.float32)
    f = lbf + (1.0 - lbf) * jax.nn.sigmoid(hf.astype(jnp.float32))
    q = jax.nn.silu(hq.astype(jnp.float32))
    o = hgrn2_chunkwise(split_heads(q, HG_HEADS, HG_KEY_DIM),
                        split_heads(1.0 - f, HG_HEADS, HG_KEY_DIM),
                        split_heads(hi.astype(jnp.float32), HG_HEADS, HG_VAL_DIM),
                        split_heads(jnp.log(f), HG_HEADS, HG_KEY_DIM))
    o = rmsnorm(o, norm_g)
    return (merge_heads(o) * jax.nn.silu(hg.astype(jnp.float32))).astype(hq.dtype)


def even_mixer(h, w_in, w_out, lb, hg_norm_g):
    sq, sk, sv, hq, hf, hi, hg = jnp.split(h @ w_in, np.cumsum(EVEN_SPLITS)[:-1].tolist(), axis=-1)
    a_out = merge_heads(stick_breaking_attention(split_heads(sq, SB_HEADS, SB_HEAD_DIM),
                                                 split_heads(sk, SB_HEADS, SB_HEAD_DIM),
                                                 split_heads(sv, SB_HEADS, SB_HEAD_DIM)))
    b_out = hgrn2(hq, hf, hi, hg, lb, hg_norm_g)
    return jnp.concatenate([a_out.astype(h.dtype), b_out.astype(h.dtype)], axis=-1) @ w_out


def causal_depthwise_conv(x, w, b):
    s = x.shape[1]
    xp = jnp.pad(x, ((0, 0), (CONV_WIDTH - 1, 0), (0, 0)))
    y = b
    for tap in range(CONV_WIDTH):
        y = y + xp[:, CONV_WIDTH - 1 - tap:CONV_WIDTH - 1 - tap + s] * w[tap]
    return y


def block_diag_linear(x, w, b):
    bsz, s, _ = x.shape
    xb = x.reshape(bsz, s, RG_BLOCKS, RG_BLOCK_WIDTH)
    return (jnp.einsum('bsni,nio->bsno', xb, w) + b).reshape(bsz, s, LRU_WIDTH)


def rg_lru(x, wa, ba, wx, bx, lam):
    s = x.shape[1]
    r = jax.nn.sigmoid(block_diag_linear(x, wa, ba).astype(jnp.float32))
    i = jax.nn.sigmoid(block_diag_linear(x, wx, bx).astype(jnp.float32))
    log_a = -RG_C * r * jax.nn.softplus(-lam.astype(jnp.float32))
    a = jnp.exp(log_a)
    mult = jnp.sqrt(-jnp.expm1(2.0 * log_a))
    mult = jnp.where((jnp.arange(s) == 0)[None, :, None], 1.0, mult)
    u = x.astype(jnp.float32) * i * mult

    def combine(left, right):
        a1, b1 = left
        a2, b2 = right
        return a1 * a2, a2 * b1 + b2

    _, hs = lax.associative_scan(combine, (a, u), axis=1)
    return hs.astype(x.dtype)


def odd_mixer(h, w_in, conv_w, conv_b, wa, ba, wx, bx, lam, w_out):
    gate_branch, x_branch = jnp.split(h @ w_in, 2, axis=-1)
    y = rg_lru(causal_depthwise_conv(x_branch, conv_w, conv_b), wa, ba, wx, bx, lam)
    return (jax.nn.gelu(gate_branch) * y) @ w_out


def swiglu(h, w_gate_up, w_down):
    g, u = jnp.split(h @ w_gate_up, 2, axis=-1)
    return (jax.nn.silu(g) * u) @ w_down


def per_layer_embedding(h, p_i, w_up, w_gate, g):
    e = p_i @ w_up
    gate = jax.nn.sigmoid(h @ w_gate)
    return rmsnorm(gate * e, g)


def setup_inputs(seed: int = 0) -> dict:
    key = jax.random.key(seed)
    ks = jax.random.split(key, 24)

    def nrm(i, shape, scale):
        return scale * jax.random.normal(ks[i], shape, jnp.float32)

    u = jax.random.uniform(ks[18], (N_ODD, LRU_WIDTH), jnp.float32, 0.9, 0.999)
    a0 = u ** (1.0 / RG_C)
    rg_lambda = jnp.log(a0) - jnp.log1p(-a0)
    return {
        'x': nrm(0, (BATCH, SEQ, D_MODEL), 1.0),
        'p': nrm(1, (DEPTH, BATCH, SEQ, PLE_DIM), 1.0),
        'mix_pre_g': 1.0 + nrm(2, (DEPTH, D_MODEL), 0.05),
        'mix_post_g': 1.0 + nrm(3, (DEPTH, D_MODEL), 0.05),
        'ffn_pre_g': 1.0 + nrm(4, (DEPTH, D_MODEL), 0.05),
        'ffn_post_g': 1.0 + nrm(5, (DEPTH, D_MODEL), 0.05),
        'ple_norm_g': 1.0 + nrm(6, (DEPTH, D_MODEL), 0.05),
        'w_in_even': nrm(7, (N_EVEN, D_MODEL, EVEN_IN_WIDTH), D_MODEL ** -0.5),
        'w_out_even': nrm(8, (N_EVEN, EVEN_OUT_WIDTH, D_MODEL), EVEN_OUT_WIDTH ** -0.5),
        'hg_lb_logits': nrm(9, (N_EVEN + 1, HG_KEY_WIDTH), 0.5),
        'hg_norm_g': 1.0 + nrm(10, (N_EVEN, HG_VAL_DIM), 0.05),
        'w_in_odd': nrm(11, (N_ODD, D_MODEL, 2 * LRU_WIDTH), D_MODEL ** -0.5),
        'conv_w': nrm(12, (N_ODD, CONV_WIDTH, LRU_WIDTH), CONV_WIDTH ** -0.5),
        'conv_b': nrm(13, (N_ODD, LRU_WIDTH), 0.02),
        'rg_wa': nrm(14, (N_ODD, RG_BLOCKS, RG_BLOCK_WIDTH, RG_BLOCK_WIDTH), RG_BLOCK_WIDTH ** -0.5),
        'rg_ba': nrm(15, (N_ODD, RG_BLOCKS, RG_BLOCK_WIDTH), 0.02),
        'rg_wx': nrm(16, (N_ODD, RG_BLOCKS, RG_BLOCK_WIDTH, RG_BLOCK_WIDTH), RG_BLOCK_WIDTH ** -0.5),
        'rg_bx': nrm(17, (N_ODD, RG_BLOCKS, RG_BLOCK_WIDTH), 0.02),
        'rg_lambda': rg_lambda,
        'w_out_odd': nrm(19, (N_ODD, LRU_WIDTH, D_MODEL), LRU_WIDTH ** -0.5),
        'w_gate_up': nrm(20, (DEPTH, D_MODEL, 2 * D_FF), D_MODEL ** -0.5),
        'w_down': nrm(21, (DEPTH, D_FF, D_MODEL), D_FF ** -0.5),
        'w_ple_up': nrm(22, (DEPTH, PLE_DIM, D_MODEL), PLE_DIM ** -0.5),
        'w_ple_gate': nrm(23, (DEPTH, D_MODEL, D_MODEL), D_MODEL ** -0.5),
    }


def reference(x, p, mix_pre_g, mix_post_g, ffn_pre_g, ffn_post_g, ple_norm_g,
              w_in_even, w_out_even, hg_lb_logits, hg_norm_g,
              w_in_odd, conv_w, conv_b, rg_wa, rg_ba, rg_wx, rg_bx, rg_lambda, w_out_odd,
              w_gate_up, w_down, w_ple_up, w_ple_gate):
    lb_all = jnp.cumsum(jax.nn.softmax(hg_lb_logits.astype(jnp.float32), axis=0), axis=0)
    h = x
    for i in range(DEPTH):
        j = i // 2
        n = rmsnorm(h, mix_pre_g[i])
        if i % 2 == 0:
            m = even_mixer(n, w_in_even[j], w_out_even[j], lb_all[j], hg_norm_g[j])
        else:
            m = odd_mixer(n, w_in_odd[j], conv_w[j], conv_b[j], rg_wa[j], rg_ba[j],
                          rg_wx[j], rg_bx[j], rg_lambda[j], w_out_odd[j])
        h = h + rmsnorm(m, mix_post_g[i])
        h = h + rmsnorm(swiglu(rmsnorm(h, ffn_pre_g[i]), w_gate_up[i], w_down[i]), ffn_post_g[i])
        h = h + per_layer_embedding(h, p[i], w_ple_up[i], w_ple_gate[i], ple_norm_g[i])
    return h
```

```python
import contextlib
import numpy as np
import concourse.bass as bass
import concourse.mybir as mybir
from concourse.bass_utils import run_bass_kernel_spmd

F32 = mybir.dt.float32
F32R = mybir.dt.float32r
BF16 = mybir.dt.bfloat16
AF = mybir.ActivationFunctionType
ALU = mybir.AluOpType

D = 2048
T = 2048
TT = 512
NTT = T // TT
KC = D // 128
DFF = 5632
FC = DFF // 128
LRU = 2816
LC = LRU // 128
PLE = 256
EPS = 1e-6
WSLOT = 8192


def _wl(w, cb):
    K, N = w.shape
    return np.ascontiguousarray(w.reshape(K // 128, 128, N // cb, cb).transpose(2, 1, 0, 3))


def _cols(v):
    return np.ascontiguousarray(v.reshape(-1, 128).T)


PRM = {}


def _prm_layout():
    off = 0
    def add(name, n):
        nonlocal off
        PRM[name] = (off, n)
        off += n
    for i in range(2):
        for nm in ("mix_pre_g", "mix_post_g", "ffn_pre_g", "ffn_post_g", "ple_norm_g"):
            add(f"{nm}{i}", 16)
    add("lb0", 8); add("lb1", 8); add("hgn", 1)
    for tap in range(4):
        add(f"convw{tap}", LC)
    add("convb", LC); add("ba", LC); add("bx", LC); add("lam", LC)
    return off


NPRM = _prm_layout()


def _consts():
    c = {}
    s = np.arange(128)[:, None]; t = np.arange(128)[None, :]
    c["ident"] = np.eye(128, dtype=np.float32)
    c["ones"] = np.ones((128, 128), np.float32)
    c["zeros"] = np.zeros((128, 512), np.float32)
    c["maskneg"] = np.where(s >= t, -30000.0, 0.0).astype(np.float32)
    c["mask01"] = (s <= t).astype(np.float32)
    c["negU"] = np.where(s >= t, -1.0, 0.0).astype(np.float32)
    c["negOnes"] = -np.ones((128, 128), np.float32)
    tt = np.arange(T)
    c["rm128"] = np.broadcast_to((tt % 128 != 0).astype(np.float32), (128, T)).copy()
    c["rm16"] = np.broadcast_to((tt % 16 != 0).astype(np.float32), (128, T)).copy()
    return c


class Tl:
    def __init__(self, name, ap):
        self.name = name; self.ap = ap
        self.w = {}; self.r = {}
        self.dsem = None; self.dcnt = 0

    def __getitem__(self, k):
        return self.ap[k]


class Eng:
    def __init__(self, name, eng, sem):
        self.name = name; self.eng = eng; self.sem = sem; self.cnt = 0
        self.waited = {}; self.pend = []


class Kern:
    def __init__(self, nc, es):
        self.nc = nc; self.es = es
        self.pe = Eng("pe", nc.tensor, es.enter_context(nc.semaphore("s_pe")))
        self.act = Eng("act", nc.scalar, es.enter_context(nc.semaphore("s_act")))
        self.dve = Eng("dve", nc.vector, es.enter_context(nc.semaphore("s_dve")))
        self.pool = Eng("pool", nc.gpsimd, es.enter_context(nc.semaphore("s_pool")))
        self.sp = Eng("sp", nc.sync, es.enter_context(nc.semaphore("s_sp")))
        self.engs = [self.pe, self.act, self.dve, self.pool, self.sp]
        self.dsems = []
        self.nsem = 5

    def sb(self, stack, name, shape, dt):
        return Tl(name, stack.enter_context(self.nc.sbuf_tensor(name, list(shape), dt)))

    def psum(self, stack, name, shape, dt=F32):
        return Tl(name, stack.enter_context(self.nc.psum_tensor(name, list(shape), dt)))

    def dram(self, name, shape, dt, kind="Internal"):
        return Tl(name, self.nc.dram_tensor(name, list(shape), dt, kind=kind).ap())

    def _wait(self, E, ev):
        for sid, (sem, val) in ev.items():
            if E is self.pe and sid == id(E.sem):
                continue
            if E.waited.get(sid, 0) < val:
                E.eng.wait_ge(sem, val)
                E.waited[sid] = val

    def _deps(self, E, reads, writes):
        ev = {}
        def mrg(d):
            for sid, (sem, val) in d.items():
                if sid not in ev or ev[sid][1] < val:
                    ev[sid] = (sem, val)
        for t in reads:
            mrg(t.w)
        for t in writes:
            mrg(t.w); mrg(t.r)
        self._wait(E, ev)

    def _record(self, ev, reads, writes):
        sid, sem, val = ev
        for t in writes:
            t.w = {sid: (sem, val)}; t.r = {}
        for t in reads:
            t.r[sid] = (sem, val)

    def op(self, E, fn, reads=(), writes=(), sig=True):
        self._deps(E, reads, writes)
        inst = fn()
        if sig:
            E.cnt += 1
            inst.then_inc(E.sem, 1)
            ev = (id(E.sem), E.sem, E.cnt)
            self._record(ev, reads, writes)
            for (r, w) in E.pend:
                self._record(ev, r, w)
            E.pend = []
        else:
            ev = (id(E.sem), E.sem, E.cnt + 1)
            self._record(ev, reads, writes)
        return inst

    def dma(self, Q, out_ap, in_ap, reads, writes, owner):
        self._deps(Q, reads, writes)
        if owner.dsem is None:
            owner.dsem = self.es.enter_context(self.nc.semaphore("d_" + owner.name))
            self.dsems.append(owner); self.nsem += 1
        owner.dcnt += 16
        Q.eng.dma_start(out=out_ap, in_=in_ap).then_inc(owner.dsem, 16)
        ev = (id(owner.dsem), owner.dsem, owner.dcnt)
        sid, sem, val = ev
        for t in writes:
            t.w = {sid: (sem, val)}; t.r = {}
        for t in reads:
            t.r[sid] = (sem, val)

    def barrier(self):
        ev = {}
        for E in self.engs:
            if E.cnt:
                ev[id(E.sem)] = (E.sem, E.cnt)
        for t in self.dsems:
            ev[id(t.dsem)] = (t.dsem, t.dcnt)
        for E in self.engs:
            self._wait(E, ev)

    def release(self, tiles):
        self.dsems = [t for t in self.dsems if t not in tiles]


class Ctx:
    pass


def wview(slot, kc, cb):
    return slot.ap[:, 0:kc * cb].rearrange("p (k c) -> p k c", c=cb)


def load_w(K, C, wl_dram, nb, kc, cb):
    slot = C.wslots[C.wi % len(C.wslots)]; C.wi += 1
    K.dma(K.pool, wview(slot, kc, cb), wl_dram.ap[nb], reads=[wl_dram], writes=[slot], owner=slot)
    return slot


def mm_group(K, C, ps_t, ps_ap, pairs, reads):
    n = len(pairs)
    for i, (l, r) in enumerate(pairs):
        K.op(K.pe, (lambda l=l, r=r, i=i: K.nc.tensor.matmul(ps_ap, l, r, start=(i == 0), stop=(i == n - 1))),
             reads=reads, writes=[ps_t], sig=(i == n - 1))


def next_ps(C):
    t = C.ps[C.pi % len(C.ps)]; C.pi += 1
    return t


def rstd_from_sq(K, C, src_list, scale, out_rstd):
    ps = next_ps(C)
    n = len(src_list)
    for i, (t, ap) in enumerate(src_list):
        sq = C.sq[C.sqi % 2]; C.sqi += 1
        K.op(K.act, (lambda ap=ap, sq=sq: K.nc.scalar.activation(out=sq.ap[:], in_=ap, func=AF.Square)), reads=[t], writes=[sq])
        K.op(K.pe, (lambda sq=sq, i=i: K.nc.tensor.matmul(ps.ap[:], C.ones.ap[:], sq.ap[:], start=(i == 0), stop=(i == n - 1))),
             reads=[sq, C.ones], writes=[ps], sig=True)
    K.op(K.act, lambda: K.nc.scalar.activation(out=out_rstd.ap[:], in_=ps.ap[:], func=AF.Sqrt, bias=C.epsb.ap[:, 0:1], scale=scale),
         reads=[ps, C.epsb], writes=[out_rstd])
    K.op(K.dve, lambda: K.nc.vector.reciprocal(out=out_rstd.ap[:], in_=out_rstd.ap[:]), reads=[out_rstd], writes=[out_rstd])


def prm(C, name, j=None):
    off, n = PRM[name]
    if j is None:
        return C.prm.ap[:, off:off + n]
    return C.prm.ap[:, off + j:off + j + 1]


def norm_bf16(K, C, src, gname, dst):
    rstd_from_sq(K, C, [(src, src.ap[:, kc, :]) for kc in range(KC)], 1.0 / D, C.rstd)
    for kc in range(KC):
        K.op(K.dve, (lambda kc=kc: K.nc.vector.scalar_tensor_tensor(out=dst.ap[:, kc, :], in0=src.ap[:, kc, :], scalar=prm(C, gname, kc),
                                                                   in1=C.rstd.ap[:], op0=ALU.mult, op1=ALU.mult)),
             reads=[src, C.rstd, C.prm], writes=[dst])


def post_norm_res(K, C, y, gname, h):
    rstd_from_sq(K, C, [(y, y.ap[:, kc, :]) for kc in range(KC)], 1.0 / D, C.rstd)
    for kc in range(KC):
        K.op(K.dve, (lambda kc=kc: K.nc.vector.scalar_tensor_tensor(out=y.ap[:, kc, :], in0=y.ap[:, kc, :], scalar=prm(C, gname, kc),
                                                                   in1=C.rstd.ap[:], op0=ALU.mult, op1=ALU.mult)),
             reads=[y, C.rstd, C.prm], writes=[y])
    for kc in range(KC):
        K.op(K.pool, (lambda kc=kc: K.nc.gpsimd.tensor_tensor(out=h.ap[:, kc, :], in0=h.ap[:, kc, :], in1=y.ap[:, kc, :], op=ALU.add)),
             reads=[y, h], writes=[h])


def linear_fm(K, C, inT, kcn, wl_dram, ncols, cb, epi):
    nblk = ncols // cb
    for nb in range(nblk):
        slot = load_w(K, C, wl_dram, nb, kcn, cb)
        wv = wview(slot, kcn, cb)
        for ci in range(cb // 128):
            ps = next_ps(C)
            mm_group(K, C, ps, ps.ap[:], [(wv[:, kc, ci * 128:(ci + 1) * 128], inT.ap[:, kc, :]) for kc in range(kcn)], reads=[slot, inT])
            epi(nb * (cb // 128) + ci, ps)


def stage_out(C, dt):
    lst = C.stg32 if dt == F32 else C.stg16
    t = lst[C.stgi[dt] % len(lst)]; C.stgi[dt] += 1
    return t


def build(stage=99, dbg=()):
    nc = bass.Bass("TRN2", target_bir_lowering=False)
    es = contextlib.ExitStack()
    K = Kern(nc, es)
    C = Ctx()
    dr = {}

    def ein(name, shape, dt=F32):
        dr[name] = K.dram(name, shape, dt, kind="ExternalInput"); return dr[name]

    def scr(name, shape, dt, out=False):
        dr[name] = K.dram(name, shape, dt, kind=("ExternalOutput" if (out or name in dbg) else "Internal")); return dr[name]

    xT = ein("xT", [D, T]); pT = ein("pT", [2, PLE, T]); prm_d = ein("prm", [128, NPRM])
    cst = {k: ein("c_" + k, list(v.shape)) for k, v in _consts().items()}
    w_sq = ein("w_sq", [2, 128, KC, 512]); w_sk = ein("w_sk", [2, 128, KC, 512]); w_sv = ein("w_sv", [2, 128, KC, 512])
    w_hq = ein("w_hq", [2, 128, KC, 512]); w_hf = ein("w_hf", [2, 128, KC, 512]); w_hi = ein("w_hi", [2, 128, KC, 512])
    w_hg = ein("w_hg", [2, 128, KC, 512])
    w_oe = ein("w_oe", [4, 128, KC, 512])
    w_io = ein("w_io", [2 * LRU // 512, 128, KC, 512])
    w_oo = ein("w_oo", [D // 256, 128, LC, 256])
    w_gu = [ein(f"w_gu{i}", [FC // 2, 128, KC, 512]) for i in range(2)]
    w_dn = [ein(f"w_dn{i}", [D // 128, 128, FC, 128]) for i in range(2)]
    w_pg = [ein(f"w_pg{i}", [4, 128, KC, 512]) for i in range(2)]
    w_pu = [ein(f"w_pu{i}", [1, 128, 2, 2048]) for i in range(2)]
    w_ra = ein("w_ra", [LRU // 256, 128, 2, 256]); w_rx = ein("w_rx", [LRU // 256, 128, 2, 256])

    outT = scr("outT", [D, T], F32, out=True)
    hT = scr("hT", [D, T], F32)
    qT = scr("qT", [1024, T], BF16); kT = scr("kT", [1024, T], BF16); vtok = scr("vtok", [8, T, 128], BF16)
    hqT = scr("hqT", [1024, T], F32); lfT = scr("lfT", [1024, T], F32); kkT = scr("kkT", [1024, T], F32)
    hgT = scr("hgT", [1024, T], BF16); hitok = scr("hitok", [8, T, 128], BF16)
    ccT = scr("ccT", [D, T], BF16)
    gbT = scr("gbT", [LRU, T], F32); xbT = scr("xbT", [LRU, T], F32)
    mxT = scr("mxT", [LRU, T], BF16)

    dbgt = {nm: scr(nm, [D, T], F32) for nm in ("dbg_m", "dbg_hmix", "dbg_hffn") if nm in dbg}
    def dbg_store(nm, tile, t0, tag):
        if nm in dbgt and tag == "p1":
            K.dma(K.sp, dbgt[nm].ap.rearrange("(k p) t -> p k t", p=128)[:, :, t0:t0 + TT], tile.ap[:], reads=[tile], writes=[dbgt[nm]], owner=tile)

    def kcv(d, t0):
        return d.ap.rearrange("(k p) t -> p k t", p=128)[:, :, t0:t0 + TT]

    gs = contextlib.ExitStack()
    C.prm = K.sb(gs, "prm_sb", [128, NPRM], F32)
    C.ones = K.sb(gs, "ones_sb", [128, 128], BF16)
    C.ident = K.sb(gs, "ident_sb", [128, 128], BF16)
    C.epsb = K.sb(gs, "epsb", [128, 1], F32)
    C.drv = K.sb(gs, "drv", [128, 64], F32)
    K.dma(K.sp, C.prm.ap[:], prm_d.ap[:, :], reads=[prm_d], writes=[C.prm], owner=C.prm)
    K.dma(K.pool, C.ones.ap[:], cst["ones"].ap[:, :], reads=[], writes=[C.ones], owner=C.ones)
    K.dma(K.pool, C.ident.ap[:], cst["ident"].ap[:, :], reads=[], writes=[C.ident], owner=C.ident)
    K.op(K.dve, lambda: nc.vector.memset(C.epsb.ap[:], EPS), writes=[C.epsb])
    o0, _ = PRM["lb0"]; o1, _ = PRM["lb1"]
    K.op(K.dve, lambda: nc.vector.tensor_tensor(out=C.drv.ap[:, 0:8], in0=C.prm.ap[:, o0:o0 + 8], in1=C.prm.ap[:, o1:o1 + 8], op=ALU.subtract),
         reads=[C.prm], writes=[C.drv])
    K.op(K.act, lambda: nc.scalar.activation(out=C.drv.ap[:, 0:8], in_=C.drv.ap[:, 0:8], func=AF.Sigmoid), reads=[C.drv], writes=[C.drv])
    K.op(K.dve, lambda: nc.vector.tensor_scalar(out=C.drv.ap[:, 8:16], in0=C.drv.ap[:, 0:8], scalar1=-1.0, scalar2=1.0, op0=ALU.mult, op1=ALU.add),
         reads=[C.drv], writes=[C.drv])
    K.op(K.dve, lambda: nc.vector.tensor_scalar(out=C.drv.ap[:, 16:24], in0=C.drv.ap[:, 8:16], scalar1=-1.0, scalar2=None, op0=ALU.mult),
         reads=[C.drv], writes=[C.drv])
    ol, _ = PRM["lam"]
    K.op(K.act, lambda: nc.scalar.activation(out=C.drv.ap[:, 24:46], in_=C.prm.ap[:, ol:ol + LC], func=AF.Exp, scale=-1.0), reads=[C.prm], writes=[C.drv])
    K.op(K.act, lambda: nc.scalar.activation(out=C.drv.ap[:, 24:46], in_=C.drv.ap[:, 24:46], func=AF.Ln, bias=1.0), reads=[C.drv], writes=[C.drv])
    K.op(K.dve, lambda: nc.vector.tensor_scalar(out=C.drv.ap[:, 24:46], in0=C.drv.ap[:, 24:46], scalar1=-8.0, scalar2=None, op0=ALU.mult),
         reads=[C.drv], writes=[C.drv])

    def token_phase(tag, layer, do_mix_ffn, inproj):
        ps_stack = contextlib.ExitStack()
        C.wslots = [K.sb(ps_stack, f"ws{i}_{tag}", [128, WSLOT], BF16) for i in range(3)]; C.wi = 0
        C.ps = [K.psum(ps_stack, f"ps{i}_{tag}", [128, TT]) for i in range(8)]; C.pi = 0
        C.sq = [K.sb(ps_stack, f"sq{i}_{tag}", [128, TT], BF16) for i in range(2)]; C.sqi = 0
        C.rstd = K.sb(ps_stack, f"rstd_{tag}", [128, TT], F32)
        C.stg32 = [K.sb(ps_stack, f"st32_{i}_{tag}", [128, TT], F32) for i in range(3)]
        C.stg16 = [K.sb(ps_stack, f"st16_{i}_{tag}", [128, TT], BF16) for i in range(3)]
        C.stgi = {F32: 0, BF16: 0}
        h = K.sb(ps_stack, f"h_{tag}", [128, KC, TT], F32)
        nT = K.sb(ps_stack, f"nT_{tag}", [128, KC, TT], BF16)
        if do_mix_ffn:
            act = K.sb(ps_stack, f"act_{tag}", [128, FC, TT], BF16)
            y = K.sb(ps_stack, f"y_{tag}", [128, KC, TT], F32)
            pt = K.sb(ps_stack, f"pt_{tag}", [128, 2, TT], BF16)
            tmp = [K.sb(ps_stack, f"tmp{i}_{tag}", [128, TT], F32) for i in range(2)]
            wpu = K.sb(ps_stack, f"wpu_{tag}", [128, 2 * D], BF16)
            K.dma(K.pool, wview(wpu, 2, D), w_pu[layer].ap[0], reads=[w_pu[layer]], writes=[wpu], owner=wpu)
        h_src = xT if layer == 0 else hT
        for tt in range(NTT):
            t0 = tt * TT
            K.dma(K.sp, h.ap[:], kcv(h_src, t0), reads=[h_src], writes=[h], owner=h)
            if do_mix_ffn:
                if layer == 0:
                    cc, ckc, wo, wcb = ccT, KC, w_oe, 512
                else:
                    cc, ckc, wo, wcb = mxT, LC, w_oo, 256
                K.dma(K.sp, act.ap[:, 0:ckc, :], kcv(cc, t0), reads=[cc], writes=[act], owner=act)
                def epi_y(c, ps):
                    K.op(K.act, lambda: nc.scalar.copy(out=y.ap[:, c, :], in_=ps.ap[:]), reads=[ps], writes=[y])
                linear_fm(K, C, act, ckc, wo, D, wcb, epi_y)
                dbg_store("dbg_m", y, t0, tag)
                post_norm_res(K, C, y, f"mix_post_g{layer}", h)
                dbg_store("dbg_hmix", h, t0, tag)
                norm_bf16(K, C, h, f"ffn_pre_g{layer}", nT)
                for nb in range(FC // 2):
                    slot = load_w(K, C, w_gu[layer], nb, KC, 512)
                    wv = wview(slot, KC, 512)
                    for ci in range(2):
                        psg = next_ps(C); psu = next_ps(C)
                        mm_group(K, C, psg, psg.ap[:], [(wv[:, kc, ci * 128:(ci + 1) * 128], nT.ap[:, kc, :]) for kc in range(KC)], reads=[slot, nT])
                        mm_group(K, C, psu, psu.ap[:], [(wv[:, kc, 256 + ci * 128:256 + (ci + 1) * 128], nT.ap[:, kc, :]) for kc in range(KC)], reads=[slot, nT])
                        tm = tmp[(nb * 2 + ci) % 2]; fc = nb * 2 + ci
                        K.op(K.act, lambda: nc.scalar.activation(out=tm.ap[:], in_=psg.ap[:], func=AF.Silu), reads=[psg], writes=[tm])
                        K.op(K.dve, lambda: nc.vector.tensor_tensor(out=act.ap[:, fc, :], in0=tm.ap[:], in1=psu.ap[:], op=ALU.mult), reads=[tm, psu], writes=[act])
                linear_fm(K, C, act, FC, w_dn[layer], D, 128, epi_y)
                post_norm_res(K, C, y, f"ffn_post_g{layer}", h)
                dbg_store("dbg_hffn", h, t0, tag)
                for kc in range(KC):
                    K.op(K.act, (lambda kc=kc: nc.scalar.copy(out=nT.ap[:, kc, :], in_=h.ap[:, kc, :])), reads=[h], writes=[nT])
                K.dma(K.pool, pt.ap[:], pT.ap[layer].rearrange("(k p) t -> p k t", p=128)[:, :, t0:t0 + TT], reads=[pT], writes=[pt], owner=pt)
                uslot = wpu
                uv = wview(wpu, 2, D)
                for nb in range(4):
                    slot = load_w(K, C, w_pg[layer], nb, KC, 512)
                    wv = wview(slot, KC, 512)
                    for ci in range(4):
                        c = nb * 4 + ci
                        psg = next_ps(C); pse = next_ps(C)
                        mm_group(K, C, psg, psg.ap[:], [(wv[:, kc, ci * 128:(ci + 1) * 128], nT.ap[:, kc, :]) for kc in range(KC)], reads=[slot, nT])
                        mm_group(K, C, pse, pse.ap[:], [(uv[:, k2, c * 128:(c + 1) * 128], pt.ap[:, k2, :]) for k2 in range(2)], reads=[uslot, pt])
                        tm = tmp[c % 2]
                        K.op(K.act, lambda: nc.scalar.activation(out=tm.ap[:], in_=psg.ap[:], func=AF.Sigmoid), reads=[psg], writes=[tm])
                        K.op(K.dve, lambda: nc.vector.tensor_tensor(out=y.ap[:, c, :], in0=tm.ap[:], in1=pse.ap[:], op=ALU.mult), reads=[tm, pse], writes=[y])
                post_norm_res(K, C, y, f"ple_norm_g{layer}", h)
                h_dst = hT if inproj is not None else outT
                K.dma(K.sp, kcv(h_dst, t0), h.ap[:], reads=[h], writes=[h_dst], owner=h)
            if inproj is not None:
                inproj(K, C, h, nT, t0)
        K.barrier()
        K.release(C.wslots + [h, nT] + ([act, pt, wpu] if do_mix_ffn else []))
        ps_stack.close()

    def inproj_even(K, C, h, nT, t0):
        norm_bf16(K, C, h, "mix_pre_g0", nT)
        def store(t, dst, c):
            K.dma(K.sp, dst.ap[c * 128:(c + 1) * 128, t0:t0 + TT], t.ap[:], reads=[t], writes=[dst], owner=t)
        def epi_q(c, ps):
            t = stage_out(C, BF16)
            K.op(K.act, lambda: nc.scalar.activation(out=t.ap[:], in_=ps.ap[:], func=AF.Copy, scale=128.0 ** -0.5), reads=[ps], writes=[t]); store(t, qT, c)
        def epi_k(c, ps):
            t = stage_out(C, BF16)
            K.op(K.act, lambda: nc.scalar.copy(out=t.ap[:], in_=ps.ap[:]), reads=[ps], writes=[t]); store(t, kT, c)
        def epi_hq(c, ps):
            t = stage_out(C, F32)
            K.op(K.act, lambda: nc.scalar.activation(out=t.ap[:], in_=ps.ap[:], func=AF.Silu), reads=[ps], writes=[t]); store(t, hqT, c)
        def epi_hg(c, ps):
            t = stage_out(C, BF16)
            K.op(K.act, lambda: nc.scalar.activation(out=t.ap[:], in_=ps.ap[:], func=AF.Silu), reads=[ps], writes=[t]); store(t, hgT, c)
        def epi_hf(c, ps):
            s = stage_out(C, F32); t1 = stage_out(C, F32); t2 = stage_out(C, F32)
            K.op(K.act, lambda: nc.scalar.activation(out=s.ap[:], in_=ps.ap[:], func=AF.Sigmoid), reads=[ps], writes=[s])
            K.op(K.act, lambda: nc.scalar.activation(out=t1.ap[:], in_=s.ap[:], func=AF.Ln, bias=C.drv.ap[:, c:c + 1], scale=C.drv.ap[:, 8 + c:9 + c]),
                 reads=[s, C.drv], writes=[t1]); store(t1, lfT, c)
            K.op(K.dve, lambda: nc.vector.tensor_scalar(out=t2.ap[:], in0=s.ap[:], scalar1=C.drv.ap[:, 16 + c:17 + c], scalar2=C.drv.ap[:, 8 + c:9 + c],
                                                        op0=ALU.mult, op1=ALU.add), reads=[s, C.drv], writes=[t2]); store(t2, kkT, c)
        linear_fm(K, C, nT, KC, w_sq, 1024, 512, epi_q)
        linear_fm(K, C, nT, KC, w_sk, 1024, 512, epi_k)
        linear_fm(K, C, nT, KC, w_hq, 1024, 512, epi_hq)
        linear_fm(K, C, nT, KC, w_hf, 1024, 512, epi_hf)
        linear_fm(K, C, nT, KC, w_hg, 1024, 512, epi_hg)
        for (wl, dst) in ((w_sv, vtok), (w_hi, hitok)):
            for nb in range(2):
                slot = load_w(K, C, wl, nb, KC, 512); wv = wview(slot, KC, 512)
                for tb in range(TT // 128):
                    ps = next_ps(C)
                    mm_group(K, C, ps, ps.ap[:], [(nT.ap[:, kc, tb * 128:(tb + 1) * 128], wv[:, kc, :]) for kc in range(KC)], reads=[slot, nT])
                    t = stage_out(C, BF16)
                    K.op(K.act, lambda: nc.scalar.copy(out=t.ap[:], in_=ps.ap[:]), reads=[ps], writes=[t])
                    K.dma(K.sp, dst.ap[nb * 4:(nb + 1) * 4, t0 + tb * 128:t0 + (tb + 1) * 128, :].rearrange("h t d -> t h d"),
                          t.ap[:].rearrange("t (h d) -> t h d", d=128), reads=[t], writes=[dst], owner=t)

    def inproj_odd(K, C, h, nT, t0):
        norm_bf16(K, C, h, "mix_pre_g1", nT)
        def epi(c, ps):
            t = stage_out(C, F32)
            K.op(K.act, lambda: nc.scalar.copy(out=t.ap[:], in_=ps.ap[:]), reads=[ps], writes=[t])
            dst, cc = (gbT, c) if c < LC else (xbT, c - LC)
            K.dma(K.sp, dst.ap[cc * 128:(cc + 1) * 128, t0:t0 + TT], t.ap[:], reads=[t], writes=[dst], owner=t)
        linear_fm(K, C, nT, KC, w_io, 2 * LRU, 512, epi)

    def attention_phase():
        st = contextlib.ExitStack()
        NS = 4
        cm = {}
        for nm, dt in (("maskneg", BF16), ("negU", F32R), ("negOnes", F32R)):
            cm[nm] = K.sb(st, "c_" + nm + "_sb", [128, 128], dt)
            K.dma(K.pool, cm[nm].ap[:], cst[nm].ap[:, :], reads=[], writes=[cm[nm]], owner=cm[nm])
        zer = K.sb(st, "zer_sb", [128, 512], BF16)
        K.dma(K.pool, zer.ap[:], cst["zeros"].ap[:, :], reads=[], writes=[zer], owner=zer)
        zer32 = K.sb(st, "zer32_sb", [128, 512], F32)
        K.dma(K.sp, zer32.ap[:], cst["zeros"].ap[:, :], reads=[], writes=[zer32], owner=zer32)
        S = []
        for s in range(NS):
            o = Ctx()
            o.q = K.sb(st, f"aq{s}", [128, T], BF16); o.k = K.sb(st, f"ak{s}", [128, T], BF16); o.v = K.sb(st, f"av{s}", [128, 16, 128], BF16)
            o.e = K.sb(st, f"ae{s}", [128, 512], F32)
            o.sp = [K.sb(st, f"asp{s}_{i}", [128, 512], F32R) for i in range(2)]
            o.A = K.sb(st, f"aA{s}", [128, 512], F32R)
            o.w = [K.sb(st, f"aw{s}_{i}", [128, 512], BF16) for i in range(2)]
            o.ob = K.sb(st, f"aob{s}", [128, 512], BF16)
            o.zps = K.psum(st, f"azps{s}", [128, 512]); o.ops = K.psum(st, f"aops{s}", [128, 512])
            S.append(o)
        for hg in range(8 // NS):
            for s, o in enumerate(S):
                hd = hg * NS + s
                K.dma(K.sp, o.q.ap[:], qT.ap[hd * 128:(hd + 1) * 128, :], reads=[qT], writes=[o.q], owner=o.q)
                K.dma(K.sp, o.k.ap[:], kT.ap[hd * 128:(hd + 1) * 128, :], reads=[kT], writes=[o.k], owner=o.k)
                K.dma(K.sp, o.v.ap[:], vtok.ap[hd].rearrange("(sb p) d -> p sb d", p=128), reads=[vtok], writes=[o.v], owner=o.v)
            step = 0
            for tq in range(4):
                nblk = 4 * tq + 4
                for o in S:
                    K.op(K.pe, (lambda o=o: nc.tensor.matmul(o.ops.ap[:], zer.ap[:, 0:128], zer.ap[:], start=True, stop=False)), reads=[zer], writes=[o.ops], sig=False)
                    K.op(K.dve, (lambda o=o: nc.vector.tensor_copy(out=o.A.ap[:], in_=zer32.ap[:])), reads=[zer32], writes=[o.A])
                for bi in range(nblk):
                    sb = nblk - 1 - bi
                    c0 = max(0, 128 * sb - 512 * tq); diag = 128 * sb >= 512 * tq
                    q0 = 512 * tq + c0
                    par = step % 2; step += 1
                    for o in S:
                        K.op(K.pe, (lambda o=o: nc.tensor.matmul(o.zps.ap[:, c0:512], o.k.ap[:, sb * 128:(sb + 1) * 128], o.q.ap[:, q0:512 * tq + 512],
                                                                 start=True, stop=False)), reads=[o.k, o.q], writes=[o.zps], sig=not diag)
                        if diag:
                            K.op(K.pe, (lambda o=o: nc.tensor.matmul(o.zps.ap[:, c0:c0 + 128], C.ident.ap[:], cm["maskneg"].ap[:], start=False, stop=False)),
                                 reads=[C.ident, cm["maskneg"]], writes=[o.zps], sig=True)
                    for o in S:
                        K.op(K.act, (lambda o=o: nc.scalar.activation(out=o.e.ap[:, c0:512], in_=o.zps.ap[:, c0:512], func=AF.Exp)), reads=[o.zps], writes=[o.e])
                        K.op(K.act, (lambda o=o: nc.scalar.activation(out=o.sp[par].ap[:, c0:512], in_=o.e.ap[:, c0:512], func=AF.Ln, bias=1.0)),
                             reads=[o.e], writes=[o.sp[par]])
                    for o in S:
                        last = (bi == 0)
                        K.op(K.pe, (lambda o=o: nc.tensor.matmul(o.zps.ap[:, c0:512], cm["negU"].ap[:], o.sp[par].ap[:, c0:512], start=False, stop=last)),
                             reads=[cm["negU"], o.sp[par]], writes=[o.zps], sig=last)
                        if not last:
                            K.op(K.pe, (lambda o=o: nc.tensor.matmul(o.zps.ap[:, c0:512], cm["negOnes"].ap[:], o.A.ap[:, c0:512], start=False, stop=True)),
                                 reads=[cm["negOnes"], o.A], writes=[o.zps], sig=True)
                    for o in S:
                        K.op(K.act, (lambda o=o: nc.scalar.activation(out=o.w[par].ap[:, c0:512], in_=o.zps.ap[:, c0:512], func=AF.Exp)),
                             reads=[o.zps], writes=[o.w[par]])
                        if sb > 0:
                            K.op(K.dve, (lambda o=o: nc.vector.tensor_tensor(out=o.A.ap[:, c0:512], in0=o.A.ap[:, c0:512].bitcast(F32), in1=o.sp[par].ap[:, c0:512].bitcast(F32), op=ALU.add)),
                                 reads=[o.A, o.sp[par]], writes=[o.A])
                    for o in S:
                        K.op(K.pe, (lambda o=o: nc.tensor.matmul(o.ops.ap[:, c0:512], o.v.ap[:, sb, :], o.w[par].ap[:, c0:512], start=False, stop=(sb == 0))),
                             reads=[o.v, o.w[par]], writes=[o.ops], sig=(sb == 0))
                for s, o in enumerate(S):
                    hd = hg * NS + s
                    K.op(K.dve, (lambda o=o: nc.vector.tensor_copy(out=o.ob.ap[:], in_=o.ops.ap[:])), reads=[o.ops], writes=[o.ob])
                    K.dma(K.sp, ccT.ap[hd * 128:(hd + 1) * 128, tq * 512:(tq + 1) * 512], o.ob.ap[:], reads=[o.ob], writes=[ccT], owner=o.ob)
        K.barrier()
        K.release([o.q for o in S] + [o.k for o in S] + [o.v for o in S] + [o.ob for o in S] + list(cm.values()) + [zer, zer32])
        st.close()

    def hgrn2_phase():
        st = contextlib.ExitStack()
        rm128 = K.sb(st, "rm128", [128, T], F32); rm16 = K.sb(st, "rm16", [128, T], F32)
        m01 = K.sb(st, "m01", [128, 128], F32)
        K.dma(K.sp, rm128.ap[:], cst["rm128"].ap[:, :], reads=[], writes=[rm128], owner=rm128)
        K.dma(K.sp, rm16.ap[:], cst["rm16"].ap[:, :], reads=[], writes=[rm16], owner=rm16)
        K.dma(K.sp, m01.ap[:], cst["mask01"].ap[:, :], reads=[], writes=[m01], owner=m01)
        q = K.sb(st, "gq", [128, T], F32); lf = K.sb(st, "glf", [128, T], F32); kk = K.sb(st, "gkk", [128, T], F32)
        L = K.sb(st, "gL", [128, T], F32); L16 = K.sb(st, "gL16", [128, T], F32)
        E = K.sb(st, "gE", [128, T], F32); Dt = K.sb(st, "gD", [128, T], F32)
        Q0 = K.sb(st, "gQ0", [128, T], BF16); Kh = K.sb(st, "gKh", [128, T], BF16); Qs = K.sb(st, "gQs", [128, T], BF16)
        Ks = [K.sb(st, f"gKs{i}", [128, T], BF16) for i in range(2)]
        vi = K.sb(st, "gvi", [128, 16, 128], BF16); Kht = K.sb(st, "gKht", [128, 16, 128], BF16)
        scm = K.sb(st, "gscm", [128, 16, 128], BF16)
        oT = K.sb(st, "goT", [128, T], F32); hgs = K.sb(st, "ghgs", [128, T], BF16)
        dec = K.sb(st, "gdec", [128, 16], F32)
        S32 = K.sb(st, "gS32", [128, 128], F32); Sb = K.sb(st, "gSb", [128, 128], BF16)
        sqb = K.sb(st, "gsq", [128, 512], BF16); rst = K.sb(st, "grst", [128, 512], F32); tmpn = K.sb(st, "gtmp", [128, 512], F32)
        bo = [K.sb(st, f"gbo{i}", [128, 512], BF16) for i in range(2)]
        scps = [K.psum(st, f"gsc{i}", [128, 512]) for i in range(4)]
        tps = [K.psum(st, f"gtp{i}", [128, 1024], BF16) for i in range(2)]
        ops = K.psum(st, "gops", [128, 512]); sps = K.psum(st, "gsps", [128, 512])
        v3 = lambda t: t.ap[:].rearrange("p (n c) -> p n c", c=128)
        for hd in range(8):
            rows = slice(hd * 128, (hd + 1) * 128)
            K.dma(K.sp, q.ap[:], hqT.ap[rows, :], reads=[hqT], writes=[q], owner=q)
            K.dma(K.sp, lf.ap[:], lfT.ap[rows, :], reads=[lfT], writes=[lf], owner=lf)
            K.dma(K.sp, kk.ap[:], kkT.ap[rows, :], reads=[kkT], writes=[kk], owner=kk)
            K.dma(K.sp, vi.ap[:], hitok.ap[hd].rearrange("(sb p) d -> p sb d", p=128), reads=[hitok], writes=[vi], owner=vi)
            K.dma(K.sp, hgs.ap[:], hgT.ap[rows, :], reads=[hgT], writes=[hgs], owner=hgs)
            K.op(K.dve, lambda: nc.vector.tensor_tensor_scan(out=L.ap[:], data0=rm128.ap[:], data1=lf.ap[:], initial=0.0, op0=ALU.mult, op1=ALU.add),
                 reads=[rm128, lf], writes=[L])
            K.op(K.dve, lambda: nc.vector.tensor_tensor_scan(out=L16.ap[:], data0=rm16.ap[:], data1=lf.ap[:], initial=0.0, op0=ALU.mult, op1=ALU.add),
                 reads=[rm16, lf], writes=[L16])
            K.op(K.act, lambda: nc.scalar.activation(out=E.ap[:], in_=L.ap[:], func=AF.Exp), reads=[L], writes=[E])
            K.op(K.dve, lambda: nc.vector.tensor_tensor(out=Q0.ap[:], in0=q.ap[:], in1=E.ap[:], op=ALU.mult), reads=[q, E], writes=[Q0])
            K.op(K.act, lambda: nc.scalar.activation(out=dec.ap[:], in_=v3(L)[:, :, 127], func=AF.Exp), reads=[L], writes=[dec])
            K.op(K.dve, lambda: nc.vector.tensor_tensor(out=v3(Dt), in0=v3(L)[:, :, 127:128].to_broadcast([128, 16, 128]), in1=v3(L), op=ALU.subtract),
                 reads=[L], writes=[Dt])
            K.op(K.act, lambda: nc.scalar.activation(out=E.ap[:], in_=Dt.ap[:], func=AF.Exp), reads=[Dt], writes=[E])
            K.op(K.dve, lambda: nc.vector.tensor_tensor(out=Kh.ap[:], in0=kk.ap[:], in1=E.ap[:], op=ALU.mult), reads=[kk, E], writes=[Kh])
            K.op(K.act, lambda: nc.scalar.activation(out=E.ap[:], in_=L16.ap[:], func=AF.Exp), reads=[L16], writes=[E])
            K.op(K.dve, lambda: nc.vector.tensor_tensor(out=Qs.ap[:], in0=q.ap[:], in1=E.ap[:], op=ALU.mult), reads=[q, E], writes=[Qs])
            for n in range(16):
                tp = tps[n // 8]
                K.op(K.pe, (lambda n=n, tp=tp: nc.tensor.transpose(tp.ap[:, (n % 8) * 128:(n % 8 + 1) * 128], Kh.ap[:, n * 128:(n + 1) * 128], C.ident.ap[:])),
                     reads=[Kh, C.ident], writes=[tp], sig=(n % 8 == 7))
            for i2 in range(2):
                K.op(K.act, (lambda i2=i2: nc.scalar.copy(out=Kht.ap[:, i2 * 8:(i2 + 1) * 8, :], in_=tps[i2].ap[:].rearrange("p (n c) -> p n c", c=128))),
                     reads=[tps[i2]], writes=[Kht])
            for i in range(8):
                ks = Ks[i % 2]
                if i == 0:
                    K.op(K.dve, lambda: nc.vector.tensor_scalar(out=Dt.ap[:], in0=L.ap[:], scalar1=-1.0, scalar2=None, op0=ALU.mult), reads=[L], writes=[Dt])
                else:
                    K.op(K.dve, (lambda i=i: nc.vector.tensor_tensor(out=v3(Dt), in0=v3(L)[:, :, 16 * i - 1:16 * i].to_broadcast([128, 16, 128]), in1=v3(L),
                                                                   op=ALU.subtract)), reads=[L], writes=[Dt])
                K.op(K.act, lambda: nc.scalar.activation(out=E.ap[:], in_=Dt.ap[:], func=AF.Exp), reads=[Dt], writes=[E])
                K.op(K.dve, (lambda ks=ks: nc.vector.scalar_tensor_tensor(out=ks.ap[:], in0=E.ap[:], scalar=1e30, in1=kk.ap[:], op0=ALU.min, op1=ALU.mult)),
                     reads=[E, kk], writes=[ks])
                for n in range(16):
                    sc = scps[n // 4]; cb0 = (n % 4) * 128 + 16 * i
                    K.op(K.pe, (lambda n=n, sc=sc, cb0=cb0, ks=ks, i=i: nc.tensor.matmul(sc.ap[:, cb0:cb0 + 16], ks.ap[:, n * 128:(n + 1) * 128],
                                                                                    Qs.ap[:, n * 128 + 16 * i:n * 128 + 16 * i + 16], start=True, stop=True)),
                         reads=[ks, Qs], writes=[sc], sig=(n == 15))
            for n4 in range(4):
                K.op(K.dve, (lambda n4=n4: nc.vector.tensor_tensor(out=scm.ap[:, n4 * 4:(n4 + 1) * 4, :], in0=scps[n4].ap[:].rearrange("p (n c) -> p n c", c=128),
                                                                   in1=m01.ap[:].rearrange("p (o c) -> p o c", o=1).to_broadcast([128, 4, 128]), op=ALU.mult)),
                     reads=[scps[n4], m01], writes=[scm])
            for n in range(16):
                oc = ops.ap[:, (n % 4) * 128:(n % 4 + 1) * 128]
                if n > 0:
                    K.op(K.pe, (lambda n=n, oc=oc: nc.tensor.matmul(oc, Sb.ap[:], Q0.ap[:, n * 128:(n + 1) * 128], start=True, stop=False)),
                         reads=[Sb, Q0], writes=[ops], sig=False)
                K.op(K.pe, (lambda n=n, oc=oc: nc.tensor.matmul(oc, vi.ap[:, n, :], scm.ap[:, n, :], start=(n == 0), stop=True)),
                     reads=[vi, scm], writes=[ops], sig=True)
                K.op(K.act, (lambda n=n, oc=oc: nc.scalar.copy(out=oT.ap[:, n * 128:(n + 1) * 128], in_=oc)), reads=[ops], writes=[oT])
                if n < 15:
                    sp_ = sps.ap[:, (n % 4) * 128:(n % 4 + 1) * 128]
                    K.op(K.pe, (lambda n=n, sp_=sp_: nc.tensor.matmul(sp_, Kht.ap[:, n, :], vi.ap[:, n, :], start=True, stop=True)),
                         reads=[Kht, vi], writes=[sps], sig=True)
                    if n == 0:
                        K.op(K.dve, (lambda sp_=sp_: nc.vector.tensor_copy(out=S32.ap[:], in_=sp_)), reads=[sps], writes=[S32])
                    else:
                        K.op(K.dve, (lambda n=n, sp_=sp_: nc.vector.scalar_tensor_tensor(out=S32.ap[:], in0=S32.ap[:], scalar=dec.ap[:, n:n + 1], in1=sp_,
                                                                                         op0=ALU.mult, op1=ALU.add)), reads=[S32, dec, sps], writes=[S32])
                    K.op(K.act, lambda: nc.scalar.copy(out=Sb.ap[:], in_=S32.ap[:]), reads=[S32], writes=[Sb])
            ohn, _ = PRM["hgn"]
            for tt in range(4):
                cs = slice(tt * 512, (tt + 1) * 512)
                K.op(K.act, (lambda cs=cs: nc.scalar.activation(out=sqb.ap[:], in_=oT.ap[:, cs], func=AF.Square)), reads=[oT], writes=[sqb])
                K.op(K.pe, lambda: nc.tensor.matmul(ops.ap[:], C.ones.ap[:], sqb.ap[:], start=True, stop=True), reads=[C.ones, sqb], writes=[ops], sig=True)
                K.op(K.act, lambda: nc.scalar.activation(out=rst.ap[:], in_=ops.ap[:], func=AF.Sqrt, bias=C.epsb.ap[:, 0:1], scale=1.0 / 128), reads=[ops, C.epsb], writes=[rst])
                K.op(K.dve, lambda: nc.vector.reciprocal(out=rst.ap[:], in_=rst.ap[:]), reads=[rst], writes=[rst])
                K.op(K.dve, (lambda cs=cs: nc.vector.scalar_tensor_tensor(out=tmpn.ap[:], in0=oT.ap[:, cs], scalar=C.prm.ap[:, ohn:ohn + 1], in1=rst.ap[:],
                                                                         op0=ALU.mult, op1=ALU.mult)), reads=[oT, rst, C.prm], writes=[tmpn])
                b = bo[tt % 2]
                K.op(K.dve, (lambda cs=cs, b=b: nc.vector.tensor_tensor(out=b.ap[:], in0=tmpn.ap[:], in1=hgs.ap[:, cs], op=ALU.mult)), reads=[tmpn, hgs], writes=[b])
                K.dma(K.sp, ccT.ap[1024 + hd * 128:1024 + (hd + 1) * 128, cs], b.ap[:], reads=[b], writes=[ccT], owner=b)
        K.barrier()
        K.release([rm128, rm16, m01, q, lf, kk, vi, hgs] + bo)
        st.close()

    def lru_phase():
        st = contextlib.ExitStack()
        xb = K.sb(st, "rxb", [128, 2, T], F32); yc = K.sb(st, "ryc", [128, 2, T], F32); ycb = K.sb(st, "rycb", [128, 2, T], BF16)
        r = K.sb(st, "rr", [128, 2, T], F32); ig = K.sb(st, "rig", [128, 2, T], F32)
        a = K.sb(st, "ra", [128, 2, T], F32); mu = K.sb(st, "rmu", [128, 2, T], F32)
        gb = K.sb(st, "rgb", [128, 2, T], F32); gt = K.sb(st, "rgt", [128, 2, T], F32)
        mo = K.sb(st, "rmo", [128, 2, T], BF16)
        wa = [K.sb(st, f"rwa{i}", [128, 2, 256], BF16) for i in range(2)]; wx = [K.sb(st, f"rwx{i}", [128, 2, 256], BF16) for i in range(2)]
        pss = [K.psum(st, f"rps{i}", [128, 512]) for i in range(8)]
        pi = 0
        ocw = [PRM[f"convw{t}"][0] for t in range(4)]; ocb = PRM["convb"][0]; oba = PRM["ba"][0]; obx = PRM["bx"][0]
        for nb in range(LRU // 256):
            rows = slice(nb * 256, (nb + 1) * 256)
            W1 = wa[nb % 2]; W2 = wx[nb % 2]
            K.dma(K.pool, W1.ap[:], w_ra.ap[nb], reads=[w_ra], writes=[W1], owner=W1)
            K.dma(K.pool, W2.ap[:], w_rx.ap[nb], reads=[w_rx], writes=[W2], owner=W2)
            K.dma(K.sp, xb.ap[:], xbT.ap[rows, :].rearrange("(k p) t -> p k t", p=128), reads=[xbT], writes=[xb], owner=xb)
            K.dma(K.sp, gb.ap[:], gbT.ap[rows, :].rearrange("(k p) t -> p k t", p=128), reads=[gbT], writes=[gb], owner=gb)
            for k in range(2):
                ch = nb * 2 + k
                K.op(K.dve, (lambda k=k, ch=ch: nc.vector.tensor_scalar(out=yc.ap[:, k, :], in0=xb.ap[:, k, :], scalar1=C.prm.ap[:, ocw[0] + ch:ocw[0] + ch + 1],
                                                                        scalar2=C.prm.ap[:, ocb + ch:ocb + ch + 1], op0=ALU.mult, op1=ALU.add)),
                     reads=[xb, C.prm], writes=[yc])
                for tap in range(1, 4):
                    K.op(K.dve, (lambda k=k, ch=ch, tap=tap: nc.vector.scalar_tensor_tensor(out=yc.ap[:, k, tap:], in0=xb.ap[:, k, 0:T - tap],
                                                                                            scalar=C.prm.ap[:, ocw[tap] + ch:ocw[tap] + ch + 1], in1=yc.ap[:, k, tap:],
                                                                                            op0=ALU.mult, op1=ALU.add)), reads=[xb, yc, C.prm], writes=[yc])
                K.op(K.act, (lambda k=k: nc.scalar.copy(out=ycb.ap[:, k, :], in_=yc.ap[:, k, :])), reads=[yc], writes=[ycb])
            for k in range(2):
                ch = nb * 2 + k
                for tt in range(4):
                    cs = slice(tt * 512, (tt + 1) * 512)
                    for (W, dstt, ob) in ((W1, r, oba), (W2, ig, obx)):
                        ps = pss[pi % 8]; pi += 1
                        for ic in range(2):
                            K.op(K.pe, (lambda W=W, ps=ps, ic=ic, k=k, cs=cs: nc.tensor.matmul(ps.ap[:], W.ap[:, ic, k * 128:(k + 1) * 128], ycb.ap[:, ic, cs],
                                                                                           start=(ic == 0), stop=(ic == 1))), reads=[W, ycb], writes=[ps], sig=(ic == 1))
                        K.op(K.act, (lambda ps=ps, dstt=dstt, ob=ob, k=k, cs=cs, ch=ch: nc.scalar.activation(out=dstt.ap[:, k, cs], in_=ps.ap[:], func=AF.Sigmoid,
                                                                                                        bias=C.prm.ap[:, ob + ch:ob + ch + 1])),
                             reads=[ps, C.prm], writes=[dstt])
            for k in range(2):
                ch = nb * 2 + k
                scp = C.drv.ap[:, 24 + ch:25 + ch]
                K.op(K.act, (lambda k=k, scp=scp: nc.scalar.activation(out=a.ap[:, k, :], in_=r.ap[:, k, :], func=AF.Exp, scale=scp)), reads=[r, C.drv], writes=[a])
                K.op(K.act, (lambda k=k: nc.scalar.activation(out=mu.ap[:, k, :], in_=a.ap[:, k, :], func=AF.Square)), reads=[a], writes=[mu])
                K.op(K.dve, (lambda k=k: nc.vector.tensor_scalar(out=mu.ap[:, k, :], in0=mu.ap[:, k, :], scalar1=1.0, scalar2=-1.0, op0=ALU.min, op1=ALU.mult)), reads=[mu], writes=[mu])
                K.op(K.act, (lambda k=k: nc.scalar.activation(out=mu.ap[:, k, :], in_=mu.ap[:, k, :], func=AF.Sqrt, bias=1.0, scale=1.0)), reads=[mu], writes=[mu])
                K.op(K.dve, (lambda k=k: nc.vector.memset(mu.ap[:, k, 0:1], 1.0)), reads=[], writes=[mu])
                K.op(K.dve, (lambda k=k: nc.vector.tensor_tensor(out=mu.ap[:, k, :], in0=mu.ap[:, k, :], in1=ig.ap[:, k, :], op=ALU.mult)), reads=[mu, ig], writes=[mu])
                K.op(K.dve, (lambda k=k: nc.vector.tensor_tensor(out=mu.ap[:, k, :], in0=mu.ap[:, k, :], in1=yc.ap[:, k, :], op=ALU.mult)), reads=[mu, yc], writes=[mu])
                K.op(K.dve, (lambda k=k: nc.vector.tensor_tensor_scan(out=r.ap[:, k, :], data0=a.ap[:, k, :], data1=mu.ap[:, k, :], initial=0.0, op0=ALU.mult, op1=ALU.add)),
                     reads=[a, mu], writes=[r])
                K.op(K.pool, (lambda k=k: nc.gpsimd.tensor_tensor(out=gt.ap[:, k, :], in0=gb.ap[:, k, :], in1=gb.ap[:, k, :], op=ALU.mult)), reads=[gb], writes=[gt])
                K.op(K.pool, (lambda k=k: nc.gpsimd.tensor_scalar(out=gt.ap[:, k, :], in0=gt.ap[:, k, :], scalar1=0.044715, scalar2=1.0, op0=ALU.mult, op1=ALU.add)),
                     reads=[gt], writes=[gt])
                K.op(K.pool, (lambda k=k: nc.gpsimd.tensor_tensor(out=gt.ap[:, k, :], in0=gt.ap[:, k, :], in1=gb.ap[:, k, :], op=ALU.mult)), reads=[gt, gb], writes=[gt])
                K.op(K.act, (lambda k=k: nc.scalar.activation(out=gt.ap[:, k, :], in_=gt.ap[:, k, :], func=AF.Sigmoid, scale=1.5957691216057308)), reads=[gt], writes=[gt])
                K.op(K.pool, (lambda k=k: nc.gpsimd.tensor_tensor(out=gt.ap[:, k, :], in0=gt.ap[:, k, :], in1=gb.ap[:, k, :], op=ALU.mult)), reads=[gt, gb], writes=[gt])
                K.op(K.dve, (lambda k=k: nc.vector.tensor_tensor(out=mo.ap[:, k, :], in0=gt.ap[:, k, :], in1=r.ap[:, k, :], op=ALU.mult)), reads=[gt, r], writes=[mo])
            K.dma(K.sp, mxT.ap[rows, :].rearrange("(k p) t -> p k t", p=128), mo.ap[:], reads=[mo], writes=[mxT], owner=mo)
        K.barrier()
        K.release([xb, gb, mo] + wa + wx)
        st.close()

    token_phase("p0", 0, False, inproj_even)
    if stage >= 2:
        attention_phase()
    if stage >= 3:
        hgrn2_phase()
    if stage >= 4:
        token_phase("p1", 0, True, inproj_odd)
    if stage >= 5:
        lru_phase()
    if stage >= 6:
        token_phase("p2", 1, True, None)
    K.barrier()
    gs.close()
    es.close()
    return nc


def host_inputs(inp):
    f = lambda a: np.ascontiguousarray(np.asarray(a, dtype=np.float32))
    sh = {}
    wi = f(inp["w_in_even"][0])
    names = ["w_sq", "w_sk", "w_sv", "w_hq", "w_hf", "w_hi", "w_hg"]
    for i, nm in enumerate(names):
        sh[nm] = _wl(wi[:, i * 1024:(i + 1) * 1024], 512)
    sh["w_oe"] = _wl(f(inp["w_out_even"][0]), 512)
    sh["w_io"] = _wl(f(inp["w_in_odd"][0]), 512)
    sh["w_oo"] = _wl(f(inp["w_out_odd"][0]), 256)
    for i in range(2):
        gu = f(inp["w_gate_up"][i])
        g = gu[:, :DFF].reshape(D, FC // 2, 256); u = gu[:, DFF:].reshape(D, FC // 2, 256)
        sh[f"w_gu{i}"] = _wl(np.concatenate([g, u], axis=2).reshape(D, FC * 256), 512)
        sh[f"w_dn{i}"] = _wl(f(inp["w_down"][i]), 128)
        sh[f"w_pg{i}"] = _wl(f(inp["w_ple_gate"][i]), 512)
        sh[f"w_pu{i}"] = _wl(f(inp["w_ple_up"][i]), 2048)
    for nm, key in (("w_ra", "rg_wa"), ("w_rx", "rg_wx")):
        w = f(inp[key][0])
        sh[nm] = np.ascontiguousarray(w.reshape(11, 2, 128, 256).transpose(0, 2, 1, 3))
    prm = np.zeros((128, NPRM), np.float32)
    def put(name, arr):
        off, n = PRM[name]; prm[:, off:off + n] = arr
    for i in range(2):
        for nm in ("mix_pre_g", "mix_post_g", "ffn_pre_g", "ffn_post_g", "ple_norm_g"):
            put(f"{nm}{i}", _cols(f(inp[nm][i])))
    put("lb0", _cols(f(inp["hg_lb_logits"][0]))); put("lb1", _cols(f(inp["hg_lb_logits"][1])))
    put("hgn", f(inp["hg_norm_g"][0]).reshape(128, 1))
    for tap in range(4):
        put(f"convw{tap}", _cols(f(inp["conv_w"][0, tap])))
    put("convb", _cols(f(inp["conv_b"][0]))); put("ba", _cols(f(inp["rg_ba"][0]).reshape(-1))); put("bx", _cols(f(inp["rg_bx"][0]).reshape(-1)))
    put("lam", _cols(f(inp["rg_lambda"][0])))
    sh["prm"] = prm
    for k, v in _consts().items():
        sh["c_" + k] = v
    return sh


def kernel(**inp):
    sh = host_inputs(inp)
    x = np.asarray(inp["x"], np.float32); p = np.asarray(inp["p"], np.float32)
    nc = build()
    in_maps = []
    for b in range(8):
        m = dict(sh)
        m["xT"] = np.ascontiguousarray(x[b].T)
        m["pT"] = np.ascontiguousarray(p[:, b].transpose(0, 2, 1))
        in_maps.append(m)
    res = run_bass_kernel_spmd(nc, in_maps, core_ids=list(range(8)))
    out = np.stack([np.ascontiguousarray(res.results[b]["outT"].T) for b in range(8)], axis=0)
    return out.astype(np.float32)
```

```python
import contextlib
import numpy as np
import concourse.bass as bass
import concourse.mybir as mybir
from concourse.bass_utils import run_bass_kernel_spmd

F32 = mybir.dt.float32
F32R = mybir.dt.float32r
BF16 = mybir.dt.bfloat16
AF = mybir.ActivationFunctionType
ALU = mybir.AluOpType

D = 2048
T = 2048
TT = 512
NTT = T // TT
KC = D // 128
DFF = 5632
FC = DFF // 128
LRU = 2816
LC = LRU // 128
PLE = 256
EPS = 1e-6
WSLOT = 8192


def _wl(w, cb):
    K, N = w.shape
    return np.ascontiguousarray(w.reshape(K // 128, 128, N // cb, cb).transpose(2, 1, 0, 3))


def _cols(v):
    return np.ascontiguousarray(v.reshape(-1, 128).T)


PRM = {}


def _prm_layout():
    off = 0
    def add(name, n):
        nonlocal off
        PRM[name] = (off, n)
        off += n
    for i in range(2):
        for nm in ("mix_pre_g", "mix_post_g", "ffn_pre_g", "ffn_post_g", "ple_norm_g"):
            add(f"{nm}{i}", 16)
    add("lb0", 8); add("lb1", 8); add("hgn", 1)
    for tap in range(4):
        add(f"convw{tap}", LC)
    add("convb", LC); add("ba", LC); add("bx", LC); add("lam", LC)
    return off


NPRM = _prm_layout()


def _consts():
    c = {}
    s = np.arange(128)[:, None]; t = np.arange(128)[None, :]
    c["ident"] = np.eye(128, dtype=np.float32)
    c["ones"] = np.ones((128, 128), np.float32)
    c["zeros"] = np.zeros((128, 512), np.float32)
    c["maskneg"] = np.where(s >= t, -30000.0, 0.0).astype(np.float32)
    c["mask01"] = (s <= t).astype(np.float32)
    c["negU"] = np.where(s >= t, -1.0, 0.0).astype(np.float32)
    c["negOnes"] = -np.ones((128, 128), np.float32)
    tt = np.arange(T)
    c["rm128"] = np.broadcast_to((tt % 128 != 0).astype(np.float32), (128, T)).copy()
    c["rm16"] = np.broadcast_to((tt % 16 != 0).astype(np.float32), (128, T)).copy()
    return c


class Tl:
    def __init__(self, name, ap):
        self.name = name; self.ap = ap
        self.w = {}; self.r = {}
        self.dsem = None; self.dcnt = 0
        self.dram = False

    def __getitem__(self, k):
        return self.ap[k]


class Eng:
    def __init__(self, name, eng, sem):
        self.name = name; self.eng = eng; self.sem = sem; self.cnt = 0
        self.waited = {}; self.pend = []


class Kern:
    def __init__(self, nc, es):
        self.nc = nc; self.es = es
        self.pe = Eng("pe", nc.tensor, es.enter_context(nc.semaphore("s_pe")))
        self.act = Eng("act", nc.scalar, es.enter_context(nc.semaphore("s_act")))
        self.dve = Eng("dve", nc.vector, es.enter_context(nc.semaphore("s_dve")))
        self.pool = Eng("pool", nc.gpsimd, es.enter_context(nc.semaphore("s_pool")))
        self.sp = Eng("sp", nc.sync, es.enter_context(nc.semaphore("s_sp")))
        self.engs = [self.pe, self.act, self.dve, self.pool, self.sp]
        self.dsems = []
        self.nsem = 5

    def sb(self, stack, name, shape, dt):
        return Tl(name, stack.enter_context(self.nc.sbuf_tensor(name, list(shape), dt)))

    def psum(self, stack, name, shape, dt=F32):
        return Tl(name, stack.enter_context(self.nc.psum_tensor(name, list(shape), dt)))

    def dram(self, name, shape, dt, kind="Internal"):
        t = Tl(name, self.nc.dram_tensor(name, list(shape), dt, kind=kind).ap())
        t.dram = True
        return t

    def _wait(self, E, ev):
        for sid, (sem, val) in ev.items():
            if E is self.pe and sid == id(E.sem):
                continue
            if E.waited.get(sid, 0) < val:
                E.eng.wait_ge(sem, val)
                E.waited[sid] = val

    def _deps(self, E, reads, writes):
        ev = {}
        def mrg(d):
            for sid, (sem, val) in d.items():
                if sid not in ev or ev[sid][1] < val:
                    ev[sid] = (sem, val)
        for t in reads:
            mrg(t.w)
        for t in writes:
            mrg(t.w); mrg(t.r)
        self._wait(E, ev)

    def _record(self, ev, reads, writes):
        sid, sem, val = ev
        for t in writes:
            t.w = {sid: (sem, val)}; t.r = {}
        for t in reads:
            t.r[sid] = (sem, val)

    def op(self, E, fn, reads=(), writes=(), sig=True):
        self._deps(E, reads, writes)
        inst = fn()
        if sig:
            E.cnt += 1
            inst.then_inc(E.sem, 1)
            ev = (id(E.sem), E.sem, E.cnt)
            self._record(ev, reads, writes)
            for (r, w) in E.pend:
                self._record(ev, r, w)
            E.pend = []
        else:
            ev = (id(E.sem), E.sem, E.cnt + 1)
            self._record(ev, reads, writes)
        return inst

    def dma(self, Q, out_ap, in_ap, reads, writes, owner):
        reads = [t for t in reads if not t.dram]
        writes = [t for t in writes if not t.dram]
        self._deps(Q, reads, writes)
        if owner.dsem is None:
            owner.dsem = self.es.enter_context(self.nc.semaphore("d_" + owner.name))
            self.dsems.append(owner); self.nsem += 1
        owner.dcnt += 16
        Q.eng.dma_start(out=out_ap, in_=in_ap).then_inc(owner.dsem, 16)
        ev = (id(owner.dsem), owner.dsem, owner.dcnt)
        sid, sem, val = ev
        for t in writes:
            t.w = {sid: (sem, val)}; t.r = {}
        for t in reads:
            t.r[sid] = (sem, val)

    def barrier(self):
        ev = {}
        for E in self.engs:
            if E.cnt:
                ev[id(E.sem)] = (E.sem, E.cnt)
        for t in self.dsems:
            ev[id(t.dsem)] = (t.dsem, t.dcnt)
        for E in self.engs:
            self._wait(E, ev)

    def release(self, tiles):
        self.dsems = [t for t in self.dsems if t not in tiles]


class Ctx:
    pass


def wview(slot, kc, cb):
    return slot.ap[:, 0:kc * cb].rearrange("p (k c) -> p k c", c=cb)


def load_w(K, C, wl_dram, nb, kc, cb):
    slot = C.wslots[C.wi % len(C.wslots)]; C.wi += 1
    K.dma(K.pool, wview(slot, kc, cb), wl_dram.ap[nb], reads=[wl_dram], writes=[slot], owner=slot)
    return slot


def mm_group(K, C, ps_t, ps_ap, pairs, reads):
    n = len(pairs)
    for i, (l, r) in enumerate(pairs):
        K.op(K.pe, (lambda l=l, r=r, i=i: K.nc.tensor.matmul(ps_ap, l, r, start=(i == 0), stop=(i == n - 1))),
             reads=reads, writes=[ps_t], sig=(i == n - 1))


def next_ps(C):
    t = C.ps[C.pi % len(C.ps)]; C.pi += 1
    return t


def stats_begin(C, n, lag):
    C.statk = 0; C.statn = n; C.pend = []; C.lag = lag


def ones_mm(K, C):
    sq = C.pend.pop(0)
    k = C.statk; C.statk += 1
    K.op(K.pe, (lambda: K.nc.tensor.matmul(C.stat.ap[:], C.ones.ap[:], sq.ap[:], start=(k == 0), stop=(k == C.statn - 1))),
         reads=[sq, C.ones], writes=[C.stat], sig=True)


def sq_push(K, C, t, ap, eng):
    sq = C.sq[C.sqi % len(C.sq)]; C.sqi += 1
    if eng == "act":
        K.op(K.act, (lambda: K.nc.scalar.activation(out=sq.ap[:], in_=ap, func=AF.Square)), reads=[t], writes=[sq])
    else:
        K.op(K.dve, (lambda: K.nc.vector.tensor_tensor(out=sq.ap[:], in0=ap, in1=ap, op=ALU.mult)), reads=[t], writes=[sq])
    C.pend.append(sq)
    if len(C.pend) > C.lag:
        ones_mm(K, C)


def stats_finish(K, C, scale, out_rstd):
    while C.pend:
        ones_mm(K, C)
    assert C.statk == C.statn
    K.op(K.act, lambda: K.nc.scalar.activation(out=out_rstd.ap[:], in_=C.stat.ap[:], func=AF.Sqrt, bias=C.epsb.ap[:, 0:1], scale=scale),
         reads=[C.stat, C.epsb], writes=[out_rstd])
    K.op(K.dve, lambda: K.nc.vector.reciprocal(out=out_rstd.ap[:], in_=out_rstd.ap[:]), reads=[out_rstd], writes=[out_rstd])


def prm(C, name, j=None):
    off, n = PRM[name]
    if j is None:
        return C.prm.ap[:, off:off + n]
    return C.prm.ap[:, off + j:off + j + 1]


def norm_bf16(K, C, src, gname, dst, have_stats=False):
    if not have_stats:
        stats_begin(C, KC, 1)
        for kc in range(KC):
            sq_push(K, C, src, src.ap[:, kc, :], "act")
    stats_finish(K, C, 1.0 / D, C.rstd)
    for kc in range(KC):
        K.op(K.dve, (lambda kc=kc: K.nc.vector.scalar_tensor_tensor(out=dst.ap[:, kc, :], in0=src.ap[:, kc, :], scalar=prm(C, gname, kc),
                                                                   in1=C.rstd.ap[:], op0=ALU.mult, op1=ALU.mult)),
             reads=[src, C.rstd, C.prm], writes=[dst])


def post_norm_res(K, C, y, gname, h, follow=None, nT=None):
    stats_finish(K, C, 1.0 / D, C.rstd2)
    if follow == "norm":
        stats_begin(C, KC, 1)
    for kc in range(KC):
        K.op(K.dve, (lambda kc=kc: K.nc.vector.scalar_tensor_tensor(out=y.ap[:, kc, :], in0=y.ap[:, kc, :], scalar=prm(C, gname, kc),
                                                                   in1=C.rstd2.ap[:], op0=ALU.mult, op1=ALU.mult)),
             reads=[y, C.rstd2, C.prm], writes=[y])
        K.op(K.dve, (lambda kc=kc: K.nc.vector.tensor_tensor(out=h.ap[:, kc, :], in0=h.ap[:, kc, :], in1=y.ap[:, kc, :], op=ALU.add)),
             reads=[y, h], writes=[h])
        if follow == "norm":
            sq_push(K, C, h, h.ap[:, kc, :], "act")
        elif follow == "copy":
            K.op(K.act, (lambda kc=kc: K.nc.scalar.copy(out=nT.ap[:, kc, :], in_=h.ap[:, kc, :])), reads=[h], writes=[nT])


def linear_fm(K, C, inT, kcn, wl_dram, ncols, cb, epi):
    nblk = ncols // cb
    for nb in range(nblk):
        slot = load_w(K, C, wl_dram, nb, kcn, cb)
        wv = wview(slot, kcn, cb)
        for ci in range(cb // 128):
            ps = next_ps(C)
            mm_group(K, C, ps, ps.ap[:], [(wv[:, kc, ci * 128:(ci + 1) * 128], inT.ap[:, kc, :]) for kc in range(kcn)], reads=[slot, inT])
            epi(nb * (cb // 128) + ci, ps)


def stage_out(C, dt):
    lst = C.stg32 if dt == F32 else C.stg16
    t = lst[C.stgi[dt] % len(lst)]; C.stgi[dt] += 1
    return t


def build(stage=99, dbg=()):
    nc = bass.Bass("TRN2", target_bir_lowering=False)
    es = contextlib.ExitStack()
    K = Kern(nc, es)
    C = Ctx()
    dr = {}

    def ein(name, shape, dt=F32):
        dr[name] = K.dram(name, shape, dt, kind="ExternalInput"); return dr[name]

    def scr(name, shape, dt, out=False):
        dr[name] = K.dram(name, shape, dt, kind=("ExternalOutput" if (out or name in dbg) else "Internal")); return dr[name]

    xT = ein("xT", [D, T]); pT = ein("pT", [2, PLE, T]); prm_d = ein("prm", [128, NPRM])
    cst = {k: ein("c_" + k, list(v.shape)) for k, v in _consts().items()}
    w_sq = ein("w_sq", [2, 128, KC, 512]); w_sk = ein("w_sk", [2, 128, KC, 512]); w_sv = ein("w_sv", [2, 128, KC, 512])
    w_hq = ein("w_hq", [2, 128, KC, 512]); w_hf = ein("w_hf", [2, 128, KC, 512]); w_hi = ein("w_hi", [2, 128, KC, 512])
    w_hg = ein("w_hg", [2, 128, KC, 512])
    w_oe = ein("w_oe", [4, 128, KC, 512])
    w_io = ein("w_io", [2 * LRU // 512, 128, KC, 512])
    w_oo = ein("w_oo", [D // 256, 128, LC, 256])
    w_gu = [ein(f"w_gu{i}", [FC // 2, 128, KC, 512]) for i in range(2)]
    w_dn = [ein(f"w_dn{i}", [D // 128, 128, FC, 128]) for i in range(2)]
    w_pg = [ein(f"w_pg{i}", [4, 128, KC, 512]) for i in range(2)]
    w_pu = [ein(f"w_pu{i}", [1, 128, 2, 2048]) for i in range(2)]
    w_ra = ein("w_ra", [LRU // 256, 128, 2, 256]); w_rx = ein("w_rx", [LRU // 256, 128, 2, 256])

    outT = scr("outT", [D, T], F32, out=True)
    hT = scr("hT", [D, T], F32)
    qT = scr("qT", [1024, T], BF16); kT = scr("kT", [1024, T], BF16); vtok = scr("vtok", [8, T, 128], BF16)
    hqT = scr("hqT", [1024, T], F32); lfT = scr("lfT", [1024, T], F32); kkT = scr("kkT", [1024, T], F32)
    hgT = scr("hgT", [1024, T], BF16); hitok = scr("hitok", [8, T, 128], BF16)
    ccT = scr("ccT", [D, T], BF16)
    gbT = scr("gbT", [LRU, T], F32); xbT = scr("xbT", [LRU, T], F32)
    mxT = scr("mxT", [LRU, T], BF16)

    dbgt = {nm: scr(nm, [D, T], F32) for nm in ("dbg_m", "dbg_hmix", "dbg_hffn") if nm in dbg}
    def dbg_store(nm, tile, t0, tag):
        if nm in dbgt and tag == "p1":
            K.dma(K.sp, dbgt[nm].ap.rearrange("(k p) t -> p k t", p=128)[:, :, t0:t0 + TT], tile.ap[:], reads=[tile], writes=[dbgt[nm]], owner=tile)

    def kcv(d, t0):
        return d.ap.rearrange("(k p) t -> p k t", p=128)[:, :, t0:t0 + TT]

    gs = contextlib.ExitStack()
    C.prm = K.sb(gs, "prm_sb", [128, NPRM], F32)
    C.ones = K.sb(gs, "ones_sb", [128, 128], BF16)
    C.ident = K.sb(gs, "ident_sb", [128, 128], BF16)
    C.epsb = K.sb(gs, "epsb", [128, 1], F32)
    C.drv = K.sb(gs, "drv", [128, 64], F32)
    K.dma(K.sp, C.prm.ap[:], prm_d.ap[:, :], reads=[prm_d], writes=[C.prm], owner=C.prm)
    K.dma(K.pool, C.ones.ap[:], cst["ones"].ap[:, :], reads=[], writes=[C.ones], owner=C.ones)
    K.dma(K.pool, C.ident.ap[:], cst["ident"].ap[:, :], reads=[], writes=[C.ident], owner=C.ident)
    K.op(K.dve, lambda: nc.vector.memset(C.epsb.ap[:], EPS), writes=[C.epsb])
    o0, _ = PRM["lb0"]; o1, _ = PRM["lb1"]
    K.op(K.dve, lambda: nc.vector.tensor_tensor(out=C.drv.ap[:, 0:8], in0=C.prm.ap[:, o0:o0 + 8], in1=C.prm.ap[:, o1:o1 + 8], op=ALU.subtract),
         reads=[C.prm], writes=[C.drv])
    K.op(K.act, lambda: nc.scalar.activation(out=C.drv.ap[:, 0:8], in_=C.drv.ap[:, 0:8], func=AF.Sigmoid), reads=[C.drv], writes=[C.drv])
    K.op(K.dve, lambda: nc.vector.tensor_scalar(out=C.drv.ap[:, 8:16], in0=C.drv.ap[:, 0:8], scalar1=-1.0, scalar2=1.0, op0=ALU.mult, op1=ALU.add),
         reads=[C.drv], writes=[C.drv])
    K.op(K.dve, lambda: nc.vector.tensor_scalar(out=C.drv.ap[:, 16:24], in0=C.drv.ap[:, 8:16], scalar1=-1.0, scalar2=None, op0=ALU.mult),
         reads=[C.drv], writes=[C.drv])
    ol, _ = PRM["lam"]
    K.op(K.act, lambda: nc.scalar.activation(out=C.drv.ap[:, 24:46], in_=C.prm.ap[:, ol:ol + LC], func=AF.Exp, scale=-1.0), reads=[C.prm], writes=[C.drv])
    K.op(K.act, lambda: nc.scalar.activation(out=C.drv.ap[:, 24:46], in_=C.drv.ap[:, 24:46], func=AF.Ln, bias=1.0), reads=[C.drv], writes=[C.drv])
    K.op(K.dve, lambda: nc.vector.tensor_scalar(out=C.drv.ap[:, 24:46], in0=C.drv.ap[:, 24:46], scalar1=-8.0, scalar2=None, op0=ALU.mult),
         reads=[C.drv], writes=[C.drv])

    def token_phase(tag, layer, do_mix_ffn, inproj):
        ps_stack = contextlib.ExitStack()
        C.wslots = [K.sb(ps_stack, f"ws{i}_{tag}", [128, WSLOT], BF16) for i in range(3)]; C.wi = 0
        C.ps = [K.psum(ps_stack, f"ps{i}_{tag}", [128, TT]) for i in range(7)]; C.pi = 0
        C.stat = K.psum(ps_stack, f"stat_{tag}", [128, TT])
        C.sq = [K.sb(ps_stack, f"sq{i}_{tag}", [128, TT], BF16) for i in range(4)]; C.sqi = 0
        C.rstd = K.sb(ps_stack, f"rstd_{tag}", [128, TT], F32)
        C.rstd2 = K.sb(ps_stack, f"rstd2_{tag}", [128, TT], F32)
        C.stg32 = [K.sb(ps_stack, f"st32_{i}_{tag}", [128, TT], F32) for i in range(3)]
        C.stg16 = [K.sb(ps_stack, f"st16_{i}_{tag}", [128, TT], BF16) for i in range(3)]
        C.stgi = {F32: 0, BF16: 0}
        h = K.sb(ps_stack, f"h_{tag}", [128, KC, TT], F32)
        nT = K.sb(ps_stack, f"nT_{tag}", [128, KC, TT], BF16)
        if do_mix_ffn:
            act = K.sb(ps_stack, f"act_{tag}", [128, FC, TT], BF16)
            y = K.sb(ps_stack, f"y_{tag}", [128, KC, TT], F32)
            pt = K.sb(ps_stack, f"pt_{tag}", [128, 2, TT], BF16)
            tmp = [K.sb(ps_stack, f"tmp{i}_{tag}", [128, TT], F32) for i in range(2)]
            wpu = K.sb(ps_stack, f"wpu_{tag}", [128, 2 * D], BF16)
            K.dma(K.pool, wview(wpu, 2, D), w_pu[layer].ap[0], reads=[w_pu[layer]], writes=[wpu], owner=wpu)
        h_src = xT if layer == 0 else hT
        for tt in range(NTT):
            t0 = tt * TT
            K.dma(K.sp, h.ap[:], kcv(h_src, t0), reads=[h_src], writes=[h], owner=h)
            if do_mix_ffn:
                if layer == 0:
                    cc, ckc, wo, wcb = ccT, KC, w_oe, 512
                else:
                    cc, ckc, wo, wcb = mxT, LC, w_oo, 256
                K.dma(K.sp, act.ap[:, 0:ckc, :], kcv(cc, t0), reads=[cc], writes=[act], owner=act)
                def epi_y(c, ps):
                    K.op(K.act, lambda: nc.scalar.copy(out=y.ap[:, c, :], in_=ps.ap[:]), reads=[ps], writes=[y])
                    sq_push(K, C, y, y.ap[:, c, :], "dve")
                stats_begin(C, KC, 2)
                linear_fm(K, C, act, ckc, wo, D, wcb, epi_y)
                dbg_store("dbg_m", y, t0, tag)
                post_norm_res(K, C, y, f"mix_post_g{layer}", h, follow="norm")
                dbg_store("dbg_hmix", h, t0, tag)
                norm_bf16(K, C, h, f"ffn_pre_g{layer}", nT, have_stats=True)
                for nb in range(FC // 2):
                    slot = load_w(K, C, w_gu[layer], nb, KC, 512)
                    wv = wview(slot, KC, 512)
                    for ci in range(2):
                        psg = next_ps(C); psu = next_ps(C)
                        mm_group(K, C, psg, psg.ap[:], [(wv[:, kc, ci * 128:(ci + 1) * 128], nT.ap[:, kc, :]) for kc in range(KC)], reads=[slot, nT])
                        mm_group(K, C, psu, psu.ap[:], [(wv[:, kc, 256 + ci * 128:256 + (ci + 1) * 128], nT.ap[:, kc, :]) for kc in range(KC)], reads=[slot, nT])
                        tm = tmp[(nb * 2 + ci) % 2]; fc = nb * 2 + ci
                        K.op(K.act, lambda: nc.scalar.activation(out=tm.ap[:], in_=psg.ap[:], func=AF.Silu), reads=[psg], writes=[tm])
                        K.op(K.dve, lambda: nc.vector.tensor_tensor(out=act.ap[:, fc, :], in0=tm.ap[:], in1=psu.ap[:], op=ALU.mult), reads=[tm, psu], writes=[act])
                stats_begin(C, KC, 2)
                linear_fm(K, C, act, FC, w_dn[layer], D, 128, epi_y)
                post_norm_res(K, C, y, f"ffn_post_g{layer}", h, follow="copy", nT=nT)
                dbg_store("dbg_hffn", h, t0, tag)
                stats_begin(C, KC, 2)
                K.dma(K.pool, pt.ap[:], pT.ap[layer].rearrange("(k p) t -> p k t", p=128)[:, :, t0:t0 + TT], reads=[pT], writes=[pt], owner=pt)
                uslot = wpu
                uv = wview(wpu, 2, D)
                for nb in range(4):
                    slot = load_w(K, C, w_pg[layer], nb, KC, 512)
                    wv = wview(slot, KC, 512)
                    for ci in range(4):
                        c = nb * 4 + ci
                        psg = next_ps(C); pse = next_ps(C)
                        mm_group(K, C, psg, psg.ap[:], [(wv[:, kc, ci * 128:(ci + 1) * 128], nT.ap[:, kc, :]) for kc in range(KC)], reads=[slot, nT])
                        mm_group(K, C, pse, pse.ap[:], [(uv[:, k2, c * 128:(c + 1) * 128], pt.ap[:, k2, :]) for k2 in range(2)], reads=[uslot, pt])
                        tm = tmp[c % 2]
                        K.op(K.act, lambda: nc.scalar.activation(out=tm.ap[:], in_=psg.ap[:], func=AF.Sigmoid), reads=[psg], writes=[tm])
                        K.op(K.dve, lambda: nc.vector.tensor_tensor(out=y.ap[:, c, :], in0=tm.ap[:], in1=pse.ap[:], op=ALU.mult), reads=[tm, pse], writes=[y])
                        sq_push(K, C, y, y.ap[:, c, :], "dve")
                post_norm_res(K, C, y, f"ple_norm_g{layer}", h, follow=("norm" if inproj is not None else None))
                h_dst = hT if inproj is not None else outT
                K.dma(K.sp, kcv(h_dst, t0), h.ap[:], reads=[h], writes=[h_dst], owner=h)
            if inproj is not None:
                inproj(K, C, h, nT, t0, do_mix_ffn)
        K.barrier()
        K.release(C.wslots + [h, nT] + ([act, pt, wpu] if do_mix_ffn else []))
        ps_stack.close()

    def inproj_even(K, C, h, nT, t0, have_stats):
        norm_bf16(K, C, h, "mix_pre_g0", nT, have_stats=have_stats)
        def store(t, dst, c):
            K.dma(K.sp, dst.ap[c * 128:(c + 1) * 128, t0:t0 + TT], t.ap[:], reads=[t], writes=[dst], owner=t)
        def epi_q(c, ps):
            t = stage_out(C, BF16)
            K.op(K.act, lambda: nc.scalar.activation(out=t.ap[:], in_=ps.ap[:], func=AF.Copy, scale=128.0 ** -0.5), reads=[ps], writes=[t]); store(t, qT, c)
        def epi_k(c, ps):
            t = stage_out(C, BF16)
            K.op(K.act, lambda: nc.scalar.copy(out=t.ap[:], in_=ps.ap[:]), reads=[ps], writes=[t]); store(t, kT, c)
        def epi_hq(c, ps):
            t = stage_out(C, F32)
            K.op(K.act, lambda: nc.scalar.activation(out=t.ap[:], in_=ps.ap[:], func=AF.Silu), reads=[ps], writes=[t]); store(t, hqT, c)
        def epi_hg(c, ps):
            t = stage_out(C, BF16)
            K.op(K.act, lambda: nc.scalar.activation(out=t.ap[:], in_=ps.ap[:], func=AF.Silu), reads=[ps], writes=[t]); store(t, hgT, c)
        def epi_hf(c, ps):
            s = stage_out(C, F32); t1 = stage_out(C, F32); t2 = stage_out(C, F32)
            K.op(K.act, lambda: nc.scalar.activation(out=s.ap[:], in_=ps.ap[:], func=AF.Sigmoid), reads=[ps], writes=[s])
            K.op(K.act, lambda: nc.scalar.activation(out=t1.ap[:], in_=s.ap[:], func=AF.Ln, bias=C.drv.ap[:, c:c + 1], scale=C.drv.ap[:, 8 + c:9 + c]),
                 reads=[s, C.drv], writes=[t1]); store(t1, lfT, c)
            K.op(K.dve, lambda: nc.vector.tensor_scalar(out=t2.ap[:], in0=s.ap[:], scalar1=C.drv.ap[:, 16 + c:17 + c], scalar2=C.drv.ap[:, 8 + c:9 + c],
                                                        op0=ALU.mult, op1=ALU.add), reads=[s, C.drv], writes=[t2]); store(t2, kkT, c)
        linear_fm(K, C, nT, KC, w_sq, 1024, 512, epi_q)
        linear_fm(K, C, nT, KC, w_sk, 1024, 512, epi_k)
        linear_fm(K, C, nT, KC, w_hq, 1024, 512, epi_hq)
        linear_fm(K, C, nT, KC, w_hf, 1024, 512, epi_hf)
        linear_fm(K, C, nT, KC, w_hg, 1024, 512, epi_hg)
        for (wl, dst) in ((w_sv, vtok), (w_hi, hitok)):
            for nb in range(2):
                slot = load_w(K, C, wl, nb, KC, 512); wv = wview(slot, KC, 512)
                for tb in range(TT // 128):
                    ps = next_ps(C)
                    mm_group(K, C, ps, ps.ap[:], [(nT.ap[:, kc, tb * 128:(tb + 1) * 128], wv[:, kc, :]) for kc in range(KC)], reads=[slot, nT])
                    t = stage_out(C, BF16)
                    K.op(K.act, lambda: nc.scalar.copy(out=t.ap[:], in_=ps.ap[:]), reads=[ps], writes=[t])
                    K.dma(K.sp, dst.ap[nb * 4:(nb + 1) * 4, t0 + tb * 128:t0 + (tb + 1) * 128, :].rearrange("h t d -> t h d"),
                          t.ap[:].rearrange("t (h d) -> t h d", d=128), reads=[t], writes=[dst], owner=t)

    def inproj_odd(K, C, h, nT, t0, have_stats):
        norm_bf16(K, C, h, "mix_pre_g1", nT, have_stats=have_stats)
        def epi(c, ps):
            t = stage_out(C, F32)
            K.op(K.act, lambda: nc.scalar.copy(out=t.ap[:], in_=ps.ap[:]), reads=[ps], writes=[t])
            dst, cc = (gbT, c) if c < LC else (xbT, c - LC)
            K.dma(K.sp, dst.ap[cc * 128:(cc + 1) * 128, t0:t0 + TT], t.ap[:], reads=[t], writes=[dst], owner=t)
        linear_fm(K, C, nT, KC, w_io, 2 * LRU, 512, epi)

    def attention_phase():
        st = contextlib.ExitStack()
        NS = 4
        cm = {}
        for nm, dt in (("maskneg", BF16), ("negU", F32R), ("negOnes", F32R)):
            cm[nm] = K.sb(st, "c_" + nm + "_sb", [128, 128], dt)
            K.dma(K.pool, cm[nm].ap[:], cst[nm].ap[:, :], reads=[], writes=[cm[nm]], owner=cm[nm])
        zer = K.sb(st, "zer_sb", [128, 512], BF16)
        K.dma(K.pool, zer.ap[:], cst["zeros"].ap[:, :], reads=[], writes=[zer], owner=zer)
        zer32 = K.sb(st, "zer32_sb", [128, 512], F32)
        K.dma(K.sp, zer32.ap[:], cst["zeros"].ap[:, :], reads=[], writes=[zer32], owner=zer32)
        S = []
        for s in range(NS):
            o = Ctx()
            o.q = K.sb(st, f"aq{s}", [128, T], BF16); o.k = K.sb(st, f"ak{s}", [128, T], BF16); o.v = K.sb(st, f"av{s}", [128, 16, 128], BF16)
            o.e = K.sb(st, f"ae{s}", [128, 512], F32)
            o.sp = [K.sb(st, f"asp{s}_{i}", [128, 512], F32R) for i in range(2)]
            o.A = K.sb(st, f"aA{s}", [128, 512], F32R)
            o.w = [K.sb(st, f"aw{s}_{i}", [128, 512], BF16) for i in range(2)]
            o.ob = K.sb(st, f"aob{s}", [128, 512], BF16)
            o.zps = K.psum(st, f"azps{s}", [128, 512]); o.ops = K.psum(st, f"aops{s}", [128, 512])
            S.append(o)
        for hg in range(8 // NS):
            for s, o in enumerate(S):
                hd = hg * NS + s
                K.dma(K.sp, o.q.ap[:], qT.ap[hd * 128:(hd + 1) * 128, :], reads=[qT], writes=[o.q], owner=o.q)
                K.dma(K.sp, o.k.ap[:], kT.ap[hd * 128:(hd + 1) * 128, :], reads=[kT], writes=[o.k], owner=o.k)
                K.dma(K.sp, o.v.ap[:], vtok.ap[hd].rearrange("(sb p) d -> p sb d", p=128), reads=[vtok], writes=[o.v], owner=o.v)
            step = 0
            for tq in range(4):
                nblk = 4 * tq + 4
                for o in S:
                    K.op(K.pe, (lambda o=o: nc.tensor.matmul(o.ops.ap[:], zer.ap[:, 0:128], zer.ap[:], start=True, stop=False)), reads=[zer], writes=[o.ops], sig=False)
                    K.op(K.dve, (lambda o=o: nc.vector.tensor_copy(out=o.A.ap[:], in_=zer32.ap[:])), reads=[zer32], writes=[o.A])
                for bi in range(nblk):
                    sb = nblk - 1 - bi
                    c0 = max(0, 128 * sb - 512 * tq); diag = 128 * sb >= 512 * tq
                    q0 = 512 * tq + c0
                    par = step % 2; step += 1
                    for o in S:
                        K.op(K.pe, (lambda o=o: nc.tensor.matmul(o.zps.ap[:, c0:512], o.k.ap[:, sb * 128:(sb + 1) * 128], o.q.ap[:, q0:512 * tq + 512],
                                                                 start=True, stop=False)), reads=[o.k, o.q], writes=[o.zps], sig=not diag)
                        if diag:
                            K.op(K.pe, (lambda o=o: nc.tensor.matmul(o.zps.ap[:, c0:c0 + 128], C.ident.ap[:], cm["maskneg"].ap[:], start=False, stop=False)),
                                 reads=[C.ident, cm["maskneg"]], writes=[o.zps], sig=True)
                    for o in S:
                        K.op(K.act, (lambda o=o: nc.scalar.activation(out=o.e.ap[:, c0:512], in_=o.zps.ap[:, c0:512], func=AF.Exp)), reads=[o.zps], writes=[o.e])
                        K.op(K.act, (lambda o=o: nc.scalar.activation(out=o.sp[par].ap[:, c0:512], in_=o.e.ap[:, c0:512], func=AF.Ln, bias=1.0)),
                             reads=[o.e], writes=[o.sp[par]])
                    for o in S:
                        last = (bi == 0)
                        K.op(K.pe, (lambda o=o: nc.tensor.matmul(o.zps.ap[:, c0:512], cm["negU"].ap[:], o.sp[par].ap[:, c0:512], start=False, stop=last)),
                             reads=[cm["negU"], o.sp[par]], writes=[o.zps], sig=last)
                        if not last:
                            K.op(K.pe, (lambda o=o: nc.tensor.matmul(o.zps.ap[:, c0:512], cm["negOnes"].ap[:], o.A.ap[:, c0:512], start=False, stop=True)),
                                 reads=[cm["negOnes"], o.A], writes=[o.zps], sig=True)
                    for o in S:
                        K.op(K.act, (lambda o=o: nc.scalar.activation(out=o.w[par].ap[:, c0:512], in_=o.zps.ap[:, c0:512], func=AF.Exp)),
                             reads=[o.zps], writes=[o.w[par]])
                        if sb > 0:
                            K.op(K.dve, (lambda o=o: nc.vector.tensor_tensor(out=o.A.ap[:, c0:512], in0=o.A.ap[:, c0:512].bitcast(F32), in1=o.sp[par].ap[:, c0:512].bitcast(F32), op=ALU.add)),
                                 reads=[o.A, o.sp[par]], writes=[o.A])
                    for o in S:
                        K.op(K.pe, (lambda o=o: nc.tensor.matmul(o.ops.ap[:, c0:512], o.v.ap[:, sb, :], o.w[par].ap[:, c0:512], start=False, stop=(sb == 0))),
                             reads=[o.v, o.w[par]], writes=[o.ops], sig=(sb == 0))
                for s, o in enumerate(S):
                    hd = hg * NS + s
                    K.op(K.dve, (lambda o=o: nc.vector.tensor_copy(out=o.ob.ap[:], in_=o.ops.ap[:])), reads=[o.ops], writes=[o.ob])
                    K.dma(K.sp, ccT.ap[hd * 128:(hd + 1) * 128, tq * 512:(tq + 1) * 512], o.ob.ap[:], reads=[o.ob], writes=[ccT], owner=o.ob)
        K.barrier()
        K.release([o.q for o in S] + [o.k for o in S] + [o.v for o in S] + [o.ob for o in S] + list(cm.values()) + [zer, zer32])
        st.close()

    def hgrn2_phase():
        st = contextlib.ExitStack()
        rm128 = K.sb(st, "rm128", [128, T], F32); rm16 = K.sb(st, "rm16", [128, T], F32)
        m01 = K.sb(st, "m01", [128, 128], F32)
        K.dma(K.sp, rm128.ap[:], cst["rm128"].ap[:, :], reads=[], writes=[rm128], owner=rm128)
        K.dma(K.sp, rm16.ap[:], cst["rm16"].ap[:, :], reads=[], writes=[rm16], owner=rm16)
        K.dma(K.sp, m01.ap[:], cst["mask01"].ap[:, :], reads=[], writes=[m01], owner=m01)
        q = K.sb(st, "gq", [128, T], F32); lf = K.sb(st, "glf", [128, T], F32); kk = K.sb(st, "gkk", [128, T], F32)
        L = K.sb(st, "gL", [128, T], F32); L16 = K.sb(st, "gL16", [128, T], F32)
        E = K.sb(st, "gE", [128, T], F32); Dt = K.sb(st, "gD", [128, T], F32)
        Q0 = K.sb(st, "gQ0", [128, T], BF16); Kh = K.sb(st, "gKh", [128, T], BF16); Qs = K.sb(st, "gQs", [128, T], BF16)
        Ks = [K.sb(st, f"gKs{i}", [128, T], BF16) for i in range(2)]
        vi = K.sb(st, "gvi", [128, 16, 128], BF16); Kht = K.sb(st, "gKht", [128, 16, 128], BF16)
        scm = K.sb(st, "gscm", [128, 16, 128], BF16)
        oT = K.sb(st, "goT", [128, T], F32); hgs = K.sb(st, "ghgs", [128, T], BF16)
        dec = K.sb(st, "gdec", [128, 16], F32)
        S32 = K.sb(st, "gS32", [128, 128], F32); Sb = K.sb(st, "gSb", [128, 128], BF16)
        sqb = K.sb(st, "gsq", [128, 512], BF16); rst = K.sb(st, "grst", [128, 512], F32); tmpn = K.sb(st, "gtmp", [128, 512], F32)
        bo = [K.sb(st, f"gbo{i}", [128, 512], BF16) for i in range(2)]
        scps = [K.psum(st, f"gsc{i}", [128, 512]) for i in range(4)]
        tps = [K.psum(st, f"gtp{i}", [128, 1024], BF16) for i in range(2)]
        ops = K.psum(st, "gops", [128, 512]); sps = K.psum(st, "gsps", [128, 512])
        v3 = lambda t: t.ap[:].rearrange("p (n c) -> p n c", c=128)
        for hd in range(8):
            rows = slice(hd * 128, (hd + 1) * 128)
            K.dma(K.sp, q.ap[:], hqT.ap[rows, :], reads=[hqT], writes=[q], owner=q)
            K.dma(K.sp, lf.ap[:], lfT.ap[rows, :], reads=[lfT], writes=[lf], owner=lf)
            K.dma(K.sp, kk.ap[:], kkT.ap[rows, :], reads=[kkT], writes=[kk], owner=kk)
            K.dma(K.sp, vi.ap[:], hitok.ap[hd].rearrange("(sb p) d -> p sb d", p=128), reads=[hitok], writes=[vi], owner=vi)
            K.dma(K.sp, hgs.ap[:], hgT.ap[rows, :], reads=[hgT], writes=[hgs], owner=hgs)
            K.op(K.dve, lambda: nc.vector.tensor_tensor_scan(out=L.ap[:], data0=rm128.ap[:], data1=lf.ap[:], initial=0.0, op0=ALU.mult, op1=ALU.add),
                 reads=[rm128, lf], writes=[L])
            K.op(K.dve, lambda: nc.vector.tensor_tensor_scan(out=L16.ap[:], data0=rm16.ap[:], data1=lf.ap[:], initial=0.0, op0=ALU.mult, op1=ALU.add),
                 reads=[rm16, lf], writes=[L16])
            K.op(K.act, lambda: nc.scalar.activation(out=E.ap[:], in_=L.ap[:], func=AF.Exp), reads=[L], writes=[E])
            K.op(K.dve, lambda: nc.vector.tensor_tensor(out=Q0.ap[:], in0=q.ap[:], in1=E.ap[:], op=ALU.mult), reads=[q, E], writes=[Q0])
            K.op(K.act, lambda: nc.scalar.activation(out=dec.ap[:], in_=v3(L)[:, :, 127], func=AF.Exp), reads=[L], writes=[dec])
            K.op(K.dve, lambda: nc.vector.tensor_tensor(out=v3(Dt), in0=v3(L)[:, :, 127:128].to_broadcast([128, 16, 128]), in1=v3(L), op=ALU.subtract),
                 reads=[L], writes=[Dt])
            K.op(K.act, lambda: nc.scalar.activation(out=E.ap[:], in_=Dt.ap[:], func=AF.Exp), reads=[Dt], writes=[E])
            K.op(K.dve, lambda: nc.vector.tensor_tensor(out=Kh.ap[:], in0=kk.ap[:], in1=E.ap[:], op=ALU.mult), reads=[kk, E], writes=[Kh])
            K.op(K.act, lambda: nc.scalar.activation(out=E.ap[:], in_=L16.ap[:], func=AF.Exp), reads=[L16], writes=[E])
            K.op(K.dve, lambda: nc.vector.tensor_tensor(out=Qs.ap[:], in0=q.ap[:], in1=E.ap[:], op=ALU.mult), reads=[q, E], writes=[Qs])
            for n in range(16):
                tp = tps[n // 8]
                K.op(K.pe, (lambda n=n, tp=tp: nc.tensor.transpose(tp.ap[:, (n % 8) * 128:(n % 8 + 1) * 128], Kh.ap[:, n * 128:(n + 1) * 128], C.ident.ap[:])),
                     reads=[Kh, C.ident], writes=[tp], sig=(n % 8 == 7))
            for i2 in range(2):
                K.op(K.act, (lambda i2=i2: nc.scalar.copy(out=Kht.ap[:, i2 * 8:(i2 + 1) * 8, :], in_=tps[i2].ap[:].rearrange("p (n c) -> p n c", c=128))),
                     reads=[tps[i2]], writes=[Kht])
            for i in range(8):
                ks = Ks[i % 2]
                if i == 0:
                    K.op(K.dve, lambda: nc.vector.tensor_scalar(out=Dt.ap[:], in0=L.ap[:], scalar1=-1.0, scalar2=None, op0=ALU.mult), reads=[L], writes=[Dt])
                else:
                    K.op(K.dve, (lambda i=i: nc.vector.tensor_tensor(out=v3(Dt), in0=v3(L)[:, :, 16 * i - 1:16 * i].to_broadcast([128, 16, 128]), in1=v3(L),
                                                                   op=ALU.subtract)), reads=[L], writes=[Dt])
                K.op(K.act, lambda: nc.scalar.activation(out=E.ap[:], in_=Dt.ap[:], func=AF.Exp), reads=[Dt], writes=[E])
                K.op(K.dve, (lambda ks=ks: nc.vector.scalar_tensor_tensor(out=ks.ap[:], in0=E.ap[:], scalar=1e30, in1=kk.ap[:], op0=ALU.min, op1=ALU.mult)),
                     reads=[E, kk], writes=[ks])
                for n in range(16):
                    sc = scps[n // 4]; cb0 = (n % 4) * 128 + 16 * i
                    K.op(K.pe, (lambda n=n, sc=sc, cb0=cb0, ks=ks, i=i: nc.tensor.matmul(sc.ap[:, cb0:cb0 + 16], ks.ap[:, n * 128:(n + 1) * 128],
                                                                                    Qs.ap[:, n * 128 + 16 * i:n * 128 + 16 * i + 16], start=True, stop=True)),
                         reads=[ks, Qs], writes=[sc], sig=(n == 15))
            for n4 in range(4):
                K.op(K.dve, (lambda n4=n4: nc.vector.tensor_tensor(out=scm.ap[:, n4 * 4:(n4 + 1) * 4, :], in0=scps[n4].ap[:].rearrange("p (n c) -> p n c", c=128),
                                                                   in1=m01.ap[:].rearrange("p (o c) -> p o c", o=1).to_broadcast([128, 4, 128]), op=ALU.mult)),
                     reads=[scps[n4], m01], writes=[scm])
            for n in range(16):
                oc = ops.ap[:, (n % 4) * 128:(n % 4 + 1) * 128]
                if n > 0:
                    K.op(K.pe, (lambda n=n, oc=oc: nc.tensor.matmul(oc, Sb.ap[:], Q0.ap[:, n * 128:(n + 1) * 128], start=True, stop=False)),
                         reads=[Sb, Q0], writes=[ops], sig=False)
                K.op(K.pe, (lambda n=n, oc=oc: nc.tensor.matmul(oc, vi.ap[:, n, :], scm.ap[:, n, :], start=(n == 0), stop=True)),
                     reads=[vi, scm], writes=[ops], sig=True)
                K.op(K.act, (lambda n=n, oc=oc: nc.scalar.copy(out=oT.ap[:, n * 128:(n + 1) * 128], in_=oc)), reads=[ops], writes=[oT])
                if n < 15:
                    sp_ = sps.ap[:, (n % 4) * 128:(n % 4 + 1) * 128]
                    K.op(K.pe, (lambda n=n, sp_=sp_: nc.tensor.matmul(sp_, Kht.ap[:, n, :], vi.ap[:, n, :], start=True, stop=True)),
                         reads=[Kht, vi], writes=[sps], sig=True)
                    if n == 0:
                        K.op(K.dve, (lambda sp_=sp_: nc.vector.tensor_copy(out=S32.ap[:], in_=sp_)), reads=[sps], writes=[S32])
                    else:
                        K.op(K.dve, (lambda n=n, sp_=sp_: nc.vector.scalar_tensor_tensor(out=S32.ap[:], in0=S32.ap[:], scalar=dec.ap[:, n:n + 1], in1=sp_,
                                                                                         op0=ALU.mult, op1=ALU.add)), reads=[S32, dec, sps], writes=[S32])
                    K.op(K.act, lambda: nc.scalar.copy(out=Sb.ap[:], in_=S32.ap[:]), reads=[S32], writes=[Sb])
            ohn, _ = PRM["hgn"]
            for tt in range(4):
                cs = slice(tt * 512, (tt + 1) * 512)
                K.op(K.act, (lambda cs=cs: nc.scalar.activation(out=sqb.ap[:], in_=oT.ap[:, cs], func=AF.Square)), reads=[oT], writes=[sqb])
                K.op(K.pe, lambda: nc.tensor.matmul(ops.ap[:], C.ones.ap[:], sqb.ap[:], start=True, stop=True), reads=[C.ones, sqb], writes=[ops], sig=True)
                K.op(K.act, lambda: nc.scalar.activation(out=rst.ap[:], in_=ops.ap[:], func=AF.Sqrt, bias=C.epsb.ap[:, 0:1], scale=1.0 / 128), reads=[ops, C.epsb], writes=[rst])
                K.op(K.dve, lambda: nc.vector.reciprocal(out=rst.ap[:], in_=rst.ap[:]), reads=[rst], writes=[rst])
                K.op(K.dve, (lambda cs=cs: nc.vector.scalar_tensor_tensor(out=tmpn.ap[:], in0=oT.ap[:, cs], scalar=C.prm.ap[:, ohn:ohn + 1], in1=rst.ap[:],
                                                                         op0=ALU.mult, op1=ALU.mult)), reads=[oT, rst, C.prm], writes=[tmpn])
                b = bo[tt % 2]
                K.op(K.dve, (lambda cs=cs, b=b: nc.vector.tensor_tensor(out=b.ap[:], in0=tmpn.ap[:], in1=hgs.ap[:, cs], op=ALU.mult)), reads=[tmpn, hgs], writes=[b])
                K.dma(K.sp, ccT.ap[1024 + hd * 128:1024 + (hd + 1) * 128, cs], b.ap[:], reads=[b], writes=[ccT], owner=b)
        K.barrier()
        K.release([rm128, rm16, m01, q, lf, kk, vi, hgs] + bo)
        st.close()

    def lru_phase():
        st = contextlib.ExitStack()
        xb = K.sb(st, "rxb", [128, 2, T], F32); yc = K.sb(st, "ryc", [128, 2, T], F32); ycb = K.sb(st, "rycb", [128, 2, T], BF16)
        r = K.sb(st, "rr", [128, 2, T], F32); ig = K.sb(st, "rig", [128, 2, T], F32)
        a = K.sb(st, "ra", [128, 2, T], F32); mu = K.sb(st, "rmu", [128, 2, T], F32)
        gb = K.sb(st, "rgb", [128, 2, T], F32); gt = K.sb(st, "rgt", [128, 2, T], F32)
        mo = K.sb(st, "rmo", [128, 2, T], BF16)
        wa = [K.sb(st, f"rwa{i}", [128, 2, 256], BF16) for i in range(2)]; wx = [K.sb(st, f"rwx{i}", [128, 2, 256], BF16) for i in range(2)]
        pss = [K.psum(st, f"rps{i}", [128, 512]) for i in range(8)]
        pi = 0
        ocw = [PRM[f"convw{t}"][0] for t in range(4)]; ocb = PRM["convb"][0]; oba = PRM["ba"][0]; obx = PRM["bx"][0]
        for nb in range(LRU // 256):
            rows = slice(nb * 256, (nb + 1) * 256)
            W1 = wa[nb % 2]; W2 = wx[nb % 2]
            K.dma(K.pool, W1.ap[:], w_ra.ap[nb], reads=[w_ra], writes=[W1], owner=W1)
            K.dma(K.pool, W2.ap[:], w_rx.ap[nb], reads=[w_rx], writes=[W2], owner=W2)
            K.dma(K.sp, xb.ap[:], xbT.ap[rows, :].rearrange("(k p) t -> p k t", p=128), reads=[xbT], writes=[xb], owner=xb)
            K.dma(K.sp, gb.ap[:], gbT.ap[rows, :].rearrange("(k p) t -> p k t", p=128), reads=[gbT], writes=[gb], owner=gb)
            for k in range(2):
                ch = nb * 2 + k
                K.op(K.dve, (lambda k=k, ch=ch: nc.vector.tensor_scalar(out=yc.ap[:, k, :], in0=xb.ap[:, k, :], scalar1=C.prm.ap[:, ocw[0] + ch:ocw[0] + ch + 1],
                                                                        scalar2=C.prm.ap[:, ocb + ch:ocb + ch + 1], op0=ALU.mult, op1=ALU.add)),
                     reads=[xb, C.prm], writes=[yc])
                for tap in range(1, 4):
                    K.op(K.dve, (lambda k=k, ch=ch, tap=tap: nc.vector.scalar_tensor_tensor(out=yc.ap[:, k, tap:], in0=xb.ap[:, k, 0:T - tap],
                                                                                            scalar=C.prm.ap[:, ocw[tap] + ch:ocw[tap] + ch + 1], in1=yc.ap[:, k, tap:],
                                                                                            op0=ALU.mult, op1=ALU.add)), reads=[xb, yc, C.prm], writes=[yc])
                K.op(K.act, (lambda k=k: nc.scalar.copy(out=ycb.ap[:, k, :], in_=yc.ap[:, k, :])), reads=[yc], writes=[ycb])
            for k in range(2):
                ch = nb * 2 + k
                for tt in range(4):
                    cs = slice(tt * 512, (tt + 1) * 512)
                    for (W, dstt, ob) in ((W1, r, oba), (W2, ig, obx)):
                        ps = pss[pi % 8]; pi += 1
                        for ic in range(2):
                            K.op(K.pe, (lambda W=W, ps=ps, ic=ic, k=k, cs=cs: nc.tensor.matmul(ps.ap[:], W.ap[:, ic, k * 128:(k + 1) * 128], ycb.ap[:, ic, cs],
                                                                                           start=(ic == 0), stop=(ic == 1))), reads=[W, ycb], writes=[ps], sig=(ic == 1))
                        K.op(K.act, (lambda ps=ps, dstt=dstt, ob=ob, k=k, cs=cs, ch=ch: nc.scalar.activation(out=dstt.ap[:, k, cs], in_=ps.ap[:], func=AF.Sigmoid,
                                                                                                        bias=C.prm.ap[:, ob + ch:ob + ch + 1])),
                             reads=[ps, C.prm], writes=[dstt])
            for k in range(2):
                ch = nb * 2 + k
                scp = C.drv.ap[:, 24 + ch:25 + ch]
                K.op(K.act, (lambda k=k, scp=scp: nc.scalar.activation(out=a.ap[:, k, :], in_=r.ap[:, k, :], func=AF.Exp, scale=scp)), reads=[r, C.drv], writes=[a])
                K.op(K.act, (lambda k=k: nc.scalar.activation(out=mu.ap[:, k, :], in_=a.ap[:, k, :], func=AF.Square)), reads=[a], writes=[mu])
                K.op(K.dve, (lambda k=k: nc.vector.tensor_scalar(out=mu.ap[:, k, :], in0=mu.ap[:, k, :], scalar1=1.0, scalar2=-1.0, op0=ALU.min, op1=ALU.mult)), reads=[mu], writes=[mu])
                K.op(K.act, (lambda k=k: nc.scalar.activation(out=mu.ap[:, k, :], in_=mu.ap[:, k, :], func=AF.Sqrt, bias=1.0, scale=1.0)), reads=[mu], writes=[mu])
                K.op(K.dve, (lambda k=k: nc.vector.memset(mu.ap[:, k, 0:1], 1.0)), reads=[], writes=[mu])
                K.op(K.dve, (lambda k=k: nc.vector.tensor_tensor(out=mu.ap[:, k, :], in0=mu.ap[:, k, :], in1=ig.ap[:, k, :], op=ALU.mult)), reads=[mu, ig], writes=[mu])
                K.op(K.dve, (lambda k=k: nc.vector.tensor_tensor(out=mu.ap[:, k, :], in0=mu.ap[:, k, :], in1=yc.ap[:, k, :], op=ALU.mult)), reads=[mu, yc], writes=[mu])
                K.op(K.dve, (lambda k=k: nc.vector.tensor_tensor_scan(out=r.ap[:, k, :], data0=a.ap[:, k, :], data1=mu.ap[:, k, :], initial=0.0, op0=ALU.mult, op1=ALU.add)),
                     reads=[a, mu], writes=[r])
                K.op(K.pool, (lambda k=k: nc.gpsimd.tensor_tensor(out=gt.ap[:, k, :], in0=gb.ap[:, k, :], in1=gb.ap[:, k, :], op=ALU.mult)), reads=[gb], writes=[gt])
                K.op(K.pool, (lambda k=k: nc.gpsimd.tensor_scalar(out=gt.ap[:, k, :], in0=gt.ap[:, k, :], scalar1=0.044715, scalar2=1.0, op0=ALU.mult, op1=ALU.add)),
                     reads=[gt], writes=[gt])
                K.op(K.pool, (lambda k=k: nc.gpsimd.tensor_tensor(out=gt.ap[:, k, :], in0=gt.ap[:, k, :], in1=gb.ap[:, k, :], op=ALU.mult)), reads=[gt, gb], writes=[gt])
                K.op(K.act, (lambda k=k: nc.scalar.activation(out=gt.ap[:, k, :], in_=gt.ap[:, k, :], func=AF.Sigmoid, scale=1.5957691216057308)), reads=[gt], writes=[gt])
                K.op(K.pool, (lambda k=k: nc.gpsimd.tensor_tensor(out=gt.ap[:, k, :], in0=gt.ap[:, k, :], in1=gb.ap[:, k, :], op=ALU.mult)), reads=[gt, gb], writes=[gt])
                K.op(K.dve, (lambda k=k: nc.vector.tensor_tensor(out=mo.ap[:, k, :], in0=gt.ap[:, k, :], in1=r.ap[:, k, :], op=ALU.mult)), reads=[gt, r], writes=[mo])
            K.dma(K.sp, mxT.ap[rows, :].rearrange("(k p) t -> p k t", p=128), mo.ap[:], reads=[mo], writes=[mxT], owner=mo)
        K.barrier()
        K.release([xb, gb, mo] + wa + wx)
        st.close()

    token_phase("p0", 0, False, inproj_even)
    if stage >= 2:
        attention_phase()
    if stage >= 3:
        hgrn2_phase()
    if stage >= 4:
        token_phase("p1", 0, True, inproj_odd)
    if stage >= 5:
        lru_phase()
    if stage >= 6:
        token_phase("p2", 1, True, None)
    K.barrier()
    gs.close()
    es.close()
    return nc


def host_inputs(inp):
    f = lambda a: np.ascontiguousarray(np.asarray(a, dtype=np.float32))
    sh = {}
    wi = f(inp["w_in_even"][0])
    names = ["w_sq", "w_sk", "w_sv", "w_hq", "w_hf", "w_hi", "w_hg"]
    for i, nm in enumerate(names):
        sh[nm] = _wl(wi[:, i * 1024:(i + 1) * 1024], 512)
    sh["w_oe"] = _wl(f(inp["w_out_even"][0]), 512)
    sh["w_io"] = _wl(f(inp["w_in_odd"][0]), 512)
    sh["w_oo"] = _wl(f(inp["w_out_odd"][0]), 256)
    for i in range(2):
        gu = f(inp["w_gate_up"][i])
        g = gu[:, :DFF].reshape(D, FC // 2, 256); u = gu[:, DFF:].reshape(D, FC // 2, 256)
        sh[f"w_gu{i}"] = _wl(np.concatenate([g, u], axis=2).reshape(D, FC * 256), 512)
        sh[f"w_dn{i}"] = _wl(f(inp["w_down"][i]), 128)
        sh[f"w_pg{i}"] = _wl(f(inp["w_ple_gate"][i]), 512)
        sh[f"w_pu{i}"] = _wl(f(inp["w_ple_up"][i]), 2048)
    for nm, key in (("w_ra", "rg_wa"), ("w_rx", "rg_wx")):
        w = f(inp[key][0])
        sh[nm] = np.ascontiguousarray(w.reshape(11, 2, 128, 256).transpose(0, 2, 1, 3))
    prm = np.zeros((128, NPRM), np.float32)
    def put(name, arr):
        off, n = PRM[name]; prm[:, off:off + n] = arr
    for i in range(2):
        for nm in ("mix_pre_g", "mix_post_g", "ffn_pre_g", "ffn_post_g", "ple_norm_g"):
            put(f"{nm}{i}", _cols(f(inp[nm][i])))
    put("lb0", _cols(f(inp["hg_lb_logits"][0]))); put("lb1", _cols(f(inp["hg_lb_logits"][1])))
    put("hgn", f(inp["hg_norm_g"][0]).reshape(128, 1))
    for tap in range(4):
        put(f"convw{tap}", _cols(f(inp["conv_w"][0, tap])))
    put("convb", _cols(f(inp["conv_b"][0]))); put("ba", _cols(f(inp["rg_ba"][0]).reshape(-1))); put("bx", _cols(f(inp["rg_bx"][0]).reshape(-1)))
    put("lam", _cols(f(inp["rg_lambda"][0])))
    sh["prm"] = prm
    for k, v in _consts().items():
        sh["c_" + k] = v
    return sh


def kernel(**inp):
    sh = host_inputs(inp)
    x = np.asarray(inp["x"], np.float32); p = np.asarray(inp["p"], np.float32)
    nc = build()
    in_maps = []
    for b in range(8):
        m = dict(sh)
        m["xT"] = np.ascontiguousarray(x[b].T)
        m["pT"] = np.ascontiguousarray(p[:, b].transpose(0, 2, 1))
        in_maps.append(m)
    res = run_bass_kernel_spmd(nc, in_maps, core_ids=list(range(8)))
    out = np.stack([np.ascontiguousarray(res.results[b]["outT"].T) for b in range(8)], axis=0)
    return out.astype(np.float32)
```

```python
import contextlib
import numpy as np
import concourse.bass as bass
import concourse.mybir as mybir
from concourse.bass_utils import run_bass_kernel_spmd

F32 = mybir.dt.float32
F32R = mybir.dt.float32r
BF16 = mybir.dt.bfloat16
AF = mybir.ActivationFunctionType
ALU = mybir.AluOpType

D = 2048
T = 2048
TT = 512
NTT = T // TT
KC = D // 128
DFF = 5632
FC = DFF // 128
LRU = 2816
LC = LRU // 128
PLE = 256
EPS = 1e-6
WSLOT = 8192


def _wl(w, cb):
    K, N = w.shape
    return np.ascontiguousarray(w.reshape(K // 128, 128, N // cb, cb).transpose(2, 1, 0, 3))


def _cols(v):
    return np.ascontiguousarray(v.reshape(-1, 128).T)


PRM = {}


def _prm_layout():
    off = 0
    def add(name, n):
        nonlocal off
        PRM[name] = (off, n)
        off += n
    for i in range(2):
        for nm in ("mix_pre_g", "mix_post_g", "ffn_pre_g", "ffn_post_g", "ple_norm_g"):
            add(f"{nm}{i}", 16)
    add("lb0", 8); add("lb1", 8); add("hgn", 1)
    for tap in range(4):
        add(f"convw{tap}", LC)
    add("convb", LC); add("ba", LC); add("bx", LC); add("lam", LC)
    return off


NPRM = _prm_layout()


def _consts():
    c = {}
    s = np.arange(128)[:, None]; t = np.arange(128)[None, :]
    c["ident"] = np.eye(128, dtype=np.float32)
    c["ones"] = np.ones((128, 128), np.float32)
    c["zeros"] = np.zeros((128, 512), np.float32)
    c["maskneg"] = np.where(s >= t, -30000.0, 0.0).astype(np.float32)
    c["mask01"] = (s <= t).astype(np.float32)
    c["negU"] = np.where(s >= t, -1.0, 0.0).astype(np.float32)
    c["negOnes"] = -np.ones((128, 128), np.float32)
    tt = np.arange(T)
    c["rm128"] = np.broadcast_to((tt % 128 != 0).astype(np.float32), (128, T)).copy()
    c["rm16"] = np.broadcast_to((tt % 16 != 0).astype(np.float32), (128, T)).copy()
    return c


class Tl:
    def __init__(self, name, ap):
        self.name = name; self.ap = ap
        self.w = {}; self.r = {}
        self.dsem = None; self.dcnt = 0
        self.dram = False

    def __getitem__(self, k):
        return self.ap[k]


class Eng:
    def __init__(self, name, eng, sem):
        self.name = name; self.eng = eng; self.sem = sem; self.cnt = 0
        self.waited = {}; self.pend = []


class Kern:
    def __init__(self, nc, es):
        self.nc = nc; self.es = es
        self.pe = Eng("pe", nc.tensor, es.enter_context(nc.semaphore("s_pe")))
        self.act = Eng("act", nc.scalar, es.enter_context(nc.semaphore("s_act")))
        self.dve = Eng("dve", nc.vector, es.enter_context(nc.semaphore("s_dve")))
        self.pool = Eng("pool", nc.gpsimd, es.enter_context(nc.semaphore("s_pool")))
        self.sp = Eng("sp", nc.sync, es.enter_context(nc.semaphore("s_sp")))
        self.engs = [self.pe, self.act, self.dve, self.pool, self.sp]
        self.dsems = []
        self.nsem = 5

    def sb(self, stack, name, shape, dt):
        return Tl(name, stack.enter_context(self.nc.sbuf_tensor(name, list(shape), dt)))

    def psum(self, stack, name, shape, dt=F32):
        return Tl(name, stack.enter_context(self.nc.psum_tensor(name, list(shape), dt)))

    def dram(self, name, shape, dt, kind="Internal"):
        t = Tl(name, self.nc.dram_tensor(name, list(shape), dt, kind=kind).ap())
        t.dram = True
        return t

    def _wait(self, E, ev):
        for sid, (sem, val) in ev.items():
            if E is self.pe and sid == id(E.sem):
                continue
            if E.waited.get(sid, 0) < val:
                E.eng.wait_ge(sem, val)
                E.waited[sid] = val

    def _deps(self, E, reads, writes):
        ev = {}
        def mrg(d):
            for sid, (sem, val) in d.items():
                if sid not in ev or ev[sid][1] < val:
                    ev[sid] = (sem, val)
        for t in reads:
            mrg(t.w)
        for t in writes:
            mrg(t.w); mrg(t.r)
        self._wait(E, ev)

    def _record(self, ev, reads, writes):
        sid, sem, val = ev
        for t in writes:
            t.w = {sid: (sem, val)}; t.r = {}
        for t in reads:
            t.r[sid] = (sem, val)

    def op(self, E, fn, reads=(), writes=(), sig=True):
        self._deps(E, reads, writes)
        inst = fn()
        if sig:
            E.cnt += 1
            inst.then_inc(E.sem, 1)
            ev = (id(E.sem), E.sem, E.cnt)
            self._record(ev, reads, writes)
            for (r, w) in E.pend:
                self._record(ev, r, w)
            E.pend = []
        else:
            ev = (id(E.sem), E.sem, E.cnt + 1)
            self._record(ev, reads, writes)
        return inst

    def dma(self, Q, out_ap, in_ap, reads, writes, owner):
        reads = [t for t in reads if not t.dram]
        writes = [t for t in writes if not t.dram]
        self._deps(Q, reads, writes)
        if owner.dsem is None:
            owner.dsem = self.es.enter_context(self.nc.semaphore("d_" + owner.name))
            self.dsems.append(owner); self.nsem += 1
        owner.dcnt += 16
        Q.eng.dma_start(out=out_ap, in_=in_ap).then_inc(owner.dsem, 16)
        ev = (id(owner.dsem), owner.dsem, owner.dcnt)
        sid, sem, val = ev
        for t in writes:
            t.w = {sid: (sem, val)}; t.r = {}
        for t in reads:
            t.r[sid] = (sem, val)

    def barrier(self):
        ev = {}
        for E in self.engs:
            if E.cnt:
                ev[id(E.sem)] = (E.sem, E.cnt)
        for t in self.dsems:
            ev[id(t.dsem)] = (t.dsem, t.dcnt)
        for E in self.engs:
            self._wait(E, ev)

    def release(self, tiles):
        self.dsems = [t for t in self.dsems if t not in tiles]


class Ctx:
    pass


def wview(slot, kc, cb):
    return slot.ap[:, 0:kc * cb].rearrange("p (k c) -> p k c", c=cb)


def load_w(K, C, wl_dram, nb, kc, cb):
    slot = C.wslots[C.wi % len(C.wslots)]; C.wi += 1
    K.dma(K.pool, wview(slot, kc, cb), wl_dram.ap[nb], reads=[wl_dram], writes=[slot], owner=slot)
    return slot


def mm_group(K, C, ps_t, ps_ap, pairs, reads):
    n = len(pairs)
    for i, (l, r) in enumerate(pairs):
        K.op(K.pe, (lambda l=l, r=r, i=i: K.nc.tensor.matmul(ps_ap, l, r, start=(i == 0), stop=(i == n - 1))),
             reads=reads, writes=[ps_t], sig=(i == n - 1))


def next_ps(C):
    t = C.ps[C.pi % len(C.ps)]; C.pi += 1
    return t


def stats_begin(C, n, lag):
    C.statk = 0; C.statn = n; C.pend = []; C.lag = lag


def ones_mm(K, C):
    sq = C.pend.pop(0)
    k = C.statk; C.statk += 1
    K.op(K.pe, (lambda: K.nc.tensor.matmul(C.stat.ap[:], C.ones.ap[:], sq.ap[:], start=(k == 0), stop=(k == C.statn - 1))),
         reads=[sq, C.ones], writes=[C.stat], sig=True)


def sq_push(K, C, t, ap, eng):
    sq = C.sq[C.sqi % len(C.sq)]; C.sqi += 1
    if eng == "act":
        K.op(K.act, (lambda: K.nc.scalar.activation(out=sq.ap[:], in_=ap, func=AF.Square)), reads=[t], writes=[sq])
    else:
        K.op(K.dve, (lambda: K.nc.vector.tensor_tensor(out=sq.ap[:], in0=ap, in1=ap, op=ALU.mult)), reads=[t], writes=[sq])
    C.pend.append(sq)
    if len(C.pend) > C.lag:
        ones_mm(K, C)


def stats_finish(K, C, scale, out_rstd):
    while C.pend:
        ones_mm(K, C)
    assert C.statk == C.statn
    K.op(K.act, lambda: K.nc.scalar.activation(out=out_rstd.ap[:], in_=C.stat.ap[:], func=AF.Sqrt, bias=C.epsb.ap[:, 0:1], scale=scale),
         reads=[C.stat, C.epsb], writes=[out_rstd])
    K.op(K.dve, lambda: K.nc.vector.reciprocal(out=out_rstd.ap[:], in_=out_rstd.ap[:]), reads=[out_rstd], writes=[out_rstd])


def prm(C, name, j=None):
    off, n = PRM[name]
    if j is None:
        return C.prm.ap[:, off:off + n]
    return C.prm.ap[:, off + j:off + j + 1]


def norm_bf16(K, C, src, gname, dst, have_stats=False):
    if not have_stats:
        stats_begin(C, KC, 1)
        for kc in range(KC):
            sq_push(K, C, src, src.ap[:, kc, :], "act")
    stats_finish(K, C, 1.0 / D, C.rstd)
    for kc in range(KC):
        K.op(K.dve, (lambda kc=kc: K.nc.vector.scalar_tensor_tensor(out=dst.ap[:, kc, :], in0=src.ap[:, kc, :], scalar=prm(C, gname, kc),
                                                                   in1=C.rstd.ap[:], op0=ALU.mult, op1=ALU.mult)),
             reads=[src, C.rstd, C.prm], writes=[dst])


def post_norm_res(K, C, y, gname, h, follow=None, nT=None):
    stats_finish(K, C, 1.0 / D, C.rstd2)
    if follow == "norm":
        stats_begin(C, KC, 1)
    for kc in range(KC):
        K.op(K.dve, (lambda kc=kc: K.nc.vector.scalar_tensor_tensor(out=y.ap[:, kc, :], in0=y.ap[:, kc, :], scalar=prm(C, gname, kc),
                                                                   in1=C.rstd2.ap[:], op0=ALU.mult, op1=ALU.mult)),
             reads=[y, C.rstd2, C.prm], writes=[y])
        K.op(K.dve, (lambda kc=kc: K.nc.vector.tensor_tensor(out=h.ap[:, kc, :], in0=h.ap[:, kc, :], in1=y.ap[:, kc, :], op=ALU.add)),
             reads=[y, h], writes=[h])
        if follow == "norm":
            sq_push(K, C, h, h.ap[:, kc, :], "act")
        elif follow == "copy":
            K.op(K.act, (lambda kc=kc: K.nc.scalar.copy(out=nT.ap[:, kc, :], in_=h.ap[:, kc, :])), reads=[h], writes=[nT])


def linear_fm(K, C, inT, kcn, wl_dram, ncols, cb, epi):
    nblk = ncols // cb
    for nb in range(nblk):
        slot = load_w(K, C, wl_dram, nb, kcn, cb)
        wv = wview(slot, kcn, cb)
        for ci in range(cb // 128):
            ps = next_ps(C)
            mm_group(K, C, ps, ps.ap[:], [(wv[:, kc, ci * 128:(ci + 1) * 128], inT.ap[:, kc, :]) for kc in range(kcn)], reads=[slot, inT])
            epi(nb * (cb // 128) + ci, ps)


def stage_out(C, dt):
    lst = C.stg32 if dt == F32 else C.stg16
    t = lst[C.stgi[dt] % len(lst)]; C.stgi[dt] += 1
    return t


def build(stage=99, dbg=()):
    nc = bass.Bass("TRN2", target_bir_lowering=False)
    es = contextlib.ExitStack()
    K = Kern(nc, es)
    C = Ctx()
    dr = {}

    def ein(name, shape, dt=F32):
        dr[name] = K.dram(name, shape, dt, kind="ExternalInput"); return dr[name]

    def scr(name, shape, dt, out=False):
        dr[name] = K.dram(name, shape, dt, kind=("ExternalOutput" if (out or name in dbg) else "Internal")); return dr[name]

    xT = ein("xT", [D, T]); pT = ein("pT", [2, PLE, T]); prm_d = ein("prm", [128, NPRM])
    cst = {k: ein("c_" + k, list(v.shape)) for k, v in _consts().items()}
    w_sq = ein("w_sq", [2, 128, KC, 512]); w_sk = ein("w_sk", [2, 128, KC, 512]); w_sv = ein("w_sv", [2, 128, KC, 512])
    w_hq = ein("w_hq", [2, 128, KC, 512]); w_hf = ein("w_hf", [2, 128, KC, 512]); w_hi = ein("w_hi", [2, 128, KC, 512])
    w_hg = ein("w_hg", [2, 128, KC, 512])
    w_oe = ein("w_oe", [4, 128, KC, 512])
    w_io = ein("w_io", [2 * LRU // 512, 128, KC, 512])
    w_oo = ein("w_oo", [D // 256, 128, LC, 256])
    w_gu = [ein(f"w_gu{i}", [FC // 2, 128, KC, 512]) for i in range(2)]
    w_dn = [ein(f"w_dn{i}", [D // 128, 128, FC, 128]) for i in range(2)]
    w_pg = [ein(f"w_pg{i}", [4, 128, KC, 512]) for i in range(2)]
    w_pu = [ein(f"w_pu{i}", [1, 128, 2, 2048]) for i in range(2)]
    w_ra = ein("w_ra", [LRU // 256, 128, 2, 256]); w_rx = ein("w_rx", [LRU // 256, 128, 2, 256])

    outT = scr("outT", [D, T], F32, out=True)
    hT = scr("hT", [D, T], F32)
    qT = scr("qT", [1024, T], BF16); kT = scr("kT", [1024, T], BF16); vtok = scr("vtok", [8, T, 128], BF16)
    hqT = scr("hqT", [1024, T], F32); lfT = scr("lfT", [1024, T], F32); kkT = scr("kkT", [1024, T], F32)
    hgT = scr("hgT", [1024, T], BF16); hitok = scr("hitok", [8, T, 128], BF16)
    ccT = scr("ccT", [D, T], BF16)
    gbT = scr("gbT", [LRU, T], F32); xbT = scr("xbT", [LRU, T], F32)
    mxT = scr("mxT", [LRU, T], BF16)

    dbgt = {nm: scr(nm, [D, T], F32) for nm in ("dbg_m", "dbg_hmix", "dbg_hffn") if nm in dbg}
    def dbg_store(nm, tile, t0, tag):
        if nm in dbgt and tag == "p1":
            K.dma(K.sp, dbgt[nm].ap.rearrange("(k p) t -> p k t", p=128)[:, :, t0:t0 + TT], tile.ap[:], reads=[tile], writes=[dbgt[nm]], owner=tile)

    def kcv(d, t0):
        return d.ap.rearrange("(k p) t -> p k t", p=128)[:, :, t0:t0 + TT]

    gs = contextlib.ExitStack()
    C.prm = K.sb(gs, "prm_sb", [128, NPRM], F32)
    C.ones = K.sb(gs, "ones_sb", [128, 128], BF16)
    C.ident = K.sb(gs, "ident_sb", [128, 128], BF16)
    C.epsb = K.sb(gs, "epsb", [128, 1], F32)
    C.drv = K.sb(gs, "drv", [128, 64], F32)
    K.dma(K.sp, C.prm.ap[:], prm_d.ap[:, :], reads=[prm_d], writes=[C.prm], owner=C.prm)
    K.dma(K.pool, C.ones.ap[:], cst["ones"].ap[:, :], reads=[], writes=[C.ones], owner=C.ones)
    K.dma(K.pool, C.ident.ap[:], cst["ident"].ap[:, :], reads=[], writes=[C.ident], owner=C.ident)
    K.op(K.dve, lambda: nc.vector.memset(C.epsb.ap[:], EPS), writes=[C.epsb])
    o0, _ = PRM["lb0"]; o1, _ = PRM["lb1"]
    K.op(K.dve, lambda: nc.vector.tensor_tensor(out=C.drv.ap[:, 0:8], in0=C.prm.ap[:, o0:o0 + 8], in1=C.prm.ap[:, o1:o1 + 8], op=ALU.subtract),
         reads=[C.prm], writes=[C.drv])
    K.op(K.act, lambda: nc.scalar.activation(out=C.drv.ap[:, 0:8], in_=C.drv.ap[:, 0:8], func=AF.Sigmoid), reads=[C.drv], writes=[C.drv])
    K.op(K.dve, lambda: nc.vector.tensor_scalar(out=C.drv.ap[:, 8:16], in0=C.drv.ap[:, 0:8], scalar1=-1.0, scalar2=1.0, op0=ALU.mult, op1=ALU.add),
         reads=[C.drv], writes=[C.drv])
    K.op(K.dve, lambda: nc.vector.tensor_scalar(out=C.drv.ap[:, 16:24], in0=C.drv.ap[:, 8:16], scalar1=-1.0, scalar2=None, op0=ALU.mult),
         reads=[C.drv], writes=[C.drv])
    ol, _ = PRM["lam"]
    K.op(K.act, lambda: nc.scalar.activation(out=C.drv.ap[:, 24:46], in_=C.prm.ap[:, ol:ol + LC], func=AF.Exp, scale=-1.0), reads=[C.prm], writes=[C.drv])
    K.op(K.act, lambda: nc.scalar.activation(out=C.drv.ap[:, 24:46], in_=C.drv.ap[:, 24:46], func=AF.Ln, bias=1.0), reads=[C.drv], writes=[C.drv])
    K.op(K.dve, lambda: nc.vector.tensor_scalar(out=C.drv.ap[:, 24:46], in0=C.drv.ap[:, 24:46], scalar1=-8.0, scalar2=None, op0=ALU.mult),
         reads=[C.drv], writes=[C.drv])

    def token_phase(tag, layer, do_mix_ffn, inproj):
        ps_stack = contextlib.ExitStack()
        C.wslots = [K.sb(ps_stack, f"ws{i}_{tag}", [128, WSLOT], BF16) for i in range(3)]; C.wi = 0
        C.ps = [K.psum(ps_stack, f"ps{i}_{tag}", [128, TT]) for i in range(7)]; C.pi = 0
        C.stat = K.psum(ps_stack, f"stat_{tag}", [128, TT])
        C.sq = [K.sb(ps_stack, f"sq{i}_{tag}", [128, TT], BF16) for i in range(4)]; C.sqi = 0
        C.rstd = K.sb(ps_stack, f"rstd_{tag}", [128, TT], F32)
        C.rstd2 = K.sb(ps_stack, f"rstd2_{tag}", [128, TT], F32)
        C.stg32 = [K.sb(ps_stack, f"st32_{i}_{tag}", [128, TT], F32) for i in range(3)]
        C.stg16 = [K.sb(ps_stack, f"st16_{i}_{tag}", [128, TT], BF16) for i in range(3)]
        C.stgi = {F32: 0, BF16: 0}
        h = K.sb(ps_stack, f"h_{tag}", [128, KC, TT], F32)
        nT = K.sb(ps_stack, f"nT_{tag}", [128, KC, TT], BF16)
        if do_mix_ffn:
            act = K.sb(ps_stack, f"act_{tag}", [128, FC, TT], BF16)
            y = K.sb(ps_stack, f"y_{tag}", [128, KC, TT], F32)
            pt = K.sb(ps_stack, f"pt_{tag}", [128, 2, TT], BF16)
            tmp = [K.sb(ps_stack, f"tmp{i}_{tag}", [128, TT], F32) for i in range(2)]
            wpu = K.sb(ps_stack, f"wpu_{tag}", [128, 2 * D], BF16)
            K.dma(K.pool, wview(wpu, 2, D), w_pu[layer].ap[0], reads=[w_pu[layer]], writes=[wpu], owner=wpu)
        h_src = xT if layer == 0 else hT
        for tt in range(NTT):
            t0 = tt * TT
            K.dma(K.sp, h.ap[:], kcv(h_src, t0), reads=[h_src], writes=[h], owner=h)
            if do_mix_ffn:
                if layer == 0:
                    cc, ckc, wo, wcb = ccT, KC, w_oe, 512
                else:
                    cc, ckc, wo, wcb = mxT, LC, w_oo, 256
                if tt == 0:
                    K.dma(K.sp, act.ap[:, 0:ckc, :], kcv(cc, t0), reads=[cc], writes=[act], owner=act)
                def epi_y(c, ps):
                    K.op(K.act, lambda: nc.scalar.copy(out=y.ap[:, c, :], in_=ps.ap[:]), reads=[ps], writes=[y])
                    sq_push(K, C, y, y.ap[:, c, :], "dve")
                stats_begin(C, KC, 2)
                linear_fm(K, C, act, ckc, wo, D, wcb, epi_y)
                dbg_store("dbg_m", y, t0, tag)
                post_norm_res(K, C, y, f"mix_post_g{layer}", h, follow="norm")
                dbg_store("dbg_hmix", h, t0, tag)
                norm_bf16(K, C, h, f"ffn_pre_g{layer}", nT, have_stats=True)
                for nb in range(FC // 2):
                    slot = load_w(K, C, w_gu[layer], nb, KC, 512)
                    wv = wview(slot, KC, 512)
                    for ci in range(2):
                        psg = next_ps(C); psu = next_ps(C)
                        mm_group(K, C, psg, psg.ap[:], [(wv[:, kc, ci * 128:(ci + 1) * 128], nT.ap[:, kc, :]) for kc in range(KC)], reads=[slot, nT])
                        mm_group(K, C, psu, psu.ap[:], [(wv[:, kc, 256 + ci * 128:256 + (ci + 1) * 128], nT.ap[:, kc, :]) for kc in range(KC)], reads=[slot, nT])
                        tm = tmp[(nb * 2 + ci) % 2]; fc = nb * 2 + ci
                        K.op(K.act, lambda: nc.scalar.activation(out=tm.ap[:], in_=psg.ap[:], func=AF.Silu), reads=[psg], writes=[tm])
                        K.op(K.dve, lambda: nc.vector.tensor_tensor(out=act.ap[:, fc, :], in0=tm.ap[:], in1=psu.ap[:], op=ALU.mult), reads=[tm, psu], writes=[act])
                stats_begin(C, KC, 2)
                linear_fm(K, C, act, FC, w_dn[layer], D, 128, epi_y)
                if tt + 1 < NTT:
                    K.dma(K.sp, act.ap[:, 0:ckc, :], kcv(cc, t0 + TT), reads=[cc], writes=[act], owner=act)
                post_norm_res(K, C, y, f"ffn_post_g{layer}", h, follow="copy", nT=nT)
                dbg_store("dbg_hffn", h, t0, tag)
                stats_begin(C, KC, 2)
                K.dma(K.pool, pt.ap[:], pT.ap[layer].rearrange("(k p) t -> p k t", p=128)[:, :, t0:t0 + TT], reads=[pT], writes=[pt], owner=pt)
                uslot = wpu
                uv = wview(wpu, 2, D)
                for nb in range(4):
                    slot = load_w(K, C, w_pg[layer], nb, KC, 512)
                    wv = wview(slot, KC, 512)
                    for ci in range(4):
                        c = nb * 4 + ci
                        psg = next_ps(C); pse = next_ps(C)
                        mm_group(K, C, psg, psg.ap[:], [(wv[:, kc, ci * 128:(ci + 1) * 128], nT.ap[:, kc, :]) for kc in range(KC)], reads=[slot, nT])
                        mm_group(K, C, pse, pse.ap[:], [(uv[:, k2, c * 128:(c + 1) * 128], pt.ap[:, k2, :]) for k2 in range(2)], reads=[uslot, pt])
                        tm = tmp[c % 2]
                        K.op(K.act, lambda: nc.scalar.activation(out=tm.ap[:], in_=psg.ap[:], func=AF.Sigmoid), reads=[psg], writes=[tm])
                        K.op(K.dve, lambda: nc.vector.tensor_tensor(out=y.ap[:, c, :], in0=tm.ap[:], in1=pse.ap[:], op=ALU.mult), reads=[tm, pse], writes=[y])
                        sq_push(K, C, y, y.ap[:, c, :], "dve")
                post_norm_res(K, C, y, f"ple_norm_g{layer}", h, follow=("norm" if inproj is not None else None))
                h_dst = hT if inproj is not None else outT
                K.dma(K.sp, kcv(h_dst, t0), h.ap[:], reads=[h], writes=[h_dst], owner=h)
            if inproj is not None:
                inproj(K, C, h, nT, t0, do_mix_ffn)
        K.barrier()
        K.release(C.wslots + [h, nT] + ([act, pt, wpu] if do_mix_ffn else []))
        ps_stack.close()

    def inproj_even(K, C, h, nT, t0, have_stats):
        norm_bf16(K, C, h, "mix_pre_g0", nT, have_stats=have_stats)
        def store(t, dst, c):
            K.dma(K.sp, dst.ap[c * 128:(c + 1) * 128, t0:t0 + TT], t.ap[:], reads=[t], writes=[dst], owner=t)
        def epi_q(c, ps):
            t = stage_out(C, BF16)
            K.op(K.act, lambda: nc.scalar.activation(out=t.ap[:], in_=ps.ap[:], func=AF.Copy, scale=128.0 ** -0.5), reads=[ps], writes=[t]); store(t, qT, c)
        def epi_k(c, ps):
            t = stage_out(C, BF16)
            K.op(K.act, lambda: nc.scalar.copy(out=t.ap[:], in_=ps.ap[:]), reads=[ps], writes=[t]); store(t, kT, c)
        def epi_hq(c, ps):
            t = stage_out(C, F32)
            K.op(K.act, lambda: nc.scalar.activation(out=t.ap[:], in_=ps.ap[:], func=AF.Silu), reads=[ps], writes=[t]); store(t, hqT, c)
        def epi_hg(c, ps):
            t = stage_out(C, BF16)
            K.op(K.act, lambda: nc.scalar.activation(out=t.ap[:], in_=ps.ap[:], func=AF.Silu), reads=[ps], writes=[t]); store(t, hgT, c)
        def epi_hf(c, ps):
            s = stage_out(C, F32); t1 = stage_out(C, F32); t2 = stage_out(C, F32)
            K.op(K.act, lambda: nc.scalar.activation(out=s.ap[:], in_=ps.ap[:], func=AF.Sigmoid), reads=[ps], writes=[s])
            K.op(K.act, lambda: nc.scalar.activation(out=t1.ap[:], in_=s.ap[:], func=AF.Ln, bias=C.drv.ap[:, c:c + 1], scale=C.drv.ap[:, 8 + c:9 + c]),
                 reads=[s, C.drv], writes=[t1]); store(t1, lfT, c)
            K.op(K.dve, lambda: nc.vector.tensor_scalar(out=t2.ap[:], in0=s.ap[:], scalar1=C.drv.ap[:, 16 + c:17 + c], scalar2=C.drv.ap[:, 8 + c:9 + c],
                                                        op0=ALU.mult, op1=ALU.add), reads=[s, C.drv], writes=[t2]); store(t2, kkT, c)
        linear_fm(K, C, nT, KC, w_sq, 1024, 512, epi_q)
        linear_fm(K, C, nT, KC, w_sk, 1024, 512, epi_k)
        linear_fm(K, C, nT, KC, w_hq, 1024, 512, epi_hq)
        linear_fm(K, C, nT, KC, w_hf, 1024, 512, epi_hf)
        linear_fm(K, C, nT, KC, w_hg, 1024, 512, epi_hg)
        for (wl, dst) in ((w_sv, vtok), (w_hi, hitok)):
            for nb in range(2):
                slot = load_w(K, C, wl, nb, KC, 512); wv = wview(slot, KC, 512)
                for tb in range(TT // 128):
                    ps = next_ps(C)
                    mm_group(K, C, ps, ps.ap[:], [(nT.ap[:, kc, tb * 128:(tb + 1) * 128], wv[:, kc, :]) for kc in range(KC)], reads=[slot, nT])
                    t = stage_out(C, BF16)
                    K.op(K.act, lambda: nc.scalar.copy(out=t.ap[:], in_=ps.ap[:]), reads=[ps], writes=[t])
                    K.dma(K.sp, dst.ap[nb * 4:(nb + 1) * 4, t0 + tb * 128:t0 + (tb + 1) * 128, :].rearrange("h t d -> t h d"),
                          t.ap[:].rearrange("t (h d) -> t h d", d=128), reads=[t], writes=[dst], owner=t)

    def inproj_odd(K, C, h, nT, t0, have_stats):
        norm_bf16(K, C, h, "mix_pre_g1", nT, have_stats=have_stats)
        def epi(c, ps):
            t = stage_out(C, F32)
            K.op(K.act, lambda: nc.scalar.copy(out=t.ap[:], in_=ps.ap[:]), reads=[ps], writes=[t])
            dst, cc = (gbT, c) if c < LC else (xbT, c - LC)
            K.dma(K.sp, dst.ap[cc * 128:(cc + 1) * 128, t0:t0 + TT], t.ap[:], reads=[t], writes=[dst], owner=t)
        linear_fm(K, C, nT, KC, w_io, 2 * LRU, 512, epi)

    def attention_phase():
        st = contextlib.ExitStack()
        NS = 4
        cm = {}
        for nm, dt in (("maskneg", BF16), ("negU", F32R), ("negOnes", F32R)):
            cm[nm] = K.sb(st, "c_" + nm + "_sb", [128, 128], dt)
            K.dma(K.pool, cm[nm].ap[:], cst[nm].ap[:, :], reads=[], writes=[cm[nm]], owner=cm[nm])
        zer = K.sb(st, "zer_sb", [128, 512], BF16)
        K.dma(K.pool, zer.ap[:], cst["zeros"].ap[:, :], reads=[], writes=[zer], owner=zer)
        zer32 = K.sb(st, "zer32_sb", [128, 512], F32)
        K.dma(K.sp, zer32.ap[:], cst["zeros"].ap[:, :], reads=[], writes=[zer32], owner=zer32)
        S = []
        for s in range(NS):
            o = Ctx()
            o.q = K.sb(st, f"aq{s}", [128, T], BF16); o.k = K.sb(st, f"ak{s}", [128, T], BF16); o.v = K.sb(st, f"av{s}", [128, 16, 128], BF16)
            o.e = K.sb(st, f"ae{s}", [128, 512], F32)
            o.sp = [K.sb(st, f"asp{s}_{i}", [128, 512], F32R) for i in range(2)]
            o.A = K.sb(st, f"aA{s}", [128, 512], F32R)
            o.w = [K.sb(st, f"aw{s}_{i}", [128, 512], BF16) for i in range(2)]
            o.ob = K.sb(st, f"aob{s}", [128, 512], BF16)
            o.zps = K.psum(st, f"azps{s}", [128, 512]); o.ops = K.psum(st, f"aops{s}", [128, 512])
            S.append(o)
        for hg in range(8 // NS):
            for s, o in enumerate(S):
                hd = hg * NS + s
                K.dma(K.sp, o.q.ap[:], qT.ap[hd * 128:(hd + 1) * 128, :], reads=[qT], writes=[o.q], owner=o.q)
                K.dma(K.sp, o.k.ap[:], kT.ap[hd * 128:(hd + 1) * 128, :], reads=[kT], writes=[o.k], owner=o.k)
                K.dma(K.sp, o.v.ap[:], vtok.ap[hd].rearrange("(sb p) d -> p sb d", p=128), reads=[vtok], writes=[o.v], owner=o.v)
            step = 0
            for tq in range(4):
                nblk = 4 * tq + 4
                for o in S:
                    K.op(K.pe, (lambda o=o: nc.tensor.matmul(o.ops.ap[:], zer.ap[:, 0:128], zer.ap[:], start=True, stop=False)), reads=[zer], writes=[o.ops], sig=False)
                    K.op(K.dve, (lambda o=o: nc.vector.tensor_copy(out=o.A.ap[:], in_=zer32.ap[:])), reads=[zer32], writes=[o.A])
                for bi in range(nblk):
                    sb = nblk - 1 - bi
                    c0 = max(0, 128 * sb - 512 * tq); diag = 128 * sb >= 512 * tq
                    q0 = 512 * tq + c0
                    par = step % 2; step += 1
                    for o in S:
                        K.op(K.pe, (lambda o=o: nc.tensor.matmul(o.zps.ap[:, c0:512], o.k.ap[:, sb * 128:(sb + 1) * 128], o.q.ap[:, q0:512 * tq + 512],
                                                                 start=True, stop=False)), reads=[o.k, o.q], writes=[o.zps], sig=not diag)
                        if diag:
                            K.op(K.pe, (lambda o=o: nc.tensor.matmul(o.zps.ap[:, c0:c0 + 128], C.ident.ap[:], cm["maskneg"].ap[:], start=False, stop=False)),
                                 reads=[C.ident, cm["maskneg"]], writes=[o.zps], sig=True)
                    for o in S:
                        K.op(K.act, (lambda o=o: nc.scalar.activation(out=o.e.ap[:, c0:512], in_=o.zps.ap[:, c0:512], func=AF.Exp)), reads=[o.zps], writes=[o.e])
                        K.op(K.act, (lambda o=o: nc.scalar.activation(out=o.sp[par].ap[:, c0:512], in_=o.e.ap[:, c0:512], func=AF.Ln, bias=1.0)),
                             reads=[o.e], writes=[o.sp[par]])
                    for o in S:
                        last = (bi == 0)
                        K.op(K.pe, (lambda o=o: nc.tensor.matmul(o.zps.ap[:, c0:512], cm["negU"].ap[:], o.sp[par].ap[:, c0:512], start=False, stop=last)),
                             reads=[cm["negU"], o.sp[par]], writes=[o.zps], sig=last)
                        if not last:
                            K.op(K.pe, (lambda o=o: nc.tensor.matmul(o.zps.ap[:, c0:512], cm["negOnes"].ap[:], o.A.ap[:, c0:512], start=False, stop=True)),
                                 reads=[cm["negOnes"], o.A], writes=[o.zps], sig=True)
                    for o in S:
                        K.op(K.act, (lambda o=o: nc.scalar.activation(out=o.w[par].ap[:, c0:512], in_=o.zps.ap[:, c0:512], func=AF.Exp)),
                             reads=[o.zps], writes=[o.w[par]])
                        if sb > 0:
                            K.op(K.dve, (lambda o=o: nc.vector.tensor_tensor(out=o.A.ap[:, c0:512], in0=o.A.ap[:, c0:512].bitcast(F32), in1=o.sp[par].ap[:, c0:512].bitcast(F32), op=ALU.add)),
                                 reads=[o.A, o.sp[par]], writes=[o.A])
                    for o in S:
                        K.op(K.pe, (lambda o=o: nc.tensor.matmul(o.ops.ap[:, c0:512], o.v.ap[:, sb, :], o.w[par].ap[:, c0:512], start=False, stop=(sb == 0))),
                             reads=[o.v, o.w[par]], writes=[o.ops], sig=(sb == 0))
                for s, o in enumerate(S):
                    hd = hg * NS + s
                    K.op(K.dve, (lambda o=o: nc.vector.tensor_copy(out=o.ob.ap[:], in_=o.ops.ap[:])), reads=[o.ops], writes=[o.ob])
                    K.dma(K.sp, ccT.ap[hd * 128:(hd + 1) * 128, tq * 512:(tq + 1) * 512], o.ob.ap[:], reads=[o.ob], writes=[ccT], owner=o.ob)
        K.barrier()
        K.release([o.q for o in S] + [o.k for o in S] + [o.v for o in S] + [o.ob for o in S] + list(cm.values()) + [zer, zer32])
        st.close()

    def hgrn2_phase():
        st = contextlib.ExitStack()
        rm128 = K.sb(st, "rm128", [128, T], F32); rm16 = K.sb(st, "rm16", [128, T], F32)
        m01 = K.sb(st, "m01", [128, 128], F32)
        K.dma(K.sp, rm128.ap[:], cst["rm128"].ap[:, :], reads=[], writes=[rm128], owner=rm128)
        K.dma(K.sp, rm16.ap[:], cst["rm16"].ap[:, :], reads=[], writes=[rm16], owner=rm16)
        K.dma(K.sp, m01.ap[:], cst["mask01"].ap[:, :], reads=[], writes=[m01], owner=m01)
        q = K.sb(st, "gq", [128, T], F32); lf = K.sb(st, "glf", [128, T], F32); kk = K.sb(st, "gkk", [128, T], F32)
        L = K.sb(st, "gL", [128, T], F32); L16 = K.sb(st, "gL16", [128, T], F32)
        E = K.sb(st, "gE", [128, T], F32); Dt = K.sb(st, "gD", [128, T], F32)
        Q0 = K.sb(st, "gQ0", [128, T], BF16); Kh = K.sb(st, "gKh", [128, T], BF16); Qs = K.sb(st, "gQs", [128, T], BF16)
        Ks = [K.sb(st, f"gKs{i}", [128, T], BF16) for i in range(2)]
        vi = K.sb(st, "gvi", [128, 16, 128], BF16); Kht = K.sb(st, "gKht", [128, 16, 128], BF16)
        scm = K.sb(st, "gscm", [128, 16, 128], BF16)
        oT = K.sb(st, "goT", [128, T], F32); hgs = K.sb(st, "ghgs", [128, T], BF16)
        dec = K.sb(st, "gdec", [128, 16], F32)
        S32 = K.sb(st, "gS32", [128, 128], F32); Sb = K.sb(st, "gSb", [128, 128], BF16)
        sqb = K.sb(st, "gsq", [128, 512], BF16); rst = K.sb(st, "grst", [128, 512], F32); tmpn = K.sb(st, "gtmp", [128, 512], F32)
        bo = [K.sb(st, f"gbo{i}", [128, 512], BF16) for i in range(2)]
        scps = [K.psum(st, f"gsc{i}", [128, 512]) for i in range(4)]
        tps = [K.psum(st, f"gtp{i}", [128, 1024], BF16) for i in range(2)]
        ops = K.psum(st, "gops", [128, 512]); sps = K.psum(st, "gsps", [128, 512])
        v3 = lambda t: t.ap[:].rearrange("p (n c) -> p n c", c=128)
        for hd in range(8):
            rows = slice(hd * 128, (hd + 1) * 128)
            K.dma(K.sp, q.ap[:], hqT.ap[rows, :], reads=[hqT], writes=[q], owner=q)
            K.dma(K.sp, lf.ap[:], lfT.ap[rows, :], reads=[lfT], writes=[lf], owner=lf)
            K.dma(K.sp, kk.ap[:], kkT.ap[rows, :], reads=[kkT], writes=[kk], owner=kk)
            K.dma(K.sp, vi.ap[:], hitok.ap[hd].rearrange("(sb p) d -> p sb d", p=128), reads=[hitok], writes=[vi], owner=vi)
            K.dma(K.sp, hgs.ap[:], hgT.ap[rows, :], reads=[hgT], writes=[hgs], owner=hgs)
            K.op(K.dve, lambda: nc.vector.tensor_tensor_scan(out=L.ap[:], data0=rm128.ap[:], data1=lf.ap[:], initial=0.0, op0=ALU.mult, op1=ALU.add),
                 reads=[rm128, lf], writes=[L])
            K.op(K.dve, lambda: nc.vector.tensor_tensor_scan(out=L16.ap[:], data0=rm16.ap[:], data1=lf.ap[:], initial=0.0, op0=ALU.mult, op1=ALU.add),
                 reads=[rm16, lf], writes=[L16])
            K.op(K.act, lambda: nc.scalar.activation(out=E.ap[:], in_=L.ap[:], func=AF.Exp), reads=[L], writes=[E])
            K.op(K.dve, lambda: nc.vector.tensor_tensor(out=Q0.ap[:], in0=q.ap[:], in1=E.ap[:], op=ALU.mult), reads=[q, E], writes=[Q0])
            K.op(K.act, lambda: nc.scalar.activation(out=dec.ap[:], in_=v3(L)[:, :, 127], func=AF.Exp), reads=[L], writes=[dec])
            K.op(K.dve, lambda: nc.vector.tensor_tensor(out=v3(Dt), in0=v3(L)[:, :, 127:128].to_broadcast([128, 16, 128]), in1=v3(L), op=ALU.subtract),
                 reads=[L], writes=[Dt])
            K.op(K.act, lambda: nc.scalar.activation(out=E.ap[:], in_=Dt.ap[:], func=AF.Exp), reads=[Dt], writes=[E])
            K.op(K.dve, lambda: nc.vector.tensor_tensor(out=Kh.ap[:], in0=kk.ap[:], in1=E.ap[:], op=ALU.mult), reads=[kk, E], writes=[Kh])
            K.op(K.act, lambda: nc.scalar.activation(out=E.ap[:], in_=L16.ap[:], func=AF.Exp), reads=[L16], writes=[E])
            K.op(K.dve, lambda: nc.vector.tensor_tensor(out=Qs.ap[:], in0=q.ap[:], in1=E.ap[:], op=ALU.mult), reads=[q, E], writes=[Qs])
            for n in range(16):
                tp = tps[n // 8]
                K.op(K.pe, (lambda n=n, tp=tp: nc.tensor.transpose(tp.ap[:, (n % 8) * 128:(n % 8 + 1) * 128], Kh.ap[:, n * 128:(n + 1) * 128], C.ident.ap[:])),
                     reads=[Kh, C.ident], writes=[tp], sig=(n % 8 == 7))
            for i2 in range(2):
                K.op(K.act, (lambda i2=i2: nc.scalar.copy(out=Kht.ap[:, i2 * 8:(i2 + 1) * 8, :], in_=tps[i2].ap[:].rearrange("p (n c) -> p n c", c=128))),
                     reads=[tps[i2]], writes=[Kht])
            for i in range(8):
                ks = Ks[i % 2]
                if i == 0:
                    K.op(K.dve, lambda: nc.vector.tensor_scalar(out=Dt.ap[:], in0=L.ap[:], scalar1=-1.0, scalar2=None, op0=ALU.mult), reads=[L], writes=[Dt])
                else:
                    K.op(K.dve, (lambda i=i: nc.vector.tensor_tensor(out=v3(Dt), in0=v3(L)[:, :, 16 * i - 1:16 * i].to_broadcast([128, 16, 128]), in1=v3(L),
                                                                   op=ALU.subtract)), reads=[L], writes=[Dt])
                K.op(K.act, lambda: nc.scalar.activation(out=E.ap[:], in_=Dt.ap[:], func=AF.Exp), reads=[Dt], writes=[E])
                K.op(K.dve, (lambda ks=ks: nc.vector.scalar_tensor_tensor(out=ks.ap[:], in0=E.ap[:], scalar=1e30, in1=kk.ap[:], op0=ALU.min, op1=ALU.mult)),
                     reads=[E, kk], writes=[ks])
                for n in range(16):
                    sc = scps[n // 4]; cb0 = (n % 4) * 128 + 16 * i
                    K.op(K.pe, (lambda n=n, sc=sc, cb0=cb0, ks=ks, i=i: nc.tensor.matmul(sc.ap[:, cb0:cb0 + 16], ks.ap[:, n * 128:(n + 1) * 128],
                                                                                    Qs.ap[:, n * 128 + 16 * i:n * 128 + 16 * i + 16], start=True, stop=True)),
                         reads=[ks, Qs], writes=[sc], sig=(n == 15))
            for n4 in range(4):
                K.op(K.dve, (lambda n4=n4: nc.vector.tensor_tensor(out=scm.ap[:, n4 * 4:(n4 + 1) * 4, :], in0=scps[n4].ap[:].rearrange("p (n c) -> p n c", c=128),
                                                                   in1=m01.ap[:].rearrange("p (o c) -> p o c", o=1).to_broadcast([128, 4, 128]), op=ALU.mult)),
                     reads=[scps[n4], m01], writes=[scm])
            for n in range(16):
                oc = ops.ap[:, (n % 4) * 128:(n % 4 + 1) * 128]
                if n > 0:
                    K.op(K.pe, (lambda n=n, oc=oc: nc.tensor.matmul(oc, Sb.ap[:], Q0.ap[:, n * 128:(n + 1) * 128], start=True, stop=False)),
                         reads=[Sb, Q0], writes=[ops], sig=False)
                K.op(K.pe, (lambda n=n, oc=oc: nc.tensor.matmul(oc, vi.ap[:, n, :], scm.ap[:, n, :], start=(n == 0), stop=True)),
                     reads=[vi, scm], writes=[ops], sig=True)
                K.op(K.act, (lambda n=n, oc=oc: nc.scalar.copy(out=oT.ap[:, n * 128:(n + 1) * 128], in_=oc)), reads=[ops], writes=[oT])
                if n < 15:
                    sp_ = sps.ap[:, (n % 4) * 128:(n % 4 + 1) * 128]
                    K.op(K.pe, (lambda n=n, sp_=sp_: nc.tensor.matmul(sp_, Kht.ap[:, n, :], vi.ap[:, n, :], start=True, stop=True)),
                         reads=[Kht, vi], writes=[sps], sig=True)
                    if n == 0:
                        K.op(K.dve, (lambda sp_=sp_: nc.vector.tensor_copy(out=S32.ap[:], in_=sp_)), reads=[sps], writes=[S32])
                    else:
                        K.op(K.dve, (lambda n=n, sp_=sp_: nc.vector.scalar_tensor_tensor(out=S32.ap[:], in0=S32.ap[:], scalar=dec.ap[:, n:n + 1], in1=sp_,
                                                                                         op0=ALU.mult, op1=ALU.add)), reads=[S32, dec, sps], writes=[S32])
                    K.op(K.act, lambda: nc.scalar.copy(out=Sb.ap[:], in_=S32.ap[:]), reads=[S32], writes=[Sb])
            ohn, _ = PRM["hgn"]
            for tt in range(4):
                cs = slice(tt * 512, (tt + 1) * 512)
                K.op(K.act, (lambda cs=cs: nc.scalar.activation(out=sqb.ap[:], in_=oT.ap[:, cs], func=AF.Square)), reads=[oT], writes=[sqb])
                K.op(K.pe, lambda: nc.tensor.matmul(ops.ap[:], C.ones.ap[:], sqb.ap[:], start=True, stop=True), reads=[C.ones, sqb], writes=[ops], sig=True)
                K.op(K.act, lambda: nc.scalar.activation(out=rst.ap[:], in_=ops.ap[:], func=AF.Sqrt, bias=C.epsb.ap[:, 0:1], scale=1.0 / 128), reads=[ops, C.epsb], writes=[rst])
                K.op(K.dve, lambda: nc.vector.reciprocal(out=rst.ap[:], in_=rst.ap[:]), reads=[rst], writes=[rst])
                K.op(K.dve, (lambda cs=cs: nc.vector.scalar_tensor_tensor(out=tmpn.ap[:], in0=oT.ap[:, cs], scalar=C.prm.ap[:, ohn:ohn + 1], in1=rst.ap[:],
                                                                         op0=ALU.mult, op1=ALU.mult)), reads=[oT, rst, C.prm], writes=[tmpn])
                b = bo[tt % 2]
                K.op(K.dve, (lambda cs=cs, b=b: nc.vector.tensor_tensor(out=b.ap[:], in0=tmpn.ap[:], in1=hgs.ap[:, cs], op=ALU.mult)), reads=[tmpn, hgs], writes=[b])
                K.dma(K.sp, ccT.ap[1024 + hd * 128:1024 + (hd + 1) * 128, cs], b.ap[:], reads=[b], writes=[ccT], owner=b)
        K.barrier()
        K.release([rm128, rm16, m01, q, lf, kk, vi, hgs] + bo)
        st.close()

    def lru_phase():
        st = contextlib.ExitStack()
        xbs = [K.sb(st, f"rxb{i}", [128, 2, T], F32) for i in range(2)]; gbs = [K.sb(st, f"rgb{i}", [128, 2, T], F32) for i in range(2)]
        yc = K.sb(st, "ryc", [128, 2, T], F32); ycb = K.sb(st, "rycb", [128, 2, T], BF16)
        r = K.sb(st, "rr", [128, 2, T], F32); ig = K.sb(st, "rig", [128, 2, T], F32)
        a = K.sb(st, "ra", [128, 2, T], F32); mu = K.sb(st, "rmu", [128, 2, T], F32)
        gt = K.sb(st, "rgt", [128, 2, T], F32)
        mo = K.sb(st, "rmo", [128, 2, T], BF16)
        wa = [K.sb(st, f"rwa{i}", [128, 2, 256], BF16) for i in range(2)]; wx = [K.sb(st, f"rwx{i}", [128, 2, 256], BF16) for i in range(2)]
        pss = [K.psum(st, f"rps{i}", [128, 512]) for i in range(8)]
        pi = 0
        ocw = [PRM[f"convw{t}"][0] for t in range(4)]; ocb = PRM["convb"][0]; oba = PRM["ba"][0]; obx = PRM["bx"][0]
        NB = LRU // 256

        def loads(nb):
            rows = slice(nb * 256, (nb + 1) * 256)
            K.dma(K.pool, wa[nb % 2].ap[:], w_ra.ap[nb], reads=[w_ra], writes=[wa[nb % 2]], owner=wa[nb % 2])
            K.dma(K.pool, wx[nb % 2].ap[:], w_rx.ap[nb], reads=[w_rx], writes=[wx[nb % 2]], owner=wx[nb % 2])
            K.dma(K.sp, xbs[nb % 2].ap[:], xbT.ap[rows, :].rearrange("(k p) t -> p k t", p=128), reads=[xbT], writes=[xbs[nb % 2]], owner=xbs[nb % 2])
            K.dma(K.sp, gbs[nb % 2].ap[:], gbT.ap[rows, :].rearrange("(k p) t -> p k t", p=128), reads=[gbT], writes=[gbs[nb % 2]], owner=gbs[nb % 2])

        loads(0)
        for nb in range(NB):
            rows = slice(nb * 256, (nb + 1) * 256)
            W1 = wa[nb % 2]; W2 = wx[nb % 2]; xb = xbs[nb % 2]; gb = gbs[nb % 2]
            if nb + 1 < NB:
                loads(nb + 1)
            for k in range(2):
                K.op(K.pool, (lambda k=k: nc.gpsimd.tensor_tensor(out=gt.ap[:, k, :], in0=gb.ap[:, k, :], in1=gb.ap[:, k, :], op=ALU.mult)), reads=[gb], writes=[gt])
                K.op(K.pool, (lambda k=k: nc.gpsimd.tensor_scalar(out=gt.ap[:, k, :], in0=gt.ap[:, k, :], scalar1=0.044715, scalar2=1.0, op0=ALU.mult, op1=ALU.add)),
                     reads=[gt], writes=[gt])
                K.op(K.pool, (lambda k=k: nc.gpsimd.tensor_tensor(out=gt.ap[:, k, :], in0=gt.ap[:, k, :], in1=gb.ap[:, k, :], op=ALU.mult)), reads=[gt, gb], writes=[gt])
            for k in range(2):
                ch = nb * 2 + k
                K.op(K.act, (lambda k=k, ch=ch: nc.scalar.activation(out=yc.ap[:, k, :], in_=xb.ap[:, k, :], func=AF.Identity,
                                                                     scale=C.prm.ap[:, ocw[0] + ch:ocw[0] + ch + 1], bias=C.prm.ap[:, ocb + ch:ocb + ch + 1])),
                     reads=[xb, C.prm], writes=[yc])
                for tap in range(1, 4):
                    K.op(K.dve, (lambda k=k, ch=ch, tap=tap: nc.vector.scalar_tensor_tensor(out=yc.ap[:, k, tap:], in0=xb.ap[:, k, 0:T - tap],
                                                                                            scalar=C.prm.ap[:, ocw[tap] + ch:ocw[tap] + ch + 1], in1=yc.ap[:, k, tap:],
                                                                                            op0=ALU.mult, op1=ALU.add)), reads=[xb, yc, C.prm], writes=[yc])
                K.op(K.act, (lambda k=k: nc.scalar.copy(out=ycb.ap[:, k, :], in_=yc.ap[:, k, :])), reads=[yc], writes=[ycb])
            for k in range(2):
                ch = nb * 2 + k
                for tt in range(4):
                    cs = slice(tt * 512, (tt + 1) * 512)
                    for (W, dstt, ob) in ((W1, r, oba), (W2, ig, obx)):
                        ps = pss[pi % 8]; pi += 1
                        for ic in range(2):
                            K.op(K.pe, (lambda W=W, ps=ps, ic=ic, k=k, cs=cs: nc.tensor.matmul(ps.ap[:], W.ap[:, ic, k * 128:(k + 1) * 128], ycb.ap[:, ic, cs],
                                                                                           start=(ic == 0), stop=(ic == 1))), reads=[W, ycb], writes=[ps], sig=(ic == 1))
                        K.op(K.act, (lambda ps=ps, dstt=dstt, ob=ob, k=k, cs=cs, ch=ch: nc.scalar.activation(out=dstt.ap[:, k, cs], in_=ps.ap[:], func=AF.Sigmoid,
                                                                                                        bias=C.prm.ap[:, ob + ch:ob + ch + 1])),
                             reads=[ps, C.prm], writes=[dstt])
            for k in range(2):
                K.op(K.act, (lambda k=k: nc.scalar.activation(out=gt.ap[:, k, :], in_=gt.ap[:, k, :], func=AF.Sigmoid, scale=1.5957691216057308)), reads=[gt], writes=[gt])
            for k in range(2):
                K.op(K.pool, (lambda k=k: nc.gpsimd.tensor_tensor(out=gt.ap[:, k, :], in0=gt.ap[:, k, :], in1=gb.ap[:, k, :], op=ALU.mult)), reads=[gt, gb], writes=[gt])
            for k in range(2):
                scp = C.drv.ap[:, 24 + nb * 2 + k:25 + nb * 2 + k]
                K.op(K.act, (lambda k=k, scp=scp: nc.scalar.activation(out=a.ap[:, k, :], in_=r.ap[:, k, :], func=AF.Exp, scale=scp)), reads=[r, C.drv], writes=[a])
            for k in range(2):
                K.op(K.act, (lambda k=k: nc.scalar.activation(out=mu.ap[:, k, :], in_=a.ap[:, k, :], func=AF.Square)), reads=[a], writes=[mu])
            for k in range(2):
                K.op(K.dve, (lambda k=k: nc.vector.tensor_scalar(out=mu.ap[:, k, :], in0=mu.ap[:, k, :], scalar1=1.0, scalar2=-1.0, op0=ALU.min, op1=ALU.mult)), reads=[mu], writes=[mu])
            for k in range(2):
                K.op(K.act, (lambda k=k: nc.scalar.activation(out=mu.ap[:, k, :], in_=mu.ap[:, k, :], func=AF.Sqrt, bias=1.0, scale=1.0)), reads=[mu], writes=[mu])
            for k in range(2):
                K.op(K.dve, (lambda k=k: nc.vector.memset(mu.ap[:, k, 0:1], 1.0)), reads=[], writes=[mu])
                K.op(K.dve, (lambda k=k: nc.vector.tensor_tensor(out=mu.ap[:, k, :], in0=mu.ap[:, k, :], in1=ig.ap[:, k, :], op=ALU.mult)), reads=[mu, ig], writes=[mu])
                K.op(K.dve, (lambda k=k: nc.vector.tensor_tensor(out=mu.ap[:, k, :], in0=mu.ap[:, k, :], in1=yc.ap[:, k, :], op=ALU.mult)), reads=[mu, yc], writes=[mu])
            for k in range(2):
                K.op(K.dve, (lambda k=k: nc.vector.tensor_tensor_scan(out=r.ap[:, k, :], data0=a.ap[:, k, :], data1=mu.ap[:, k, :], initial=0.0, op0=ALU.mult, op1=ALU.add)),
                     reads=[a, mu], writes=[r])
            for k in range(2):
                K.op(K.pool, (lambda k=k: nc.gpsimd.tensor_tensor(out=mo.ap[:, k, :], in0=gt.ap[:, k, :], in1=r.ap[:, k, :], op=ALU.mult)), reads=[gt, r], writes=[mo])
            K.dma(K.sp, mxT.ap[rows, :].rearrange("(k p) t -> p k t", p=128), mo.ap[:], reads=[mo], writes=[mxT], owner=mo)
        K.barrier()
        K.release(xbs + gbs + [mo] + wa + wx)
        st.close()

    token_phase("p0", 0, False, inproj_even)
    if stage >= 2:
        attention_phase()
    if stage >= 3:
        hgrn2_phase()
    if stage >= 4:
        token_phase("p1", 0, True, inproj_odd)
    if stage >= 5:
        lru_phase()
    if stage >= 6:
        token_phase("p2", 1, True, None)
    K.barrier()
    gs.close()
    es.close()
    return nc


def host_inputs(inp):
    f = lambda a: np.ascontiguousarray(np.asarray(a, dtype=np.float32))
    sh = {}
    wi = f(inp["w_in_even"][0])
    names = ["w_sq", "w_sk", "w_sv", "w_hq", "w_hf", "w_hi", "w_hg"]
    for i, nm in enumerate(names):
        sh[nm] = _wl(wi[:, i * 1024:(i + 1) * 1024], 512)
    sh["w_oe"] = _wl(f(inp["w_out_even"][0]), 512)
    sh["w_io"] = _wl(f(inp["w_in_odd"][0]), 512)
    sh["w_oo"] = _wl(f(inp["w_out_odd"][0]), 256)
    for i in range(2):
        gu = f(inp["w_gate_up"][i])
        g = gu[:, :DFF].reshape(D, FC // 2, 256); u = gu[:, DFF:].reshape(D, FC // 2, 256)
        sh[f"w_gu{i}"] = _wl(np.concatenate([g, u], axis=2).reshape(D, FC * 256), 512)
        sh[f"w_dn{i}"] = _wl(f(inp["w_down"][i]), 128)
        sh[f"w_pg{i}"] = _wl(f(inp["w_ple_gate"][i]), 512)
        sh[f"w_pu{i}"] = _wl(f(inp["w_ple_up"][i]), 2048)
    for nm, key in (("w_ra", "rg_wa"), ("w_rx", "rg_wx")):
        w = f(inp[key][0])
        sh[nm] = np.ascontiguousarray(w.reshape(11, 2, 128, 256).transpose(0, 2, 1, 3))
    prm = np.zeros((128, NPRM), np.float32)
    def put(name, arr):
        off, n = PRM[name]; prm[:, off:off + n] = arr
    for i in range(2):
        for nm in ("mix_pre_g", "mix_post_g", "ffn_pre_g", "ffn_post_g", "ple_norm_g"):
            put(f"{nm}{i}", _cols(f(inp[nm][i])))
    put("lb0", _cols(f(inp["hg_lb_logits"][0]))); put("lb1", _cols(f(inp["hg_lb_logits"][1])))
    put("hgn", f(inp["hg_norm_g"][0]).reshape(128, 1))
    for tap in range(4):
        put(f"convw{tap}", _cols(f(inp["conv_w"][0, tap])))
    put("convb", _cols(f(inp["conv_b"][0]))); put("ba", _cols(f(inp["rg_ba"][0]).reshape(-1))); put("bx", _cols(f(inp["rg_bx"][0]).reshape(-1)))
    put("lam", _cols(f(inp["rg_lambda"][0])))
    sh["prm"] = prm
    for k, v in _consts().items():
        sh["c_" + k] = v
    return sh


def kernel(**inp):
    sh = host_inputs(inp)
    x = np.asarray(inp["x"], np.float32); p = np.asarray(inp["p"], np.float32)
    nc = build()
    in_maps = []
    for b in range(8):
        m = dict(sh)
        m["xT"] = np.ascontiguousarray(x[b].T)
        m["pT"] = np.ascontiguousarray(p[:, b].transpose(0, 2, 1))
        in_maps.append(m)
    res = run_bass_kernel_spmd(nc, in_maps, core_ids=list(range(8)))
    out = np.stack([np.ascontiguousarray(res.results[b]["outT"].T) for b in range(8)], axis=0)
    return out.astype(np.float32)
```

```python
import contextlib
import numpy as np
import concourse.bass as bass
import concourse.mybir as mybir
from concourse.bass_utils import run_bass_kernel_spmd

F32 = mybir.dt.float32
F32R = mybir.dt.float32r
BF16 = mybir.dt.bfloat16
AF = mybir.ActivationFunctionType
ALU = mybir.AluOpType

D = 2048
T = 2048
TT = 512
NTT = T // TT
KC = D // 128
DFF = 5632
FC = DFF // 128
LRU = 2816
LC = LRU // 128
PLE = 256
EPS = 1e-6
WSLOT = 8192


def _wl(w, cb):
    K, N = w.shape
    return np.ascontiguousarray(w.reshape(K // 128, 128, N // cb, cb).transpose(2, 1, 0, 3))


def _cols(v):
    return np.ascontiguousarray(v.reshape(-1, 128).T)


PRM = {}


def _prm_layout():
    off = 0
    def add(name, n):
        nonlocal off
        PRM[name] = (off, n)
        off += n
    for i in range(2):
        for nm in ("mix_pre_g", "mix_post_g", "ffn_pre_g", "ffn_post_g", "ple_norm_g"):
            add(f"{nm}{i}", 16)
    add("lb0", 8); add("lb1", 8); add("hgn", 1)
    for tap in range(4):
        add(f"convw{tap}", LC)
    add("convb", LC); add("ba", LC); add("bx", LC); add("lam", LC)
    return off


NPRM = _prm_layout()


def _consts():
    c = {}
    s = np.arange(128)[:, None]; t = np.arange(128)[None, :]
    c["ident"] = np.eye(128, dtype=np.float32)
    c["ones"] = np.ones((128, 128), np.float32)
    c["zeros"] = np.zeros((128, 512), np.float32)
    c["maskneg"] = np.where(s >= t, -30000.0, 0.0).astype(np.float32)
    c["mask01"] = (s <= t).astype(np.float32)
    c["negU"] = np.where(s >= t, -1.0, 0.0).astype(np.float32)
    c["negOnes"] = -np.ones((128, 128), np.float32)
    tt = np.arange(T)
    c["rm128"] = np.broadcast_to((tt % 128 != 0).astype(np.float32), (128, T)).copy()
    c["rm16"] = np.broadcast_to((tt % 16 != 0).astype(np.float32), (128, T)).copy()
    return c


class Tl:
    def __init__(self, name, ap):
        self.name = name; self.ap = ap
        self.w = {}; self.r = {}
        self.dsem = None; self.dcnt = 0
        self.dram = False

    def __getitem__(self, k):
        return self.ap[k]


class Eng:
    def __init__(self, name, eng, sem):
        self.name = name; self.eng = eng; self.sem = sem; self.cnt = 0
        self.waited = {}; self.pend = []


class Kern:
    def __init__(self, nc, es):
        self.nc = nc; self.es = es
        self.pe = Eng("pe", nc.tensor, es.enter_context(nc.semaphore("s_pe")))
        self.act = Eng("act", nc.scalar, es.enter_context(nc.semaphore("s_act")))
        self.dve = Eng("dve", nc.vector, es.enter_context(nc.semaphore("s_dve")))
        self.pool = Eng("pool", nc.gpsimd, es.enter_context(nc.semaphore("s_pool")))
        self.sp = Eng("sp", nc.sync, es.enter_context(nc.semaphore("s_sp")))
        self.engs = [self.pe, self.act, self.dve, self.pool, self.sp]
        self.dsems = []
        self.nsem = 5

    def sb(self, stack, name, shape, dt):
        return Tl(name, stack.enter_context(self.nc.sbuf_tensor(name, list(shape), dt)))

    def psum(self, stack, name, shape, dt=F32):
        return Tl(name, stack.enter_context(self.nc.psum_tensor(name, list(shape), dt)))

    def dram(self, name, shape, dt, kind="Internal"):
        t = Tl(name, self.nc.dram_tensor(name, list(shape), dt, kind=kind).ap())
        t.dram = True
        return t

    def _wait(self, E, ev):
        for sid, (sem, val) in ev.items():
            if E is self.pe and sid == id(E.sem):
                continue
            if E.waited.get(sid, 0) < val:
                E.eng.wait_ge(sem, val)
                E.waited[sid] = val

    def _deps(self, E, reads, writes):
        ev = {}
        def mrg(d):
            for sid, (sem, val) in d.items():
                if sid not in ev or ev[sid][1] < val:
                    ev[sid] = (sem, val)
        for t in reads:
            mrg(t.w)
        for t in writes:
            mrg(t.w); mrg(t.r)
        self._wait(E, ev)

    def _record(self, ev, reads, writes):
        sid, sem, val = ev
        for t in writes:
            t.w = {sid: (sem, val)}; t.r = {}
        for t in reads:
            t.r[sid] = (sem, val)

    def op(self, E, fn, reads=(), writes=(), sig=True):
        self._deps(E, reads, writes)
        inst = fn()
        if sig:
            E.cnt += 1
            inst.then_inc(E.sem, 1)
            ev = (id(E.sem), E.sem, E.cnt)
            self._record(ev, reads, writes)
            for (r, w) in E.pend:
                self._record(ev, r, w)
            E.pend = []
        else:
            ev = (id(E.sem), E.sem, E.cnt + 1)
            self._record(ev, reads, writes)
        return inst

    def dma(self, Q, out_ap, in_ap, reads, writes, owner):
        reads = [t for t in reads if not t.dram]
        writes = [t for t in writes if not t.dram]
        self._deps(Q, reads, writes)
        if owner.dsem is None:
            owner.dsem = self.es.enter_context(self.nc.semaphore("d_" + owner.name))
            self.dsems.append(owner); self.nsem += 1
        owner.dcnt += 16
        Q.eng.dma_start(out=out_ap, in_=in_ap).then_inc(owner.dsem, 16)
        ev = (id(owner.dsem), owner.dsem, owner.dcnt)
        sid, sem, val = ev
        for t in writes:
            t.w = {sid: (sem, val)}; t.r = {}
        for t in reads:
            t.r[sid] = (sem, val)

    def barrier(self):
        ev = {}
        for E in self.engs:
            if E.cnt:
                ev[id(E.sem)] = (E.sem, E.cnt)
        for t in self.dsems:
            ev[id(t.dsem)] = (t.dsem, t.dcnt)
        for E in self.engs:
            self._wait(E, ev)

    def release(self, tiles):
        self.dsems = [t for t in self.dsems if t not in tiles]


class Ctx:
    pass


def wview(slot, kc, cb):
    return slot.ap[:, 0:kc * cb].rearrange("p (k c) -> p k c", c=cb)


def load_w(K, C, wl_dram, nb, kc, cb):
    slot = C.wslots[C.wi % len(C.wslots)]; C.wi += 1
    K.dma(K.pool, wview(slot, kc, cb), wl_dram.ap[nb], reads=[wl_dram], writes=[slot], owner=slot)
    return slot


def mm_group(K, C, ps_t, ps_ap, pairs, reads):
    n = len(pairs)
    for i, (l, r) in enumerate(pairs):
        K.op(K.pe, (lambda l=l, r=r, i=i: K.nc.tensor.matmul(ps_ap, l, r, start=(i == 0), stop=(i == n - 1))),
             reads=reads, writes=[ps_t], sig=(i == n - 1))


def next_ps(C):
    t = C.ps[C.pi % len(C.ps)]; C.pi += 1
    return t


def stats_begin(C, n, lag):
    C.statk = 0; C.statn = n; C.pend = []; C.lag = lag


def ones_mm(K, C):
    sq = C.pend.pop(0)
    k = C.statk; C.statk += 1
    K.op(K.pe, (lambda: K.nc.tensor.matmul(C.stat.ap[:], C.ones.ap[:], sq.ap[:], start=(k == 0), stop=(k == C.statn - 1))),
         reads=[sq, C.ones], writes=[C.stat], sig=True)


def sq_push(K, C, t, ap, eng):
    sq = C.sq[C.sqi % len(C.sq)]; C.sqi += 1
    if eng == "act":
        K.op(K.act, (lambda: K.nc.scalar.activation(out=sq.ap[:], in_=ap, func=AF.Square)), reads=[t], writes=[sq])
    else:
        K.op(K.dve, (lambda: K.nc.vector.tensor_tensor(out=sq.ap[:], in0=ap, in1=ap, op=ALU.mult)), reads=[t], writes=[sq])
    C.pend.append(sq)
    if len(C.pend) > C.lag:
        ones_mm(K, C)


def stats_finish(K, C, scale, out_rstd):
    while C.pend:
        ones_mm(K, C)
    assert C.statk == C.statn
    K.op(K.act, lambda: K.nc.scalar.activation(out=out_rstd.ap[:], in_=C.stat.ap[:], func=AF.Sqrt, bias=C.epsb.ap[:, 0:1], scale=scale),
         reads=[C.stat, C.epsb], writes=[out_rstd])
    K.op(K.dve, lambda: K.nc.vector.reciprocal(out=out_rstd.ap[:], in_=out_rstd.ap[:]), reads=[out_rstd], writes=[out_rstd])


def prm(C, name, j=None):
    off, n = PRM[name]
    if j is None:
        return C.prm.ap[:, off:off + n]
    return C.prm.ap[:, off + j:off + j + 1]


def norm_bf16(K, C, src, gname, dst, have_stats=False):
    if not have_stats:
        stats_begin(C, KC, 1)
        for kc in range(KC):
            sq_push(K, C, src, src.ap[:, kc, :], "act")
    stats_finish(K, C, 1.0 / D, C.rstd)
    for kc in range(KC):
        K.op(K.dve, (lambda kc=kc: K.nc.vector.scalar_tensor_tensor(out=dst.ap[:, kc, :], in0=src.ap[:, kc, :], scalar=prm(C, gname, kc),
                                                                   in1=C.rstd.ap[:], op0=ALU.mult, op1=ALU.mult)),
             reads=[src, C.rstd, C.prm], writes=[dst])


def post_norm_res(K, C, y, gname, h, follow=None, nT=None):
    stats_finish(K, C, 1.0 / D, C.rstd2)
    if follow == "norm":
        stats_begin(C, KC, 1)
    for kc in range(KC):
        K.op(K.dve, (lambda kc=kc: K.nc.vector.scalar_tensor_tensor(out=y.ap[:, kc, :], in0=y.ap[:, kc, :], scalar=prm(C, gname, kc),
                                                                   in1=C.rstd2.ap[:], op0=ALU.mult, op1=ALU.mult)),
             reads=[y, C.rstd2, C.prm], writes=[y])
        K.op(K.dve, (lambda kc=kc: K.nc.vector.tensor_tensor(out=h.ap[:, kc, :], in0=h.ap[:, kc, :], in1=y.ap[:, kc, :], op=ALU.add)),
             reads=[y, h], writes=[h])
        if follow == "norm":
            sq_push(K, C, h, h.ap[:, kc, :], "act")
        elif follow == "copy":
            K.op(K.act, (lambda kc=kc: K.nc.scalar.copy(out=nT.ap[:, kc, :], in_=h.ap[:, kc, :])), reads=[h], writes=[nT])


def linear_fm(K, C, inT, kcn, wl_dram, ncols, cb, epi):
    nblk = ncols // cb
    for nb in range(nblk):
        slot = load_w(K, C, wl_dram, nb, kcn, cb)
        wv = wview(slot, kcn, cb)
        for ci in range(cb // 128):
            ps = next_ps(C)
            mm_group(K, C, ps, ps.ap[:], [(wv[:, kc, ci * 128:(ci + 1) * 128], inT.ap[:, kc, :]) for kc in range(kcn)], reads=[slot, inT])
            epi(nb * (cb // 128) + ci, ps)


def stage_out(C, dt):
    lst = C.stg32 if dt == F32 else C.stg16
    t = lst[C.stgi[dt] % len(lst)]; C.stgi[dt] += 1
    return t


def build(stage=99, dbg=()):
    nc = bass.Bass("TRN2", target_bir_lowering=False)
    es = contextlib.ExitStack()
    K = Kern(nc, es)
    C = Ctx()
    dr = {}

    def ein(name, shape, dt=F32):
        dr[name] = K.dram(name, shape, dt, kind="ExternalInput"); return dr[name]

    def scr(name, shape, dt, out=False):
        dr[name] = K.dram(name, shape, dt, kind=("ExternalOutput" if (out or name in dbg) else "Internal")); return dr[name]

    xT = ein("xT", [D, T]); pT = ein("pT", [2, PLE, T]); prm_d = ein("prm", [128, NPRM])
    cst = {k: ein("c_" + k, list(v.shape)) for k, v in _consts().items()}
    w_sq = ein("w_sq", [2, 128, KC, 512]); w_sk = ein("w_sk", [2, 128, KC, 512]); w_sv = ein("w_sv", [2, 128, KC, 512])
    w_hq = ein("w_hq", [2, 128, KC, 512]); w_hf = ein("w_hf", [2, 128, KC, 512]); w_hi = ein("w_hi", [2, 128, KC, 512])
    w_hg = ein("w_hg", [2, 128, KC, 512])
    w_oe = ein("w_oe", [4, 128, KC, 512])
    w_io = ein("w_io", [2 * LRU // 512, 128, KC, 512])
    w_oo = ein("w_oo", [D // 256, 128, LC, 256])
    w_gu = [ein(f"w_gu{i}", [FC // 2, 128, KC, 512]) for i in range(2)]
    w_dn = [ein(f"w_dn{i}", [D // 128, 128, FC, 128]) for i in range(2)]
    w_pg = [ein(f"w_pg{i}", [4, 128, KC, 512]) for i in range(2)]
    w_pu = [ein(f"w_pu{i}", [1, 128, 2, 2048]) for i in range(2)]
    w_ra = ein("w_ra", [LRU // 256, 128, 2, 256]); w_rx = ein("w_rx", [LRU // 256, 128, 2, 256])

    outT = scr("outT", [D, T], F32, out=True)
    hT = scr("hT", [D, T], F32)
    qT = scr("qT", [1024, T], BF16); kT = scr("kT", [1024, T], BF16); vtok = scr("vtok", [8, T, 128], BF16)
    hqT = scr("hqT", [1024, T], F32); lfT = scr("lfT", [1024, T], F32); kkT = scr("kkT", [1024, T], F32)
    hgT = scr("hgT", [1024, T], BF16); hitok = scr("hitok", [8, T, 128], BF16)
    ccT = scr("ccT", [D, T], BF16)
    gbT = scr("gbT", [LRU, T], F32); xbT = scr("xbT", [LRU, T], F32)
    mxT = scr("mxT", [LRU, T], BF16)

    dbgt = {nm: scr(nm, [D, T], F32) for nm in ("dbg_m", "dbg_hmix", "dbg_hffn") if nm in dbg}
    def dbg_store(nm, tile, t0, tag):
        if nm in dbgt and tag == "p1":
            K.dma(K.sp, dbgt[nm].ap.rearrange("(k p) t -> p k t", p=128)[:, :, t0:t0 + TT], tile.ap[:], reads=[tile], writes=[dbgt[nm]], owner=tile)

    def kcv(d, t0):
        return d.ap.rearrange("(k p) t -> p k t", p=128)[:, :, t0:t0 + TT]

    gs = contextlib.ExitStack()
    C.prm = K.sb(gs, "prm_sb", [128, NPRM], F32)
    C.ones = K.sb(gs, "ones_sb", [128, 128], BF16)
    C.ident = K.sb(gs, "ident_sb", [128, 128], BF16)
    C.epsb = K.sb(gs, "epsb", [128, 1], F32)
    C.drv = K.sb(gs, "drv", [128, 64], F32)
    K.dma(K.sp, C.prm.ap[:], prm_d.ap[:, :], reads=[prm_d], writes=[C.prm], owner=C.prm)
    K.dma(K.pool, C.ones.ap[:], cst["ones"].ap[:, :], reads=[], writes=[C.ones], owner=C.ones)
    K.dma(K.pool, C.ident.ap[:], cst["ident"].ap[:, :], reads=[], writes=[C.ident], owner=C.ident)
    K.op(K.dve, lambda: nc.vector.memset(C.epsb.ap[:], EPS), writes=[C.epsb])
    o0, _ = PRM["lb0"]; o1, _ = PRM["lb1"]
    K.op(K.dve, lambda: nc.vector.tensor_tensor(out=C.drv.ap[:, 0:8], in0=C.prm.ap[:, o0:o0 + 8], in1=C.prm.ap[:, o1:o1 + 8], op=ALU.subtract),
         reads=[C.prm], writes=[C.drv])
    K.op(K.act, lambda: nc.scalar.activation(out=C.drv.ap[:, 0:8], in_=C.drv.ap[:, 0:8], func=AF.Sigmoid), reads=[C.drv], writes=[C.drv])
    K.op(K.dve, lambda: nc.vector.tensor_scalar(out=C.drv.ap[:, 8:16], in0=C.drv.ap[:, 0:8], scalar1=-1.0, scalar2=1.0, op0=ALU.mult, op1=ALU.add),
         reads=[C.drv], writes=[C.drv])
    K.op(K.dve, lambda: nc.vector.tensor_scalar(out=C.drv.ap[:, 16:24], in0=C.drv.ap[:, 8:16], scalar1=-1.0, scalar2=None, op0=ALU.mult),
         reads=[C.drv], writes=[C.drv])
    ol, _ = PRM["lam"]
    K.op(K.act, lambda: nc.scalar.activation(out=C.drv.ap[:, 24:46], in_=C.prm.ap[:, ol:ol + LC], func=AF.Exp, scale=-1.0), reads=[C.prm], writes=[C.drv])
    K.op(K.act, lambda: nc.scalar.activation(out=C.drv.ap[:, 24:46], in_=C.drv.ap[:, 24:46], func=AF.Ln, bias=1.0), reads=[C.drv], writes=[C.drv])
    K.op(K.dve, lambda: nc.vector.tensor_scalar(out=C.drv.ap[:, 24:46], in0=C.drv.ap[:, 24:46], scalar1=-8.0, scalar2=None, op0=ALU.mult),
         reads=[C.drv], writes=[C.drv])

    def token_phase(tag, layer, do_mix_ffn, inproj):
        ps_stack = contextlib.ExitStack()
        C.wslots = [K.sb(ps_stack, f"ws{i}_{tag}", [128, WSLOT], BF16) for i in range(3)]; C.wi = 0
        C.ps = [K.psum(ps_stack, f"ps{i}_{tag}", [128, TT]) for i in range(7)]; C.pi = 0
        C.stat = K.psum(ps_stack, f"stat_{tag}", [128, TT])
        C.sq = [K.sb(ps_stack, f"sq{i}_{tag}", [128, TT], BF16) for i in range(4)]; C.sqi = 0
        C.rstd = K.sb(ps_stack, f"rstd_{tag}", [128, TT], F32)
        C.rstd2 = K.sb(ps_stack, f"rstd2_{tag}", [128, TT], F32)
        nst = 3 if do_mix_ffn else 9
        C.stg32 = [K.sb(ps_stack, f"st32_{i}_{tag}", [128, TT], F32) for i in range(nst)]
        C.stg16 = [K.sb(ps_stack, f"st16_{i}_{tag}", [128, TT], BF16) for i in range(nst)]
        C.stgi = {F32: 0, BF16: 0}
        h = K.sb(ps_stack, f"h_{tag}", [128, KC, TT], F32)
        nT = K.sb(ps_stack, f"nT_{tag}", [128, KC, TT], BF16)
        if do_mix_ffn:
            act = K.sb(ps_stack, f"act_{tag}", [128, FC, TT], BF16)
            y = K.sb(ps_stack, f"y_{tag}", [128, KC, TT], F32)
            pt = K.sb(ps_stack, f"pt_{tag}", [128, 2, TT], BF16)
            tmp = [K.sb(ps_stack, f"tmp{i}_{tag}", [128, TT], F32) for i in range(2)]
            wpu = K.sb(ps_stack, f"wpu_{tag}", [128, 2 * D], BF16)
            K.dma(K.pool, wview(wpu, 2, D), w_pu[layer].ap[0], reads=[w_pu[layer]], writes=[wpu], owner=wpu)
        h_src = xT if layer == 0 else hT
        for tt in range(NTT):
            t0 = tt * TT
            K.dma(K.sp, h.ap[:], kcv(h_src, t0), reads=[h_src], writes=[h], owner=h)
            if do_mix_ffn:
                if layer == 0:
                    cc, ckc, wo, wcb = ccT, KC, w_oe, 512
                else:
                    cc, ckc, wo, wcb = mxT, LC, w_oo, 256
                if tt == 0:
                    K.dma(K.sp, act.ap[:, 0:ckc, :], kcv(cc, t0), reads=[cc], writes=[act], owner=act)
                def epi_y(c, ps):
                    K.op(K.act, lambda: nc.scalar.copy(out=y.ap[:, c, :], in_=ps.ap[:]), reads=[ps], writes=[y])
                    sq_push(K, C, y, y.ap[:, c, :], "dve")
                stats_begin(C, KC, 2)
                linear_fm(K, C, act, ckc, wo, D, wcb, epi_y)
                dbg_store("dbg_m", y, t0, tag)
                post_norm_res(K, C, y, f"mix_post_g{layer}", h, follow="norm")
                dbg_store("dbg_hmix", h, t0, tag)
                norm_bf16(K, C, h, f"ffn_pre_g{layer}", nT, have_stats=True)
                for nb in range(FC // 2):
                    slot = load_w(K, C, w_gu[layer], nb, KC, 512)
                    wv = wview(slot, KC, 512)
                    for ci in range(2):
                        psg = next_ps(C); psu = next_ps(C)
                        mm_group(K, C, psg, psg.ap[:], [(wv[:, kc, ci * 128:(ci + 1) * 128], nT.ap[:, kc, :]) for kc in range(KC)], reads=[slot, nT])
                        mm_group(K, C, psu, psu.ap[:], [(wv[:, kc, 256 + ci * 128:256 + (ci + 1) * 128], nT.ap[:, kc, :]) for kc in range(KC)], reads=[slot, nT])
                        tm = tmp[(nb * 2 + ci) % 2]; fc = nb * 2 + ci
                        K.op(K.act, lambda: nc.scalar.activation(out=tm.ap[:], in_=psg.ap[:], func=AF.Silu), reads=[psg], writes=[tm])
                        K.op(K.dve, lambda: nc.vector.tensor_tensor(out=act.ap[:, fc, :], in0=tm.ap[:], in1=psu.ap[:], op=ALU.mult), reads=[tm, psu], writes=[act])
                stats_begin(C, KC, 2)
                linear_fm(K, C, act, FC, w_dn[layer], D, 128, epi_y)
                if tt + 1 < NTT:
                    K.dma(K.sp, act.ap[:, 0:ckc, :], kcv(cc, t0 + TT), reads=[cc], writes=[act], owner=act)
                post_norm_res(K, C, y, f"ffn_post_g{layer}", h, follow="copy", nT=nT)
                dbg_store("dbg_hffn", h, t0, tag)
                stats_begin(C, KC, 2)
                K.dma(K.pool, pt.ap[:], pT.ap[layer].rearrange("(k p) t -> p k t", p=128)[:, :, t0:t0 + TT], reads=[pT], writes=[pt], owner=pt)
                uslot = wpu
                uv = wview(wpu, 2, D)
                for nb in range(4):
                    slot = load_w(K, C, w_pg[layer], nb, KC, 512)
                    wv = wview(slot, KC, 512)
                    for ci in range(4):
                        c = nb * 4 + ci
                        psg = next_ps(C); pse = next_ps(C)
                        mm_group(K, C, psg, psg.ap[:], [(wv[:, kc, ci * 128:(ci + 1) * 128], nT.ap[:, kc, :]) for kc in range(KC)], reads=[slot, nT])
                        mm_group(K, C, pse, pse.ap[:], [(uv[:, k2, c * 128:(c + 1) * 128], pt.ap[:, k2, :]) for k2 in range(2)], reads=[uslot, pt])
                        tm = tmp[c % 2]
                        K.op(K.act, lambda: nc.scalar.activation(out=tm.ap[:], in_=psg.ap[:], func=AF.Sigmoid), reads=[psg], writes=[tm])
                        K.op(K.dve, lambda: nc.vector.tensor_tensor(out=y.ap[:, c, :], in0=tm.ap[:], in1=pse.ap[:], op=ALU.mult), reads=[tm, pse], writes=[y])
                        sq_push(K, C, y, y.ap[:, c, :], "dve")
                post_norm_res(K, C, y, f"ple_norm_g{layer}", h, follow=("norm" if inproj is not None else None))
                h_dst = hT if inproj is not None else outT
                K.dma(K.sp, kcv(h_dst, t0), h.ap[:], reads=[h], writes=[h_dst], owner=h)
            if inproj is not None:
                inproj(K, C, h, nT, t0, do_mix_ffn)
        K.barrier()
        K.release(C.wslots + C.stg32 + C.stg16 + [h, nT] + ([act, pt, wpu] if do_mix_ffn else []))
        ps_stack.close()

    def inproj_even(K, C, h, nT, t0, have_stats):
        norm_bf16(K, C, h, "mix_pre_g0", nT, have_stats=have_stats)
        def store(t, dst, c):
            K.dma(K.sp, dst.ap[c * 128:(c + 1) * 128, t0:t0 + TT], t.ap[:], reads=[t], writes=[dst], owner=t)
        def epi_q(c, ps):
            t = stage_out(C, BF16)
            K.op(K.act, lambda: nc.scalar.activation(out=t.ap[:], in_=ps.ap[:], func=AF.Copy, scale=128.0 ** -0.5), reads=[ps], writes=[t]); store(t, qT, c)
        def epi_k(c, ps):
            t = stage_out(C, BF16)
            K.op(K.act, lambda: nc.scalar.copy(out=t.ap[:], in_=ps.ap[:]), reads=[ps], writes=[t]); store(t, kT, c)
        def epi_hq(c, ps):
            t = stage_out(C, F32)
            K.op(K.act, lambda: nc.scalar.activation(out=t.ap[:], in_=ps.ap[:], func=AF.Silu), reads=[ps], writes=[t]); store(t, hqT, c)
        def epi_hg(c, ps):
            t = stage_out(C, BF16)
            K.op(K.act, lambda: nc.scalar.activation(out=t.ap[:], in_=ps.ap[:], func=AF.Silu), reads=[ps], writes=[t]); store(t, hgT, c)
        def epi_hf(c, ps):
            s = stage_out(C, F32); t1 = stage_out(C, F32); t2 = stage_out(C, F32)
            K.op(K.act, lambda: nc.scalar.activation(out=s.ap[:], in_=ps.ap[:], func=AF.Sigmoid), reads=[ps], writes=[s])
            K.op(K.act, lambda: nc.scalar.activation(out=t1.ap[:], in_=s.ap[:], func=AF.Ln, bias=C.drv.ap[:, c:c + 1], scale=C.drv.ap[:, 8 + c:9 + c]),
                 reads=[s, C.drv], writes=[t1]); store(t1, lfT, c)
            K.op(K.dve, lambda: nc.vector.tensor_scalar(out=t2.ap[:], in0=s.ap[:], scalar1=C.drv.ap[:, 16 + c:17 + c], scalar2=C.drv.ap[:, 8 + c:9 + c],
                                                        op0=ALU.mult, op1=ALU.add), reads=[s, C.drv], writes=[t2]); store(t2, kkT, c)
        linear_fm(K, C, nT, KC, w_sq, 1024, 512, epi_q)
        linear_fm(K, C, nT, KC, w_sk, 1024, 512, epi_k)
        linear_fm(K, C, nT, KC, w_hq, 1024, 512, epi_hq)
        linear_fm(K, C, nT, KC, w_hf, 1024, 512, epi_hf)
        linear_fm(K, C, nT, KC, w_hg, 1024, 512, epi_hg)
        for (wl, dst) in ((w_sv, vtok), (w_hi, hitok)):
            for nb in range(2):
                slot = load_w(K, C, wl, nb, KC, 512); wv = wview(slot, KC, 512)
                for tb in range(TT // 128):
                    ps = next_ps(C)
                    mm_group(K, C, ps, ps.ap[:], [(nT.ap[:, kc, tb * 128:(tb + 1) * 128], wv[:, kc, :]) for kc in range(KC)], reads=[slot, nT])
                    t = stage_out(C, BF16)
                    K.op(K.act, lambda: nc.scalar.copy(out=t.ap[:], in_=ps.ap[:]), reads=[ps], writes=[t])
                    K.dma(K.sp, dst.ap[nb * 4:(nb + 1) * 4, t0 + tb * 128:t0 + (tb + 1) * 128, :].rearrange("h t d -> t h d"),
                          t.ap[:].rearrange("t (h d) -> t h d", d=128), reads=[t], writes=[dst], owner=t)

    def inproj_odd(K, C, h, nT, t0, have_stats):
        norm_bf16(K, C, h, "mix_pre_g1", nT, have_stats=have_stats)
        def epi(c, ps):
            t = stage_out(C, F32)
            K.op(K.act, lambda: nc.scalar.copy(out=t.ap[:], in_=ps.ap[:]), reads=[ps], writes=[t])
            dst, cc = (gbT, c) if c < LC else (xbT, c - LC)
            K.dma(K.sp, dst.ap[cc * 128:(cc + 1) * 128, t0:t0 + TT], t.ap[:], reads=[t], writes=[dst], owner=t)
        linear_fm(K, C, nT, KC, w_io, 2 * LRU, 512, epi)

    def attention_phase():
        st = contextlib.ExitStack()
        NS = 4
        cm = {}
        for nm, dt in (("maskneg", BF16), ("negU", F32R), ("negOnes", F32R)):
            cm[nm] = K.sb(st, "c_" + nm + "_sb", [128, 128], dt)
            K.dma(K.pool, cm[nm].ap[:], cst[nm].ap[:, :], reads=[], writes=[cm[nm]], owner=cm[nm])
        zer = K.sb(st, "zer_sb", [128, 512], BF16)
        K.dma(K.pool, zer.ap[:], cst["zeros"].ap[:, :], reads=[], writes=[zer], owner=zer)
        zer32 = K.sb(st, "zer32_sb", [128, 512], F32)
        K.dma(K.sp, zer32.ap[:], cst["zeros"].ap[:, :], reads=[], writes=[zer32], owner=zer32)
        S = []
        for s in range(NS):
            o = Ctx()
            o.q = K.sb(st, f"aq{s}", [128, T], BF16); o.k = K.sb(st, f"ak{s}", [128, T], BF16); o.v = K.sb(st, f"av{s}", [128, 16, 128], BF16)
            o.e = K.sb(st, f"ae{s}", [128, 512], F32)
            o.sp = [K.sb(st, f"asp{s}_{i}", [128, 512], F32R) for i in range(2)]
            o.A = K.sb(st, f"aA{s}", [128, 512], F32R)
            o.w = [K.sb(st, f"aw{s}_{i}", [128, 512], BF16) for i in range(2)]
            o.ob = K.sb(st, f"aob{s}", [128, 512], BF16)
            o.zps = K.psum(st, f"azps{s}", [128, 512]); o.ops = K.psum(st, f"aops{s}", [128, 512])
            S.append(o)
        for hg in range(8 // NS):
            for s, o in enumerate(S):
                hd = hg * NS + s
                K.dma(K.sp, o.q.ap[:], qT.ap[hd * 128:(hd + 1) * 128, :], reads=[qT], writes=[o.q], owner=o.q)
                K.dma(K.sp, o.k.ap[:], kT.ap[hd * 128:(hd + 1) * 128, :], reads=[kT], writes=[o.k], owner=o.k)
                K.dma(K.sp, o.v.ap[:], vtok.ap[hd].rearrange("(sb p) d -> p sb d", p=128), reads=[vtok], writes=[o.v], owner=o.v)
            step = 0
            for tq in range(4):
                nblk = 4 * tq + 4
                for o in S:
                    K.op(K.pe, (lambda o=o: nc.tensor.matmul(o.ops.ap[:], zer.ap[:, 0:128], zer.ap[:], start=True, stop=False)), reads=[zer], writes=[o.ops], sig=False)
                    K.op(K.dve, (lambda o=o: nc.vector.tensor_copy(out=o.A.ap[:], in_=zer32.ap[:])), reads=[zer32], writes=[o.A])
                for bi in range(nblk):
                    sb = nblk - 1 - bi
                    c0 = max(0, 128 * sb - 512 * tq); diag = 128 * sb >= 512 * tq
                    q0 = 512 * tq + c0
                    par = step % 2; step += 1
                    for o in S:
                        K.op(K.pe, (lambda o=o: nc.tensor.matmul(o.zps.ap[:, c0:512], o.k.ap[:, sb * 128:(sb + 1) * 128], o.q.ap[:, q0:512 * tq + 512],
                                                                 start=True, stop=False)), reads=[o.k, o.q], writes=[o.zps], sig=not diag)
                        if diag:
                            K.op(K.pe, (lambda o=o: nc.tensor.matmul(o.zps.ap[:, c0:c0 + 128], C.ident.ap[:], cm["maskneg"].ap[:], start=False, stop=False)),
                                 reads=[C.ident, cm["maskneg"]], writes=[o.zps], sig=True)
                    for o in S:
                        K.op(K.act, (lambda o=o: nc.scalar.activation(out=o.e.ap[:, c0:512], in_=o.zps.ap[:, c0:512], func=AF.Exp)), reads=[o.zps], writes=[o.e])
                        K.op(K.act, (lambda o=o: nc.scalar.activation(out=o.sp[par].ap[:, c0:512], in_=o.e.ap[:, c0:512], func=AF.Ln, bias=1.0)),
                             reads=[o.e], writes=[o.sp[par]])
                    for o in S:
                        last = (bi == 0)
                        K.op(K.pe, (lambda o=o: nc.tensor.matmul(o.zps.ap[:, c0:512], cm["negU"].ap[:], o.sp[par].ap[:, c0:512], start=False, stop=last)),
                             reads=[cm["negU"], o.sp[par]], writes=[o.zps], sig=last)
                        if not last:
                            K.op(K.pe, (lambda o=o: nc.tensor.matmul(o.zps.ap[:, c0:512], cm["negOnes"].ap[:], o.A.ap[:, c0:512], start=False, stop=True)),
                                 reads=[cm["negOnes"], o.A], writes=[o.zps], sig=True)
                    for o in S:
                        K.op(K.act, (lambda o=o: nc.scalar.activation(out=o.w[par].ap[:, c0:512], in_=o.zps.ap[:, c0:512], func=AF.Exp)),
                             reads=[o.zps], writes=[o.w[par]])
                        if sb > 0:
                            K.op(K.dve, (lambda o=o: nc.vector.tensor_tensor(out=o.A.ap[:, c0:512], in0=o.A.ap[:, c0:512].bitcast(F32), in1=o.sp[par].ap[:, c0:512].bitcast(F32), op=ALU.add)),
                                 reads=[o.A, o.sp[par]], writes=[o.A])
                    for o in S:
                        K.op(K.pe, (lambda o=o: nc.tensor.matmul(o.ops.ap[:, c0:512], o.v.ap[:, sb, :], o.w[par].ap[:, c0:512], start=False, stop=(sb == 0))),
                             reads=[o.v, o.w[par]], writes=[o.ops], sig=(sb == 0))
                for s, o in enumerate(S):
                    hd = hg * NS + s
                    K.op(K.dve, (lambda o=o: nc.vector.tensor_copy(out=o.ob.ap[:], in_=o.ops.ap[:])), reads=[o.ops], writes=[o.ob])
                    K.dma(K.sp, ccT.ap[hd * 128:(hd + 1) * 128, tq * 512:(tq + 1) * 512], o.ob.ap[:], reads=[o.ob], writes=[ccT], owner=o.ob)
        K.barrier()
        K.release([o.q for o in S] + [o.k for o in S] + [o.v for o in S] + [o.ob for o in S] + list(cm.values()) + [zer, zer32])
        st.close()

    def hgrn2_phase():
        st = contextlib.ExitStack()
        rm128 = K.sb(st, "rm128", [128, T], F32); rm16 = K.sb(st, "rm16", [128, T], F32)
        m01 = K.sb(st, "m01", [128, 128], F32)
        K.dma(K.sp, rm128.ap[:], cst["rm128"].ap[:, :], reads=[], writes=[rm128], owner=rm128)
        K.dma(K.sp, rm16.ap[:], cst["rm16"].ap[:, :], reads=[], writes=[rm16], owner=rm16)
        K.dma(K.sp, m01.ap[:], cst["mask01"].ap[:, :], reads=[], writes=[m01], owner=m01)
        qs_ = [K.sb(st, f"gq{i}", [128, T], F32) for i in range(2)]; lfs_ = [K.sb(st, f"glf{i}", [128, T], F32) for i in range(2)]
        kks_ = [K.sb(st, f"gkk{i}", [128, T], F32) for i in range(2)]
        L = K.sb(st, "gL", [128, T], F32); L16 = K.sb(st, "gL16", [128, T], F32)
        Es = [K.sb(st, f"gE{i}", [128, T], F32) for i in range(2)]; Dts = [K.sb(st, f"gD{i}", [128, T], F32) for i in range(2)]
        Q0 = K.sb(st, "gQ0", [128, T], BF16); Kh = K.sb(st, "gKh", [128, T], BF16); Qs = K.sb(st, "gQs", [128, T], BF16)
        Ks = [K.sb(st, f"gKs{i}", [128, T], BF16) for i in range(2)]
        vis_ = [K.sb(st, f"gvi{i}", [128, 16, 128], BF16) for i in range(2)]; Kht = K.sb(st, "gKht", [128, 16, 128], BF16)
        scm = K.sb(st, "gscm", [128, 16, 128], BF16)
        oT = K.sb(st, "goT", [128, T], F32); hgss_ = [K.sb(st, f"ghgs{i}", [128, T], BF16) for i in range(2)]
        dec = K.sb(st, "gdec", [128, 16], F32)
        S32 = K.sb(st, "gS32", [128, 128], F32); Sb = K.sb(st, "gSb", [128, 128], BF16)
        sqb = K.sb(st, "gsq", [128, 512], BF16); rst = K.sb(st, "grst", [128, 512], F32); tmpn = K.sb(st, "gtmp", [128, 512], F32)
        bo = [K.sb(st, f"gbo{i}", [128, 512], BF16) for i in range(2)]
        scps = [K.psum(st, f"gsc{i}", [128, 512]) for i in range(4)]
        tps = [K.psum(st, f"gtp{i}", [128, 1024], BF16) for i in range(2)]
        ops = K.psum(st, "gops", [128, 512]); sps = K.psum(st, "gsps", [128, 512])
        v3 = lambda t: t.ap[:].rearrange("p (n c) -> p n c", c=128)
        def g_loads(hd):
            rows = slice(hd * 128, (hd + 1) * 128); b = hd % 2
            K.dma(K.sp, lfs_[b].ap[:], lfT.ap[rows, :], reads=[lfT], writes=[lfs_[b]], owner=lfs_[b])
            K.dma(K.sp, qs_[b].ap[:], hqT.ap[rows, :], reads=[hqT], writes=[qs_[b]], owner=qs_[b])
            K.dma(K.sp, kks_[b].ap[:], kkT.ap[rows, :], reads=[kkT], writes=[kks_[b]], owner=kks_[b])
            K.dma(K.sp, vis_[b].ap[:], hitok.ap[hd].rearrange("(sb p) d -> p sb d", p=128), reads=[hitok], writes=[vis_[b]], owner=vis_[b])
            K.dma(K.sp, hgss_[b].ap[:], hgT.ap[rows, :], reads=[hgT], writes=[hgss_[b]], owner=hgss_[b])
        g_loads(0)
        ej = [0]
        def nxtE():
            ej[0] += 1
            return Es[ej[0] % 2], Dts[ej[0] % 2]
        for hd in range(8):
            rows = slice(hd * 128, (hd + 1) * 128)
            q = qs_[hd % 2]; lf = lfs_[hd % 2]; kk = kks_[hd % 2]; vi = vis_[hd % 2]; hgs = hgss_[hd % 2]
            if hd + 1 < 8:
                g_loads(hd + 1)
            K.op(K.dve, lambda: nc.vector.tensor_tensor_scan(out=L.ap[:], data0=rm128.ap[:], data1=lf.ap[:], initial=0.0, op0=ALU.mult, op1=ALU.add),
                 reads=[rm128, lf], writes=[L])
            K.op(K.dve, lambda: nc.vector.tensor_tensor_scan(out=L16.ap[:], data0=rm16.ap[:], data1=lf.ap[:], initial=0.0, op0=ALU.mult, op1=ALU.add),
                 reads=[rm16, lf], writes=[L16])
            E, Dt = nxtE()
            K.op(K.act, lambda: nc.scalar.activation(out=E.ap[:], in_=L.ap[:], func=AF.Exp), reads=[L], writes=[E])
            K.op(K.dve, lambda: nc.vector.tensor_tensor(out=Q0.ap[:], in0=q.ap[:], in1=E.ap[:], op=ALU.mult), reads=[q, E], writes=[Q0])
            K.op(K.act, lambda: nc.scalar.activation(out=dec.ap[:], in_=v3(L)[:, :, 127], func=AF.Exp), reads=[L], writes=[dec])
            E, Dt = nxtE()
            K.op(K.dve, lambda: nc.vector.tensor_tensor(out=v3(Dt), in0=v3(L)[:, :, 127:128].to_broadcast([128, 16, 128]), in1=v3(L), op=ALU.subtract),
                 reads=[L], writes=[Dt])
            K.op(K.act, lambda: nc.scalar.activation(out=E.ap[:], in_=Dt.ap[:], func=AF.Exp), reads=[Dt], writes=[E])
            K.op(K.dve, lambda: nc.vector.tensor_tensor(out=Kh.ap[:], in0=kk.ap[:], in1=E.ap[:], op=ALU.mult), reads=[kk, E], writes=[Kh])
            E, Dt = nxtE()
            K.op(K.act, lambda: nc.scalar.activation(out=E.ap[:], in_=L16.ap[:], func=AF.Exp), reads=[L16], writes=[E])
            K.op(K.dve, lambda: nc.vector.tensor_tensor(out=Qs.ap[:], in0=q.ap[:], in1=E.ap[:], op=ALU.mult), reads=[q, E], writes=[Qs])
            for n in range(16):
                tp = tps[n // 8]
                K.op(K.pe, (lambda n=n, tp=tp: nc.tensor.transpose(tp.ap[:, (n % 8) * 128:(n % 8 + 1) * 128], Kh.ap[:, n * 128:(n + 1) * 128], C.ident.ap[:])),
                     reads=[Kh, C.ident], writes=[tp], sig=(n % 8 == 7))
            for i2 in range(2):
                K.op(K.act, (lambda i2=i2: nc.scalar.copy(out=Kht.ap[:, i2 * 8:(i2 + 1) * 8, :], in_=tps[i2].ap[:].rearrange("p (n c) -> p n c", c=128))),
                     reads=[tps[i2]], writes=[Kht])
            EB = [None] * 8
            def sub_exp(i):
                E, Dt = nxtE()
                EB[i] = E
                if i == 0:
                    K.op(K.dve, lambda: nc.vector.tensor_scalar(out=Dt.ap[:], in0=L.ap[:], scalar1=-1.0, scalar2=None, op0=ALU.mult), reads=[L], writes=[Dt])
                else:
                    K.op(K.dve, (lambda: nc.vector.tensor_tensor(out=v3(Dt), in0=v3(L)[:, :, 16 * i - 1:16 * i].to_broadcast([128, 16, 128]), in1=v3(L),
                                                                 op=ALU.subtract)), reads=[L], writes=[Dt])
                K.op(K.act, lambda: nc.scalar.activation(out=E.ap[:], in_=Dt.ap[:], func=AF.Exp), reads=[Dt], writes=[E])
            sub_exp(0)
            for i in range(8):
                ks = Ks[i % 2]
                if i + 1 < 8:
                    sub_exp(i + 1)
                E = EB[i]
                K.op(K.dve, (lambda ks=ks, E=E: nc.vector.scalar_tensor_tensor(out=ks.ap[:], in0=E.ap[:], scalar=1e30, in1=kk.ap[:], op0=ALU.min, op1=ALU.mult)),
                     reads=[E, kk], writes=[ks])
                for n in range(16):
                    sc = scps[n // 4]; cb0 = (n % 4) * 128 + 16 * i
                    K.op(K.pe, (lambda n=n, sc=sc, cb0=cb0, ks=ks, i=i: nc.tensor.matmul(sc.ap[:, cb0:cb0 + 16], ks.ap[:, n * 128:(n + 1) * 128],
                                                                                    Qs.ap[:, n * 128 + 16 * i:n * 128 + 16 * i + 16], start=True, stop=True)),
                         reads=[ks, Qs], writes=[sc], sig=(n == 15))
            for n4 in range(4):
                K.op(K.dve, (lambda n4=n4: nc.vector.tensor_tensor(out=scm.ap[:, n4 * 4:(n4 + 1) * 4, :], in0=scps[n4].ap[:].rearrange("p (n c) -> p n c", c=128),
                                                                   in1=m01.ap[:].rearrange("p (o c) -> p o c", o=1).to_broadcast([128, 4, 128]), op=ALU.mult)),
                     reads=[scps[n4], m01], writes=[scm])
            for n in range(16):
                oc = ops.ap[:, (n % 4) * 128:(n % 4 + 1) * 128]
                if n > 0:
                    K.op(K.pe, (lambda n=n, oc=oc: nc.tensor.matmul(oc, Sb.ap[:], Q0.ap[:, n * 128:(n + 1) * 128], start=True, stop=False)),
                         reads=[Sb, Q0], writes=[ops], sig=False)
                K.op(K.pe, (lambda n=n, oc=oc: nc.tensor.matmul(oc, vi.ap[:, n, :], scm.ap[:, n, :], start=(n == 0), stop=True)),
                     reads=[vi, scm], writes=[ops], sig=True)
                K.op(K.act, (lambda n=n, oc=oc: nc.scalar.copy(out=oT.ap[:, n * 128:(n + 1) * 128], in_=oc)), reads=[ops], writes=[oT])
                if n < 15:
                    sp_ = sps.ap[:, (n % 4) * 128:(n % 4 + 1) * 128]
                    K.op(K.pe, (lambda n=n, sp_=sp_: nc.tensor.matmul(sp_, Kht.ap[:, n, :], vi.ap[:, n, :], start=True, stop=True)),
                         reads=[Kht, vi], writes=[sps], sig=True)
                    if n == 0:
                        K.op(K.dve, (lambda sp_=sp_: nc.vector.tensor_copy(out=S32.ap[:], in_=sp_)), reads=[sps], writes=[S32])
                    else:
                        K.op(K.dve, (lambda n=n, sp_=sp_: nc.vector.scalar_tensor_tensor(out=S32.ap[:], in0=S32.ap[:], scalar=dec.ap[:, n:n + 1], in1=sp_,
                                                                                         op0=ALU.mult, op1=ALU.add)), reads=[S32, dec, sps], writes=[S32])
                    K.op(K.act, lambda: nc.scalar.copy(out=Sb.ap[:], in_=S32.ap[:]), reads=[S32], writes=[Sb])
            ohn, _ = PRM["hgn"]
            for tt in range(4):
                cs = slice(tt * 512, (tt + 1) * 512)
                K.op(K.act, (lambda cs=cs: nc.scalar.activation(out=sqb.ap[:], in_=oT.ap[:, cs], func=AF.Square)), reads=[oT], writes=[sqb])
                K.op(K.pe, lambda: nc.tensor.matmul(ops.ap[:], C.ones.ap[:], sqb.ap[:], start=True, stop=True), reads=[C.ones, sqb], writes=[ops], sig=True)
                K.op(K.act, lambda: nc.scalar.activation(out=rst.ap[:], in_=ops.ap[:], func=AF.Sqrt, bias=C.epsb.ap[:, 0:1], scale=1.0 / 128), reads=[ops, C.epsb], writes=[rst])
                K.op(K.dve, lambda: nc.vector.reciprocal(out=rst.ap[:], in_=rst.ap[:]), reads=[rst], writes=[rst])
                K.op(K.dve, (lambda cs=cs: nc.vector.scalar_tensor_tensor(out=tmpn.ap[:], in0=oT.ap[:, cs], scalar=C.prm.ap[:, ohn:ohn + 1], in1=rst.ap[:],
                                                                         op0=ALU.mult, op1=ALU.mult)), reads=[oT, rst, C.prm], writes=[tmpn])
                b = bo[tt % 2]
                K.op(K.dve, (lambda cs=cs, b=b: nc.vector.tensor_tensor(out=b.ap[:], in0=tmpn.ap[:], in1=hgs.ap[:, cs], op=ALU.mult)), reads=[tmpn, hgs], writes=[b])
                K.dma(K.sp, ccT.ap[1024 + hd * 128:1024 + (hd + 1) * 128, cs], b.ap[:], reads=[b], writes=[ccT], owner=b)
        K.barrier()
        K.release([rm128, rm16, m01] + qs_ + lfs_ + kks_ + vis_ + hgss_ + bo)
        st.close()

    def lru_phase():
        st = contextlib.ExitStack()
        xbs = [K.sb(st, f"rxb{i}", [128, 2, T], F32) for i in range(2)]; gbs = [K.sb(st, f"rgb{i}", [128, 2, T], F32) for i in range(2)]
        yc = K.sb(st, "ryc", [128, 2, T], F32); ycb = K.sb(st, "rycb", [128, 2, T], BF16)
        r = K.sb(st, "rr", [128, 2, T], F32); ig = K.sb(st, "rig", [128, 2, T], F32)
        a = K.sb(st, "ra", [128, 2, T], F32); mu = K.sb(st, "rmu", [128, 2, T], F32)
        gt = K.sb(st, "rgt", [128, 2, T], F32)
        mo = K.sb(st, "rmo", [128, 2, T], BF16)
        wa = [K.sb(st, f"rwa{i}", [128, 2, 256], BF16) for i in range(2)]; wx = [K.sb(st, f"rwx{i}", [128, 2, 256], BF16) for i in range(2)]
        pss = [K.psum(st, f"rps{i}", [128, 512]) for i in range(8)]
        pi = 0
        ocw = [PRM[f"convw{t}"][0] for t in range(4)]; ocb = PRM["convb"][0]; oba = PRM["ba"][0]; obx = PRM["bx"][0]
        NB = LRU // 256

        def loads(nb):
            rows = slice(nb * 256, (nb + 1) * 256)
            K.dma(K.pool, wa[nb % 2].ap[:], w_ra.ap[nb], reads=[w_ra], writes=[wa[nb % 2]], owner=wa[nb % 2])
            K.dma(K.pool, wx[nb % 2].ap[:], w_rx.ap[nb], reads=[w_rx], writes=[wx[nb % 2]], owner=wx[nb % 2])
            K.dma(K.sp, xbs[nb % 2].ap[:], xbT.ap[rows, :].rearrange("(k p) t -> p k t", p=128), reads=[xbT], writes=[xbs[nb % 2]], owner=xbs[nb % 2])
            K.dma(K.sp, gbs[nb % 2].ap[:], gbT.ap[rows, :].rearrange("(k p) t -> p k t", p=128), reads=[gbT], writes=[gbs[nb % 2]], owner=gbs[nb % 2])

        loads(0)
        for nb in range(NB):
            rows = slice(nb * 256, (nb + 1) * 256)
            W1 = wa[nb % 2]; W2 = wx[nb % 2]; xb = xbs[nb % 2]; gb = gbs[nb % 2]
            if nb + 1 < NB:
                loads(nb + 1)
            for k in range(2):
                K.op(K.pool, (lambda k=k: nc.gpsimd.tensor_tensor(out=gt.ap[:, k, :], in0=gb.ap[:, k, :], in1=gb.ap[:, k, :], op=ALU.mult)), reads=[gb], writes=[gt])
                K.op(K.pool, (lambda k=k: nc.gpsimd.tensor_scalar(out=gt.ap[:, k, :], in0=gt.ap[:, k, :], scalar1=0.044715, scalar2=1.0, op0=ALU.mult, op1=ALU.add)),
                     reads=[gt], writes=[gt])
                K.op(K.pool, (lambda k=k: nc.gpsimd.tensor_tensor(out=gt.ap[:, k, :], in0=gt.ap[:, k, :], in1=gb.ap[:, k, :], op=ALU.mult)), reads=[gt, gb], writes=[gt])
            for k in range(2):
                ch = nb * 2 + k
                K.op(K.act, (lambda k=k, ch=ch: nc.scalar.activation(out=yc.ap[:, k, :], in_=xb.ap[:, k, :], func=AF.Identity,
                                                                     scale=C.prm.ap[:, ocw[0] + ch:ocw[0] + ch + 1], bias=C.prm.ap[:, ocb + ch:ocb + ch + 1])),
                     reads=[xb, C.prm], writes=[yc])
                for tap in range(1, 4):
                    K.op(K.dve, (lambda k=k, ch=ch, tap=tap: nc.vector.scalar_tensor_tensor(out=yc.ap[:, k, tap:], in0=xb.ap[:, k, 0:T - tap],
                                                                                            scalar=C.prm.ap[:, ocw[tap] + ch:ocw[tap] + ch + 1], in1=yc.ap[:, k, tap:],
                                                                                            op0=ALU.mult, op1=ALU.add)), reads=[xb, yc, C.prm], writes=[yc])
                K.op(K.act, (lambda k=k: nc.scalar.copy(out=ycb.ap[:, k, :], in_=yc.ap[:, k, :])), reads=[yc], writes=[ycb])
            for k in range(2):
                ch = nb * 2 + k
                for tt in range(4):
                    cs = slice(tt * 512, (tt + 1) * 512)
                    for (W, dstt, ob) in ((W1, r, oba), (W2, ig, obx)):
                        ps = pss[pi % 8]; pi += 1
                        for ic in range(2):
                            K.op(K.pe, (lambda W=W, ps=ps, ic=ic, k=k, cs=cs: nc.tensor.matmul(ps.ap[:], W.ap[:, ic, k * 128:(k + 1) * 128], ycb.ap[:, ic, cs],
                                                                                           start=(ic == 0), stop=(ic == 1))), reads=[W, ycb], writes=[ps], sig=(ic == 1))
                        K.op(K.act, (lambda ps=ps, dstt=dstt, ob=ob, k=k, cs=cs, ch=ch: nc.scalar.activation(out=dstt.ap[:, k, cs], in_=ps.ap[:], func=AF.Sigmoid,
                                                                                                        bias=C.prm.ap[:, ob + ch:ob + ch + 1])),
                             reads=[ps, C.prm], writes=[dstt])
            for k in range(2):
                K.op(K.act, (lambda k=k: nc.scalar.activation(out=gt.ap[:, k, :], in_=gt.ap[:, k, :], func=AF.Sigmoid, scale=1.5957691216057308)), reads=[gt], writes=[gt])
            for k in range(2):
                K.op(K.pool, (lambda k=k: nc.gpsimd.tensor_tensor(out=gt.ap[:, k, :], in0=gt.ap[:, k, :], in1=gb.ap[:, k, :], op=ALU.mult)), reads=[gt, gb], writes=[gt])
            for k in range(2):
                scp = C.drv.ap[:, 24 + nb * 2 + k:25 + nb * 2 + k]
                K.op(K.act, (lambda k=k, scp=scp: nc.scalar.activation(out=a.ap[:, k, :], in_=r.ap[:, k, :], func=AF.Exp, scale=scp)), reads=[r, C.drv], writes=[a])
            for k in range(2):
                K.op(K.act, (lambda k=k: nc.scalar.activation(out=mu.ap[:, k, :], in_=a.ap[:, k, :], func=AF.Square)), reads=[a], writes=[mu])
            for k in range(2):
                K.op(K.dve, (lambda k=k: nc.vector.tensor_scalar(out=mu.ap[:, k, :], in0=mu.ap[:, k, :], scalar1=1.0, scalar2=-1.0, op0=ALU.min, op1=ALU.mult)), reads=[mu], writes=[mu])
            for k in range(2):
                K.op(K.act, (lambda k=k: nc.scalar.activation(out=mu.ap[:, k, :], in_=mu.ap[:, k, :], func=AF.Sqrt, bias=1.0, scale=1.0)), reads=[mu], writes=[mu])
            for k in range(2):
                K.op(K.dve, (lambda k=k: nc.vector.memset(mu.ap[:, k, 0:1], 1.0)), reads=[], writes=[mu])
                K.op(K.dve, (lambda k=k: nc.vector.tensor_tensor(out=mu.ap[:, k, :], in0=mu.ap[:, k, :], in1=ig.ap[:, k, :], op=ALU.mult)), reads=[mu, ig], writes=[mu])
                K.op(K.dve, (lambda k=k: nc.vector.tensor_tensor(out=mu.ap[:, k, :], in0=mu.ap[:, k, :], in1=yc.ap[:, k, :], op=ALU.mult)), reads=[mu, yc], writes=[mu])
            for k in range(2):
                K.op(K.dve, (lambda k=k: nc.vector.tensor_tensor_scan(out=r.ap[:, k, :], data0=a.ap[:, k, :], data1=mu.ap[:, k, :], initial=0.0, op0=ALU.mult, op1=ALU.add)),
                     reads=[a, mu], writes=[r])
            for k in range(2):
                K.op(K.pool, (lambda k=k: nc.gpsimd.tensor_tensor(out=mo.ap[:, k, :], in0=gt.ap[:, k, :], in1=r.ap[:, k, :], op=ALU.mult)), reads=[gt, r], writes=[mo])
            K.dma(K.sp, mxT.ap[rows, :].rearrange("(k p) t -> p k t", p=128), mo.ap[:], reads=[mo], writes=[mxT], owner=mo)
        K.barrier()
        K.release(xbs + gbs + [mo] + wa + wx)
        st.close()

    token_phase("p0", 0, False, inproj_even)
    if stage >= 2:
        attention_phase()
    if stage >= 3:
        hgrn2_phase()
    if stage >= 4:
        token_phase("p1", 0, True, inproj_odd)
    if stage >= 5:
        lru_phase()
    if stage >= 6:
        token_phase("p2", 1, True, None)
    K.barrier()
    gs.close()
    es.close()
    return nc


def host_inputs(inp):
    f = lambda a: np.ascontiguousarray(np.asarray(a, dtype=np.float32))
    sh = {}
    wi = f(inp["w_in_even"][0])
    names = ["w_sq", "w_sk", "w_sv", "w_hq", "w_hf", "w_hi", "w_hg"]
    for i, nm in enumerate(names):
        sh[nm] = _wl(wi[:, i * 1024:(i + 1) * 1024], 512)
    sh["w_oe"] = _wl(f(inp["w_out_even"][0]), 512)
    sh["w_io"] = _wl(f(inp["w_in_odd"][0]), 512)
    sh["w_oo"] = _wl(f(inp["w_out_odd"][0]), 256)
    for i in range(2):
        gu = f(inp["w_gate_up"][i])
        g = gu[:, :DFF].reshape(D, FC // 2, 256); u = gu[:, DFF:].reshape(D, FC // 2, 256)
        sh[f"w_gu{i}"] = _wl(np.concatenate([g, u], axis=2).reshape(D, FC * 256), 512)
        sh[f"w_dn{i}"] = _wl(f(inp["w_down"][i]), 128)
        sh[f"w_pg{i}"] = _wl(f(inp["w_ple_gate"][i]), 512)
        sh[f"w_pu{i}"] = _wl(f(inp["w_ple_up"][i]), 2048)
    for nm, key in (("w_ra", "rg_wa"), ("w_rx", "rg_wx")):
        w = f(inp[key][0])
        sh[nm] = np.ascontiguousarray(w.reshape(11, 2, 128, 256).transpose(0, 2, 1, 3))
    prm = np.zeros((128, NPRM), np.float32)
    def put(name, arr):
        off, n = PRM[name]; prm[:, off:off + n] = arr
    for i in range(2):
        for nm in ("mix_pre_g", "mix_post_g", "ffn_pre_g", "ffn_post_g", "ple_norm_g"):
            put(f"{nm}{i}", _cols(f(inp[nm][i])))
    put("lb0", _cols(f(inp["hg_lb_logits"][0]))); put("lb1", _cols(f(inp["hg_lb_logits"][1])))
    put("hgn", f(inp["hg_norm_g"][0]).reshape(128, 1))
    for tap in range(4):
        put(f"convw{tap}", _cols(f(inp["conv_w"][0, tap])))
    put("convb", _cols(f(inp["conv_b"][0]))); put("ba", _cols(f(inp["rg_ba"][0]).reshape(-1))); put("bx", _cols(f(inp["rg_bx"][0]).reshape(-1)))
    put("lam", _cols(f(inp["rg_lambda"][0])))
    sh["prm"] = prm
    for k, v in _consts().items():
        sh["c_" + k] = v
    return sh


def kernel(**inp):
    sh = host_inputs(inp)
    x = np.asarray(inp["x"], np.float32); p = np.asarray(inp["p"], np.float32)
    nc = build()
    in_maps = []
    for b in range(8):
        m = dict(sh)
        m["xT"] = np.ascontiguousarray(x[b].T)
        m["pT"] = np.ascontiguousarray(p[:, b].transpose(0, 2, 1))
        in_maps.append(m)
    res = run_bass_kernel_spmd(nc, in_maps, core_ids=list(range(8)))
    out = np.stack([np.ascontiguousarray(res.results[b]["outT"].T) for b in range(8)], axis=0)
    return out.astype(np.float32)
```

```python
import contextlib
import numpy as np
import concourse.bass as bass
import concourse.mybir as mybir
from concourse.bass_utils import run_bass_kernel_spmd

F32 = mybir.dt.float32
F32R = mybir.dt.float32r
BF16 = mybir.dt.bfloat16
AF = mybir.ActivationFunctionType
ALU = mybir.AluOpType

D = 2048
T = 2048
TT = 512
NTT = T // TT
KC = D // 128
DFF = 5632
FC = DFF // 128
LRU = 2816
LC = LRU // 128
PLE = 256
EPS = 1e-6
WSLOT = 8192


def _wl(w, cb):
    K, N = w.shape
    return np.ascontiguousarray(w.reshape(K // 128, 128, N // cb, cb).transpose(2, 1, 0, 3))


def _cols(v):
    return np.ascontiguousarray(v.reshape(-1, 128).T)


PRM = {}


def _prm_layout():
    off = 0
    def add(name, n):
        nonlocal off
        PRM[name] = (off, n)
        off += n
    for i in range(2):
        for nm in ("mix_pre_g", "mix_post_g", "ffn_pre_g", "ffn_post_g", "ple_norm_g"):
            add(f"{nm}{i}", 16)
    add("lb0", 8); add("lb1", 8); add("hgn", 1)
    for tap in range(4):
        add(f"convw{tap}", LC)
    add("convb", LC); add("ba", LC); add("bx", LC); add("lam", LC)
    return off


NPRM = _prm_layout()


def _consts():
    c = {}
    s = np.arange(128)[:, None]; t = np.arange(128)[None, :]
    c["ident"] = np.eye(128, dtype=np.float32)
    c["ones"] = np.ones((128, 128), np.float32)
    c["zeros"] = np.zeros((128, 512), np.float32)
    c["maskneg"] = np.where(s >= t, -30000.0, 0.0).astype(np.float32)
    c["mask01"] = (s <= t).astype(np.float32)
    c["negU"] = np.where(s >= t, -1.0, 0.0).astype(np.float32)
    c["negOnes"] = -np.ones((128, 128), np.float32)
    tt = np.arange(T)
    c["rm128"] = np.broadcast_to((tt % 128 != 0).astype(np.float32), (128, T)).copy()
    c["rm16"] = np.broadcast_to((tt % 16 != 0).astype(np.float32), (128, T)).copy()
    return c


class Tl:
    def __init__(self, name, ap):
        self.name = name; self.ap = ap
        self.w = {}; self.r = {}
        self.dsem = None; self.dcnt = 0
        self.dram = False

    def __getitem__(self, k):
        return self.ap[k]


class Eng:
    def __init__(self, name, eng, sem):
        self.name = name; self.eng = eng; self.sem = sem; self.cnt = 0
        self.waited = {}; self.pend = []


class Kern:
    def __init__(self, nc, es):
        self.nc = nc; self.es = es
        self.pe = Eng("pe", nc.tensor, es.enter_context(nc.semaphore("s_pe")))
        self.act = Eng("act", nc.scalar, es.enter_context(nc.semaphore("s_act")))
        self.dve = Eng("dve", nc.vector, es.enter_context(nc.semaphore("s_dve")))
        self.pool = Eng("pool", nc.gpsimd, es.enter_context(nc.semaphore("s_pool")))
        self.sp = Eng("sp", nc.sync, es.enter_context(nc.semaphore("s_sp")))
        self.engs = [self.pe, self.act, self.dve, self.pool, self.sp]
        self.dsems = []
        self.nsem = 5

    def sb(self, stack, name, shape, dt):
        return Tl(name, stack.enter_context(self.nc.sbuf_tensor(name, list(shape), dt)))

    def psum(self, stack, name, shape, dt=F32):
        return Tl(name, stack.enter_context(self.nc.psum_tensor(name, list(shape), dt)))

    def dram(self, name, shape, dt, kind="Internal"):
        t = Tl(name, self.nc.dram_tensor(name, list(shape), dt, kind=kind).ap())
        t.dram = True
        return t

    def _wait(self, E, ev, selfskip=False):
        for sid, (sem, val) in ev.items():
            if selfskip and sid == id(E.sem):
                continue
            if E is self.pe and sid == id(E.sem):
                continue
            if E.waited.get(sid, 0) < val:
                E.eng.wait_ge(sem, val)
                E.waited[sid] = val

    def _deps(self, E, reads, writes, selfskip=False):
        ev = {}
        def mrg(d):
            for sid, (sem, val) in d.items():
                if sid not in ev or ev[sid][1] < val:
                    ev[sid] = (sem, val)
        for t in reads:
            mrg(t.w)
        for t in writes:
            mrg(t.w); mrg(t.r)
        self._wait(E, ev, selfskip)

    def _record(self, ev, reads, writes):
        sid, sem, val = ev
        for t in writes:
            t.w = {sid: (sem, val)}; t.r = {}
        for t in reads:
            t.r[sid] = (sem, val)

    def op(self, E, fn, reads=(), writes=(), sig=True, selfskip=False, selfwait=None):
        self._deps(E, reads, writes, selfskip)
        if selfwait is not None and E.waited.get(id(E.sem), 0) < selfwait:
            E.eng.wait_ge(E.sem, selfwait)
            E.waited[id(E.sem)] = selfwait
        inst = fn()
        if sig:
            E.cnt += 1
            inst.then_inc(E.sem, 1)
            ev = (id(E.sem), E.sem, E.cnt)
            self._record(ev, reads, writes)
            for (r, w) in E.pend:
                self._record(ev, r, w)
            E.pend = []
        else:
            ev = (id(E.sem), E.sem, E.cnt + 1)
            self._record(ev, reads, writes)
        return inst

    def dma(self, Q, out_ap, in_ap, reads, writes, owner):
        reads = [t for t in reads if not t.dram]
        writes = [t for t in writes if not t.dram]
        self._deps(Q, reads, writes)
        if owner.dsem is None:
            owner.dsem = self.es.enter_context(self.nc.semaphore("d_" + owner.name))
            self.dsems.append(owner); self.nsem += 1
        owner.dcnt += 16
        Q.eng.dma_start(out=out_ap, in_=in_ap).then_inc(owner.dsem, 16)
        ev = (id(owner.dsem), owner.dsem, owner.dcnt)
        sid, sem, val = ev
        for t in writes:
            t.w = {sid: (sem, val)}; t.r = {}
        for t in reads:
            t.r[sid] = (sem, val)

    def barrier(self):
        ev = {}
        for E in self.engs:
            if E.cnt:
                ev[id(E.sem)] = (E.sem, E.cnt)
        for t in self.dsems:
            ev[id(t.dsem)] = (t.dsem, t.dcnt)
        for E in self.engs:
            self._wait(E, ev)

    def release(self, tiles):
        self.dsems = [t for t in self.dsems if t not in tiles]


class Ctx:
    pass


def wview(slot, kc, cb):
    return slot.ap[:, 0:kc * cb].rearrange("p (k c) -> p k c", c=cb)


def load_w(K, C, wl_dram, nb, kc, cb):
    slot = C.wslots[C.wi % len(C.wslots)]; C.wi += 1
    K.dma(K.pool, wview(slot, kc, cb), wl_dram.ap[nb], reads=[wl_dram], writes=[slot], owner=slot)
    return slot


def mm_group(K, C, ps_t, ps_ap, pairs, reads):
    n = len(pairs)
    for i, (l, r) in enumerate(pairs):
        K.op(K.pe, (lambda l=l, r=r, i=i: K.nc.tensor.matmul(ps_ap, l, r, start=(i == 0), stop=(i == n - 1))),
             reads=reads, writes=[ps_t], sig=(i == n - 1))


def next_ps(C):
    t = C.ps[C.pi % len(C.ps)]; C.pi += 1
    return t


def stats_begin(C, n, lag):
    C.statk = 0; C.statn = n; C.pend = []; C.lag = lag


def ones_mm(K, C):
    sq = C.pend.pop(0)
    k = C.statk; C.statk += 1
    K.op(K.pe, (lambda: K.nc.tensor.matmul(C.stat.ap[:], C.ones.ap[:], sq.ap[:], start=(k == 0), stop=(k == C.statn - 1))),
         reads=[sq, C.ones], writes=[C.stat], sig=True)


def sq_push(K, C, t, ap, eng):
    sq = C.sq[C.sqi % len(C.sq)]; C.sqi += 1
    if eng == "act":
        K.op(K.act, (lambda: K.nc.scalar.activation(out=sq.ap[:], in_=ap, func=AF.Square)), reads=[t], writes=[sq])
    else:
        K.op(K.dve, (lambda: K.nc.vector.tensor_tensor(out=sq.ap[:], in0=ap, in1=ap, op=ALU.mult)), reads=[t], writes=[sq])
    C.pend.append(sq)
    if len(C.pend) > C.lag:
        ones_mm(K, C)


def stats_finish(K, C, scale, out_rstd):
    while C.pend:
        ones_mm(K, C)
    assert C.statk == C.statn
    K.op(K.act, lambda: K.nc.scalar.activation(out=out_rstd.ap[:], in_=C.stat.ap[:], func=AF.Sqrt, bias=C.epsb.ap[:, 0:1], scale=scale),
         reads=[C.stat, C.epsb], writes=[out_rstd])
    K.op(K.dve, lambda: K.nc.vector.reciprocal(out=out_rstd.ap[:], in_=out_rstd.ap[:]), reads=[out_rstd], writes=[out_rstd])


def prm(C, name, j=None):
    off, n = PRM[name]
    if j is None:
        return C.prm.ap[:, off:off + n]
    return C.prm.ap[:, off + j:off + j + 1]


def norm_bf16(K, C, src, gname, dst, have_stats=False):
    if not have_stats:
        stats_begin(C, KC, 1)
        for kc in range(KC):
            sq_push(K, C, src, src.ap[:, kc, :], "act")
    stats_finish(K, C, 1.0 / D, C.rstd)
    for kc in range(KC):
        K.op(K.dve, (lambda kc=kc: K.nc.vector.scalar_tensor_tensor(out=dst.ap[:, kc, :], in0=src.ap[:, kc, :], scalar=prm(C, gname, kc),
                                                                   in1=C.rstd.ap[:], op0=ALU.mult, op1=ALU.mult)),
             reads=[src, C.rstd, C.prm], writes=[dst], selfskip=(kc > 0))


def post_norm_res(K, C, y, gname, h, follow=None, nT=None):
    stats_finish(K, C, 1.0 / D, C.rstd2)
    if follow == "norm":
        stats_begin(C, KC, 1)
    cnt_scale = {}

    def scale(kc):
        K.op(K.dve, (lambda: K.nc.vector.scalar_tensor_tensor(out=y.ap[:, kc, :], in0=y.ap[:, kc, :], scalar=prm(C, gname, kc),
                                                             in1=C.rstd2.ap[:], op0=ALU.mult, op1=ALU.mult)),
             reads=[y, C.rstd2, C.prm], writes=[y], selfskip=(kc > 0))
        cnt_scale[kc] = K.dve.cnt
    scale(0)
    for kc in range(KC):
        if kc + 1 < KC:
            scale(kc + 1)
        K.op(K.dve, (lambda kc=kc: K.nc.vector.tensor_tensor(out=h.ap[:, kc, :], in0=h.ap[:, kc, :], in1=y.ap[:, kc, :], op=ALU.add)),
             reads=[y, h], writes=[h], selfskip=True, selfwait=cnt_scale[kc])
        if follow == "norm":
            sq_push(K, C, h, h.ap[:, kc, :], "act")
        elif follow == "copy":
            K.op(K.act, (lambda kc=kc: K.nc.scalar.copy(out=nT.ap[:, kc, :], in_=h.ap[:, kc, :])), reads=[h], writes=[nT])


def linear_fm(K, C, inT, kcn, wl_dram, ncols, cb, epi):
    nblk = ncols // cb
    for nb in range(nblk):
        slot = load_w(K, C, wl_dram, nb, kcn, cb)
        wv = wview(slot, kcn, cb)
        for ci in range(cb // 128):
            ps = next_ps(C)
            mm_group(K, C, ps, ps.ap[:], [(wv[:, kc, ci * 128:(ci + 1) * 128], inT.ap[:, kc, :]) for kc in range(kcn)], reads=[slot, inT])
            epi(nb * (cb // 128) + ci, ps)


def stage_out(C, dt):
    lst = C.stg32 if dt == F32 else C.stg16
    t = lst[C.stgi[dt] % len(lst)]; C.stgi[dt] += 1
    return t


def build(stage=99, dbg=()):
    nc = bass.Bass("TRN2", target_bir_lowering=False)
    es = contextlib.ExitStack()
    K = Kern(nc, es)
    C = Ctx()
    dr = {}

    def ein(name, shape, dt=F32):
        dr[name] = K.dram(name, shape, dt, kind="ExternalInput"); return dr[name]

    def scr(name, shape, dt, out=False):
        dr[name] = K.dram(name, shape, dt, kind=("ExternalOutput" if (out or name in dbg) else "Internal")); return dr[name]

    xT = ein("xT", [D, T]); pT = ein("pT", [2, PLE, T]); prm_d = ein("prm", [128, NPRM])
    cst = {k: ein("c_" + k, list(v.shape)) for k, v in _consts().items()}
    w_sq = ein("w_sq", [2, 128, KC, 512]); w_sk = ein("w_sk", [2, 128, KC, 512]); w_sv = ein("w_sv", [2, 128, KC, 512])
    w_hq = ein("w_hq", [2, 128, KC, 512]); w_hf = ein("w_hf", [2, 128, KC, 512]); w_hi = ein("w_hi", [2, 128, KC, 512])
    w_hg = ein("w_hg", [2, 128, KC, 512])
    w_oe = ein("w_oe", [4, 128, KC, 512])
    w_io = ein("w_io", [2 * LRU // 512, 128, KC, 512])
    w_oo = ein("w_oo", [D // 256, 128, LC, 256])
    w_gu = [ein(f"w_gu{i}", [FC // 2, 128, KC, 512]) for i in range(2)]
    w_dn = [ein(f"w_dn{i}", [D // 128, 128, FC, 128]) for i in range(2)]
    w_pg = [ein(f"w_pg{i}", [4, 128, KC, 512]) for i in range(2)]
    w_pu = [ein(f"w_pu{i}", [1, 128, 2, 2048]) for i in range(2)]
    w_ra = ein("w_ra", [LRU // 256, 128, 2, 256]); w_rx = ein("w_rx", [LRU // 256, 128, 2, 256])

    outT = scr("outT", [D, T], F32, out=True)
    hT = scr("hT", [D, T], F32)
    qT = scr("qT", [1024, T], BF16); kT = scr("kT", [1024, T], BF16); vtok = scr("vtok", [8, T, 128], BF16)
    hqT = scr("hqT", [1024, T], F32); lfT = scr("lfT", [1024, T], F32); kkT = scr("kkT", [1024, T], F32)
    hgT = scr("hgT", [1024, T], BF16); hitok = scr("hitok", [8, T, 128], BF16)
    ccT = scr("ccT", [D, T], BF16)
    gbT = scr("gbT", [LRU, T], F32); xbT = scr("xbT", [LRU, T], F32)
    mxT = scr("mxT", [LRU, T], BF16)

    dbgt = {nm: scr(nm, [D, T], F32) for nm in ("dbg_m", "dbg_hmix", "dbg_hffn") if nm in dbg}
    def dbg_store(nm, tile, t0, tag):
        if nm in dbgt and tag == "p1":
            K.dma(K.sp, dbgt[nm].ap.rearrange("(k p) t -> p k t", p=128)[:, :, t0:t0 + TT], tile.ap[:], reads=[tile], writes=[dbgt[nm]], owner=tile)

    def kcv(d, t0):
        return d.ap.rearrange("(k p) t -> p k t", p=128)[:, :, t0:t0 + TT]

    gs = contextlib.ExitStack()
    C.prm = K.sb(gs, "prm_sb", [128, NPRM], F32)
    C.ones = K.sb(gs, "ones_sb", [128, 128], BF16)
    C.ident = K.sb(gs, "ident_sb", [128, 128], BF16)
    C.epsb = K.sb(gs, "epsb", [128, 1], F32)
    C.drv = K.sb(gs, "drv", [128, 64], F32)
    K.dma(K.sp, C.prm.ap[:], prm_d.ap[:, :], reads=[prm_d], writes=[C.prm], owner=C.prm)
    K.dma(K.pool, C.ones.ap[:], cst["ones"].ap[:, :], reads=[], writes=[C.ones], owner=C.ones)
    K.dma(K.pool, C.ident.ap[:], cst["ident"].ap[:, :], reads=[], writes=[C.ident], owner=C.ident)
    K.op(K.dve, lambda: nc.vector.memset(C.epsb.ap[:], EPS), writes=[C.epsb])
    o0, _ = PRM["lb0"]; o1, _ = PRM["lb1"]
    K.op(K.dve, lambda: nc.vector.tensor_tensor(out=C.drv.ap[:, 0:8], in0=C.prm.ap[:, o0:o0 + 8], in1=C.prm.ap[:, o1:o1 + 8], op=ALU.subtract),
         reads=[C.prm], writes=[C.drv])
    K.op(K.act, lambda: nc.scalar.activation(out=C.drv.ap[:, 0:8], in_=C.drv.ap[:, 0:8], func=AF.Sigmoid), reads=[C.drv], writes=[C.drv])
    K.op(K.dve, lambda: nc.vector.tensor_scalar(out=C.drv.ap[:, 8:16], in0=C.drv.ap[:, 0:8], scalar1=-1.0, scalar2=1.0, op0=ALU.mult, op1=ALU.add),
         reads=[C.drv], writes=[C.drv])
    K.op(K.dve, lambda: nc.vector.tensor_scalar(out=C.drv.ap[:, 16:24], in0=C.drv.ap[:, 8:16], scalar1=-1.0, scalar2=None, op0=ALU.mult),
         reads=[C.drv], writes=[C.drv])
    ol, _ = PRM["lam"]
    K.op(K.act, lambda: nc.scalar.activation(out=C.drv.ap[:, 24:46], in_=C.prm.ap[:, ol:ol + LC], func=AF.Exp, scale=-1.0), reads=[C.prm], writes=[C.drv])
    K.op(K.act, lambda: nc.scalar.activation(out=C.drv.ap[:, 24:46], in_=C.drv.ap[:, 24:46], func=AF.Ln, bias=1.0), reads=[C.drv], writes=[C.drv])
    K.op(K.dve, lambda: nc.vector.tensor_scalar(out=C.drv.ap[:, 24:46], in0=C.drv.ap[:, 24:46], scalar1=-8.0, scalar2=None, op0=ALU.mult),
         reads=[C.drv], writes=[C.drv])

    def token_phase(tag, layer, do_mix_ffn, inproj):
        ps_stack = contextlib.ExitStack()
        C.wslots = [K.sb(ps_stack, f"ws{i}_{tag}", [128, WSLOT], BF16) for i in range(3)]; C.wi = 0
        C.ps = [K.psum(ps_stack, f"ps{i}_{tag}", [128, TT]) for i in range(7)]; C.pi = 0
        C.stat = K.psum(ps_stack, f"stat_{tag}", [128, TT])
        C.sq = [K.sb(ps_stack, f"sq{i}_{tag}", [128, TT], BF16) for i in range(4)]; C.sqi = 0
        C.rstd = K.sb(ps_stack, f"rstd_{tag}", [128, TT], F32)
        C.rstd2 = K.sb(ps_stack, f"rstd2_{tag}", [128, TT], F32)
        nst = 3 if do_mix_ffn else 9
        C.stg32 = [K.sb(ps_stack, f"st32_{i}_{tag}", [128, TT], F32) for i in range(nst)]
        C.stg16 = [K.sb(ps_stack, f"st16_{i}_{tag}", [128, TT], BF16) for i in range(nst)]
        C.stgi = {F32: 0, BF16: 0}
        h = K.sb(ps_stack, f"h_{tag}", [128, KC, TT], F32)
        nT = K.sb(ps_stack, f"nT_{tag}", [128, KC, TT], BF16)
        if do_mix_ffn:
            act = K.sb(ps_stack, f"act_{tag}", [128, FC, TT], BF16)
            y = K.sb(ps_stack, f"y_{tag}", [128, KC, TT], F32)
            pt = K.sb(ps_stack, f"pt_{tag}", [128, 2, TT], BF16)
            tmp = [K.sb(ps_stack, f"tmp{i}_{tag}", [128, TT], F32) for i in range(2)]
            wpu = K.sb(ps_stack, f"wpu_{tag}", [128, 2 * D], BF16)
            K.dma(K.pool, wview(wpu, 2, D), w_pu[layer].ap[0], reads=[w_pu[layer]], writes=[wpu], owner=wpu)
        h_src = xT if layer == 0 else hT
        for tt in range(NTT):
            t0 = tt * TT
            K.dma(K.sp, h.ap[:], kcv(h_src, t0), reads=[h_src], writes=[h], owner=h)
            if do_mix_ffn:
                if layer == 0:
                    cc, ckc, wo, wcb = ccT, KC, w_oe, 512
                else:
                    cc, ckc, wo, wcb = mxT, LC, w_oo, 256
                if tt == 0:
                    K.dma(K.sp, act.ap[:, 0:ckc, :], kcv(cc, t0), reads=[cc], writes=[act], owner=act)
                def epi_y(c, ps):
                    K.op(K.act, lambda: nc.scalar.copy(out=y.ap[:, c, :], in_=ps.ap[:]), reads=[ps], writes=[y])
                    sq_push(K, C, y, y.ap[:, c, :], "dve")
                stats_begin(C, KC, 2)
                linear_fm(K, C, act, ckc, wo, D, wcb, epi_y)
                dbg_store("dbg_m", y, t0, tag)
                post_norm_res(K, C, y, f"mix_post_g{layer}", h, follow="norm")
                dbg_store("dbg_hmix", h, t0, tag)
                norm_bf16(K, C, h, f"ffn_pre_g{layer}", nT, have_stats=True)
                for nb in range(FC // 2):
                    slot = load_w(K, C, w_gu[layer], nb, KC, 512)
                    wv = wview(slot, KC, 512)
                    for ci in range(2):
                        psg = next_ps(C); psu = next_ps(C)
                        mm_group(K, C, psg, psg.ap[:], [(wv[:, kc, ci * 128:(ci + 1) * 128], nT.ap[:, kc, :]) for kc in range(KC)], reads=[slot, nT])
                        mm_group(K, C, psu, psu.ap[:], [(wv[:, kc, 256 + ci * 128:256 + (ci + 1) * 128], nT.ap[:, kc, :]) for kc in range(KC)], reads=[slot, nT])
                        tm = tmp[(nb * 2 + ci) % 2]; fc = nb * 2 + ci
                        K.op(K.act, lambda: nc.scalar.activation(out=tm.ap[:], in_=psg.ap[:], func=AF.Silu), reads=[psg], writes=[tm])
                        K.op(K.dve, lambda: nc.vector.tensor_tensor(out=act.ap[:, fc, :], in0=tm.ap[:], in1=psu.ap[:], op=ALU.mult), reads=[tm, psu], writes=[act])
                stats_begin(C, KC, 2)
                linear_fm(K, C, act, FC, w_dn[layer], D, 128, epi_y)
                if tt + 1 < NTT:
                    K.dma(K.sp, act.ap[:, 0:ckc, :], kcv(cc, t0 + TT), reads=[cc], writes=[act], owner=act)
                post_norm_res(K, C, y, f"ffn_post_g{layer}", h, follow="copy", nT=nT)
                dbg_store("dbg_hffn", h, t0, tag)
                stats_begin(C, KC, 2)
                K.dma(K.pool, pt.ap[:], pT.ap[layer].rearrange("(k p) t -> p k t", p=128)[:, :, t0:t0 + TT], reads=[pT], writes=[pt], owner=pt)
                uslot = wpu
                uv = wview(wpu, 2, D)
                for nb in range(4):
                    slot = load_w(K, C, w_pg[layer], nb, KC, 512)
                    wv = wview(slot, KC, 512)
                    for ci in range(4):
                        c = nb * 4 + ci
                        psg = next_ps(C); pse = next_ps(C)
                        mm_group(K, C, psg, psg.ap[:], [(wv[:, kc, ci * 128:(ci + 1) * 128], nT.ap[:, kc, :]) for kc in range(KC)], reads=[slot, nT])
                        mm_group(K, C, pse, pse.ap[:], [(uv[:, k2, c * 128:(c + 1) * 128], pt.ap[:, k2, :]) for k2 in range(2)], reads=[uslot, pt])
                        tm = tmp[c % 2]
                        K.op(K.act, lambda: nc.scalar.activation(out=tm.ap[:], in_=psg.ap[:], func=AF.Sigmoid), reads=[psg], writes=[tm])
                        K.op(K.dve, lambda: nc.vector.tensor_tensor(out=y.ap[:, c, :], in0=tm.ap[:], in1=pse.ap[:], op=ALU.mult), reads=[tm, pse], writes=[y])
                        sq_push(K, C, y, y.ap[:, c, :], "dve")
                post_norm_res(K, C, y, f"ple_norm_g{layer}", h, follow=("norm" if inproj is not None else None))
                h_dst = hT if inproj is not None else outT
                K.dma(K.sp, kcv(h_dst, t0), h.ap[:], reads=[h], writes=[h_dst], owner=h)
            if inproj is not None:
                inproj(K, C, h, nT, t0, do_mix_ffn)
        K.barrier()
        K.release(C.wslots + C.stg32 + C.stg16 + [h, nT] + ([act, pt, wpu] if do_mix_ffn else []))
        ps_stack.close()

    def inproj_even(K, C, h, nT, t0, have_stats):
        norm_bf16(K, C, h, "mix_pre_g0", nT, have_stats=have_stats)
        def store(t, dst, c):
            K.dma(K.sp, dst.ap[c * 128:(c + 1) * 128, t0:t0 + TT], t.ap[:], reads=[t], writes=[dst], owner=t)
        def epi_q(c, ps):
            t = stage_out(C, BF16)
            K.op(K.act, lambda: nc.scalar.activation(out=t.ap[:], in_=ps.ap[:], func=AF.Copy, scale=128.0 ** -0.5), reads=[ps], writes=[t]); store(t, qT, c)
        def epi_k(c, ps):
            t = stage_out(C, BF16)
            K.op(K.act, lambda: nc.scalar.copy(out=t.ap[:], in_=ps.ap[:]), reads=[ps], writes=[t]); store(t, kT, c)
        def epi_hq(c, ps):
            t = stage_out(C, F32)
            K.op(K.act, lambda: nc.scalar.activation(out=t.ap[:], in_=ps.ap[:], func=AF.Silu), reads=[ps], writes=[t]); store(t, hqT, c)
        def epi_hg(c, ps):
            t = stage_out(C, BF16)
            K.op(K.act, lambda: nc.scalar.activation(out=t.ap[:], in_=ps.ap[:], func=AF.Silu), reads=[ps], writes=[t]); store(t, hgT, c)
        def epi_hf(c, ps):
            s = stage_out(C, F32); t1 = stage_out(C, F32); t2 = stage_out(C, F32)
            K.op(K.act, lambda: nc.scalar.activation(out=s.ap[:], in_=ps.ap[:], func=AF.Sigmoid), reads=[ps], writes=[s])
            K.op(K.act, lambda: nc.scalar.activation(out=t1.ap[:], in_=s.ap[:], func=AF.Ln, bias=C.drv.ap[:, c:c + 1], scale=C.drv.ap[:, 8 + c:9 + c]),
                 reads=[s, C.drv], writes=[t1]); store(t1, lfT, c)
            K.op(K.dve, lambda: nc.vector.tensor_scalar(out=t2.ap[:], in0=s.ap[:], scalar1=C.drv.ap[:, 16 + c:17 + c], scalar2=C.drv.ap[:, 8 + c:9 + c],
                                                        op0=ALU.mult, op1=ALU.add), reads=[s, C.drv], writes=[t2]); store(t2, kkT, c)
        linear_fm(K, C, nT, KC, w_sq, 1024, 512, epi_q)
        linear_fm(K, C, nT, KC, w_sk, 1024, 512, epi_k)
        linear_fm(K, C, nT, KC, w_hq, 1024, 512, epi_hq)
        linear_fm(K, C, nT, KC, w_hf, 1024, 512, epi_hf)
        linear_fm(K, C, nT, KC, w_hg, 1024, 512, epi_hg)
        for (wl, dst) in ((w_sv, vtok), (w_hi, hitok)):
            for nb in range(2):
                slot = load_w(K, C, wl, nb, KC, 512); wv = wview(slot, KC, 512)
                for tb in range(TT // 128):
                    ps = next_ps(C)
                    mm_group(K, C, ps, ps.ap[:], [(nT.ap[:, kc, tb * 128:(tb + 1) * 128], wv[:, kc, :]) for kc in range(KC)], reads=[slot, nT])
                    t = stage_out(C, BF16)
                    K.op(K.act, lambda: nc.scalar.copy(out=t.ap[:], in_=ps.ap[:]), reads=[ps], writes=[t])
                    K.dma(K.sp, dst.ap[nb * 4:(nb + 1) * 4, t0 + tb * 128:t0 + (tb + 1) * 128, :].rearrange("h t d -> t h d"),
                          t.ap[:].rearrange("t (h d) -> t h d", d=128), reads=[t], writes=[dst], owner=t)

    def inproj_odd(K, C, h, nT, t0, have_stats):
        norm_bf16(K, C, h, "mix_pre_g1", nT, have_stats=have_stats)
        def epi(c, ps):
            t = stage_out(C, F32)
            K.op(K.act, lambda: nc.scalar.copy(out=t.ap[:], in_=ps.ap[:]), reads=[ps], writes=[t])
            dst, cc = (gbT, c) if c < LC else (xbT, c - LC)
            K.dma(K.sp, dst.ap[cc * 128:(cc + 1) * 128, t0:t0 + TT], t.ap[:], reads=[t], writes=[dst], owner=t)
        linear_fm(K, C, nT, KC, w_io, 2 * LRU, 512, epi)

    def attention_phase():
        st = contextlib.ExitStack()
        NS = 4
        cm = {}
        for nm, dt in (("maskneg", BF16), ("negU", F32R), ("negOnes", F32R)):
            cm[nm] = K.sb(st, "c_" + nm + "_sb", [128, 128], dt)
            K.dma(K.pool, cm[nm].ap[:], cst[nm].ap[:, :], reads=[], writes=[cm[nm]], owner=cm[nm])
        zer = K.sb(st, "zer_sb", [128, 512], BF16)
        K.dma(K.pool, zer.ap[:], cst["zeros"].ap[:, :], reads=[], writes=[zer], owner=zer)
        zer32 = K.sb(st, "zer32_sb", [128, 512], F32)
        K.dma(K.sp, zer32.ap[:], cst["zeros"].ap[:, :], reads=[], writes=[zer32], owner=zer32)
        S = []
        for s in range(NS):
            o = Ctx()
            o.q = K.sb(st, f"aq{s}", [128, T], BF16); o.k = K.sb(st, f"ak{s}", [128, T], BF16); o.v = K.sb(st, f"av{s}", [128, 16, 128], BF16)
            o.e = K.sb(st, f"ae{s}", [128, 512], F32)
            o.sp = [K.sb(st, f"asp{s}_{i}", [128, 512], F32R) for i in range(2)]
            o.A = K.sb(st, f"aA{s}", [128, 512], F32R)
            o.w = [K.sb(st, f"aw{s}_{i}", [128, 512], BF16) for i in range(2)]
            o.ob = K.sb(st, f"aob{s}", [128, 512], BF16)
            o.zps = K.psum(st, f"azps{s}", [128, 512]); o.ops = K.psum(st, f"aops{s}", [128, 512])
            S.append(o)
        for hg in range(8 // NS):
            for s, o in enumerate(S):
                hd = hg * NS + s
                K.dma(K.sp, o.q.ap[:], qT.ap[hd * 128:(hd + 1) * 128, :], reads=[qT], writes=[o.q], owner=o.q)
                K.dma(K.sp, o.k.ap[:], kT.ap[hd * 128:(hd + 1) * 128, :], reads=[kT], writes=[o.k], owner=o.k)
                K.dma(K.sp, o.v.ap[:], vtok.ap[hd].rearrange("(sb p) d -> p sb d", p=128), reads=[vtok], writes=[o.v], owner=o.v)
            step = 0
            for tq in range(4):
                nblk = 4 * tq + 4
                for o in S:
                    K.op(K.pe, (lambda o=o: nc.tensor.matmul(o.ops.ap[:], zer.ap[:, 0:128], zer.ap[:], start=True, stop=False)), reads=[zer], writes=[o.ops], sig=False)
                    K.op(K.dve, (lambda o=o: nc.vector.tensor_copy(out=o.A.ap[:], in_=zer32.ap[:])), reads=[zer32], writes=[o.A])
                for bi in range(nblk):
                    sb = nblk - 1 - bi
                    c0 = max(0, 128 * sb - 512 * tq); diag = 128 * sb >= 512 * tq
                    q0 = 512 * tq + c0
                    par = step % 2; step += 1
                    for o in S:
                        K.op(K.pe, (lambda o=o: nc.tensor.matmul(o.zps.ap[:, c0:512], o.k.ap[:, sb * 128:(sb + 1) * 128], o.q.ap[:, q0:512 * tq + 512],
                                                                 start=True, stop=False)), reads=[o.k, o.q], writes=[o.zps], sig=not diag)
                        if diag:
                            K.op(K.pe, (lambda o=o: nc.tensor.matmul(o.zps.ap[:, c0:c0 + 128], C.ident.ap[:], cm["maskneg"].ap[:], start=False, stop=False)),
                                 reads=[C.ident, cm["maskneg"]], writes=[o.zps], sig=True)
                    for o in S:
                        K.op(K.act, (lambda o=o: nc.scalar.activation(out=o.e.ap[:, c0:512], in_=o.zps.ap[:, c0:512], func=AF.Exp)), reads=[o.zps], writes=[o.e])
                        K.op(K.act, (lambda o=o: nc.scalar.activation(out=o.sp[par].ap[:, c0:512], in_=o.e.ap[:, c0:512], func=AF.Ln, bias=1.0)),
                             reads=[o.e], writes=[o.sp[par]])
                    for o in S:
                        last = (bi == 0)
                        K.op(K.pe, (lambda o=o: nc.tensor.matmul(o.zps.ap[:, c0:512], cm["negU"].ap[:], o.sp[par].ap[:, c0:512], start=False, stop=last)),
                             reads=[cm["negU"], o.sp[par]], writes=[o.zps], sig=last)
                        if not last:
                            K.op(K.pe, (lambda o=o: nc.tensor.matmul(o.zps.ap[:, c0:512], cm["negOnes"].ap[:], o.A.ap[:, c0:512], start=False, stop=True)),
                                 reads=[cm["negOnes"], o.A], writes=[o.zps], sig=True)
                    for o in S:
                        K.op(K.act, (lambda o=o: nc.scalar.activation(out=o.w[par].ap[:, c0:512], in_=o.zps.ap[:, c0:512], func=AF.Exp)),
                             reads=[o.zps], writes=[o.w[par]])
                        if sb > 0:
                            K.op(K.dve, (lambda o=o: nc.vector.tensor_tensor(out=o.A.ap[:, c0:512], in0=o.A.ap[:, c0:512].bitcast(F32), in1=o.sp[par].ap[:, c0:512].bitcast(F32), op=ALU.add)),
                                 reads=[o.A, o.sp[par]], writes=[o.A])
                    for o in S:
                        K.op(K.pe, (lambda o=o: nc.tensor.matmul(o.ops.ap[:, c0:512], o.v.ap[:, sb, :], o.w[par].ap[:, c0:512], start=False, stop=(sb == 0))),
                             reads=[o.v, o.w[par]], writes=[o.ops], sig=(sb == 0))
                for s, o in enumerate(S):
                    hd = hg * NS + s
                    K.op(K.dve, (lambda o=o: nc.vector.tensor_copy(out=o.ob.ap[:], in_=o.ops.ap[:])), reads=[o.ops], writes=[o.ob])
                    K.dma(K.sp, ccT.ap[hd * 128:(hd + 1) * 128, tq * 512:(tq + 1) * 512], o.ob.ap[:], reads=[o.ob], writes=[ccT], owner=o.ob)
        K.barrier()
        K.release([o.q for o in S] + [o.k for o in S] + [o.v for o in S] + [o.ob for o in S] + list(cm.values()) + [zer, zer32])
        st.close()

    def hgrn2_phase():
        st = contextlib.ExitStack()
        rm128 = K.sb(st, "rm128", [128, T], F32); rm16 = K.sb(st, "rm16", [128, T], F32)
        m01 = K.sb(st, "m01", [128, 128], F32)
        K.dma(K.sp, rm128.ap[:], cst["rm128"].ap[:, :], reads=[], writes=[rm128], owner=rm128)
        K.dma(K.sp, rm16.ap[:], cst["rm16"].ap[:, :], reads=[], writes=[rm16], owner=rm16)
        K.dma(K.sp, m01.ap[:], cst["mask01"].ap[:, :], reads=[], writes=[m01], owner=m01)
        qs_ = [K.sb(st, f"gq{i}", [128, T], F32) for i in range(2)]; lfs_ = [K.sb(st, f"glf{i}", [128, T], F32) for i in range(2)]
        kks_ = [K.sb(st, f"gkk{i}", [128, T], F32) for i in range(2)]
        L = K.sb(st, "gL", [128, T], F32); L16 = K.sb(st, "gL16", [128, T], F32)
        Es = [K.sb(st, f"gE{i}", [128, T], F32) for i in range(2)]; Dts = [K.sb(st, f"gD{i}", [128, T], F32) for i in range(2)]
        Q0 = K.sb(st, "gQ0", [128, T], BF16); Kh = K.sb(st, "gKh", [128, T], BF16); Qs = K.sb(st, "gQs", [128, T], BF16)
        Ks = [K.sb(st, f"gKs{i}", [128, T], BF16) for i in range(2)]
        vis_ = [K.sb(st, f"gvi{i}", [128, 16, 128], BF16) for i in range(2)]; Kht = K.sb(st, "gKht", [128, 16, 128], BF16)
        scm = K.sb(st, "gscm", [128, 16, 128], BF16)
        oT = K.sb(st, "goT", [128, T], F32); hgss_ = [K.sb(st, f"ghgs{i}", [128, T], BF16) for i in range(2)]
        dec = K.sb(st, "gdec", [128, 16], F32)
        S32 = K.sb(st, "gS32", [128, 128], F32); Sb = K.sb(st, "gSb", [128, 128], BF16)
        sqb = K.sb(st, "gsq", [128, 512], BF16); rst = K.sb(st, "grst", [128, 512], F32); tmpn = K.sb(st, "gtmp", [128, 512], F32)
        bo = [K.sb(st, f"gbo{i}", [128, 512], BF16) for i in range(2)]
        scps = [K.psum(st, f"gsc{i}", [128, 512]) for i in range(4)]
        tps = [K.psum(st, f"gtp{i}", [128, 1024], BF16) for i in range(2)]
        ops = K.psum(st, "gops", [128, 512]); sps = K.psum(st, "gsps", [128, 512])
        v3 = lambda t: t.ap[:].rearrange("p (n c) -> p n c", c=128)
        def g_loads(hd):
            rows = slice(hd * 128, (hd + 1) * 128); b = hd % 2
            K.dma(K.sp, lfs_[b].ap[:], lfT.ap[rows, :], reads=[lfT], writes=[lfs_[b]], owner=lfs_[b])
            K.dma(K.sp, qs_[b].ap[:], hqT.ap[rows, :], reads=[hqT], writes=[qs_[b]], owner=qs_[b])
            K.dma(K.sp, kks_[b].ap[:], kkT.ap[rows, :], reads=[kkT], writes=[kks_[b]], owner=kks_[b])
            K.dma(K.sp, vis_[b].ap[:], hitok.ap[hd].rearrange("(sb p) d -> p sb d", p=128), reads=[hitok], writes=[vis_[b]], owner=vis_[b])
            K.dma(K.sp, hgss_[b].ap[:], hgT.ap[rows, :], reads=[hgT], writes=[hgss_[b]], owner=hgss_[b])
        g_loads(0)
        ej = [0]
        def nxtE():
            ej[0] += 1
            return Es[ej[0] % 2], Dts[ej[0] % 2]
        for hd in range(8):
            rows = slice(hd * 128, (hd + 1) * 128)
            q = qs_[hd % 2]; lf = lfs_[hd % 2]; kk = kks_[hd % 2]; vi = vis_[hd % 2]; hgs = hgss_[hd % 2]
            if hd + 1 < 8:
                g_loads(hd + 1)
            K.op(K.dve, lambda: nc.vector.tensor_tensor_scan(out=L.ap[:], data0=rm128.ap[:], data1=lf.ap[:], initial=0.0, op0=ALU.mult, op1=ALU.add),
                 reads=[rm128, lf], writes=[L])
            K.op(K.dve, lambda: nc.vector.tensor_tensor_scan(out=L16.ap[:], data0=rm16.ap[:], data1=lf.ap[:], initial=0.0, op0=ALU.mult, op1=ALU.add),
                 reads=[rm16, lf], writes=[L16])
            E, Dt = nxtE()
            K.op(K.act, lambda: nc.scalar.activation(out=E.ap[:], in_=L.ap[:], func=AF.Exp), reads=[L], writes=[E])
            K.op(K.dve, lambda: nc.vector.tensor_tensor(out=Q0.ap[:], in0=q.ap[:], in1=E.ap[:], op=ALU.mult), reads=[q, E], writes=[Q0])
            K.op(K.act, lambda: nc.scalar.activation(out=dec.ap[:], in_=v3(L)[:, :, 127], func=AF.Exp), reads=[L], writes=[dec])
            E, Dt = nxtE()
            K.op(K.dve, lambda: nc.vector.tensor_tensor(out=v3(Dt), in0=v3(L)[:, :, 127:128].to_broadcast([128, 16, 128]), in1=v3(L), op=ALU.subtract),
                 reads=[L], writes=[Dt])
            K.op(K.act, lambda: nc.scalar.activation(out=E.ap[:], in_=Dt.ap[:], func=AF.Exp), reads=[Dt], writes=[E])
            K.op(K.dve, lambda: nc.vector.tensor_tensor(out=Kh.ap[:], in0=kk.ap[:], in1=E.ap[:], op=ALU.mult), reads=[kk, E], writes=[Kh])
            E, Dt = nxtE()
            K.op(K.act, lambda: nc.scalar.activation(out=E.ap[:], in_=L16.ap[:], func=AF.Exp), reads=[L16], writes=[E])
            K.op(K.dve, lambda: nc.vector.tensor_tensor(out=Qs.ap[:], in0=q.ap[:], in1=E.ap[:], op=ALU.mult), reads=[q, E], writes=[Qs])
            for n in range(16):
                tp = tps[n // 8]
                K.op(K.pe, (lambda n=n, tp=tp: nc.tensor.transpose(tp.ap[:, (n % 8) * 128:(n % 8 + 1) * 128], Kh.ap[:, n * 128:(n + 1) * 128], C.ident.ap[:])),
                     reads=[Kh, C.ident], writes=[tp], sig=(n % 8 == 7))
            for i2 in range(2):
                K.op(K.act, (lambda i2=i2: nc.scalar.copy(out=Kht.ap[:, i2 * 8:(i2 + 1) * 8, :], in_=tps[i2].ap[:].rearrange("p (n c) -> p n c", c=128))),
                     reads=[tps[i2]], writes=[Kht])
            EB = [None] * 8
            def sub_exp(i):
                E, Dt = nxtE()
                EB[i] = E
                if i == 0:
                    K.op(K.dve, lambda: nc.vector.tensor_scalar(out=Dt.ap[:], in0=L.ap[:], scalar1=-1.0, scalar2=None, op0=ALU.mult), reads=[L], writes=[Dt])
                else:
                    K.op(K.dve, (lambda: nc.vector.tensor_tensor(out=v3(Dt), in0=v3(L)[:, :, 16 * i - 1:16 * i].to_broadcast([128, 16, 128]), in1=v3(L),
                                                                 op=ALU.subtract)), reads=[L], writes=[Dt])
                K.op(K.act, lambda: nc.scalar.activation(out=E.ap[:], in_=Dt.ap[:], func=AF.Exp), reads=[Dt], writes=[E])
            sub_exp(0)
            for i in range(8):
                ks = Ks[i % 2]
                if i + 1 < 8:
                    sub_exp(i + 1)
                E = EB[i]
                K.op(K.dve, (lambda ks=ks, E=E: nc.vector.scalar_tensor_tensor(out=ks.ap[:], in0=E.ap[:], scalar=1e30, in1=kk.ap[:], op0=ALU.min, op1=ALU.mult)),
                     reads=[E, kk], writes=[ks])
                for n in range(16):
                    sc = scps[n // 4]; cb0 = (n % 4) * 128 + 16 * i
                    K.op(K.pe, (lambda n=n, sc=sc, cb0=cb0, ks=ks, i=i: nc.tensor.matmul(sc.ap[:, cb0:cb0 + 16], ks.ap[:, n * 128:(n + 1) * 128],
                                                                                    Qs.ap[:, n * 128 + 16 * i:n * 128 + 16 * i + 16], start=True, stop=True)),
                         reads=[ks, Qs], writes=[sc], sig=(n == 15))
            for n4 in range(4):
                K.op(K.dve, (lambda n4=n4: nc.vector.tensor_tensor(out=scm.ap[:, n4 * 4:(n4 + 1) * 4, :], in0=scps[n4].ap[:].rearrange("p (n c) -> p n c", c=128),
                                                                   in1=m01.ap[:].rearrange("p (o c) -> p o c", o=1).to_broadcast([128, 4, 128]), op=ALU.mult)),
                     reads=[scps[n4], m01], writes=[scm])
            for n in range(16):
                oc = ops.ap[:, (n % 4) * 128:(n % 4 + 1) * 128]
                if n > 0:
                    K.op(K.pe, (lambda n=n, oc=oc: nc.tensor.matmul(oc, Sb.ap[:], Q0.ap[:, n * 128:(n + 1) * 128], start=True, stop=False)),
                         reads=[Sb, Q0], writes=[ops], sig=False)
                K.op(K.pe, (lambda n=n, oc=oc: nc.tensor.matmul(oc, vi.ap[:, n, :], scm.ap[:, n, :], start=(n == 0), stop=True)),
                     reads=[vi, scm], writes=[ops], sig=True)
                K.op(K.act, (lambda n=n, oc=oc: nc.scalar.copy(out=oT.ap[:, n * 128:(n + 1) * 128], in_=oc)), reads=[ops], writes=[oT])
                if n < 15:
                    sp_ = sps.ap[:, (n % 4) * 128:(n % 4 + 1) * 128]
                    K.op(K.pe, (lambda n=n, sp_=sp_: nc.tensor.matmul(sp_, Kht.ap[:, n, :], vi.ap[:, n, :], start=True, stop=True)),
                         reads=[Kht, vi], writes=[sps], sig=True)
                    if n == 0:
                        K.op(K.dve, (lambda sp_=sp_: nc.vector.tensor_copy(out=S32.ap[:], in_=sp_)), reads=[sps], writes=[S32])
                    else:
                        K.op(K.dve, (lambda n=n, sp_=sp_: nc.vector.scalar_tensor_tensor(out=S32.ap[:], in0=S32.ap[:], scalar=dec.ap[:, n:n + 1], in1=sp_,
                                                                                         op0=ALU.mult, op1=ALU.add)), reads=[S32, dec, sps], writes=[S32])
                    K.op(K.act, lambda: nc.scalar.copy(out=Sb.ap[:], in_=S32.ap[:]), reads=[S32], writes=[Sb])
            ohn, _ = PRM["hgn"]
            for tt in range(4):
                cs = slice(tt * 512, (tt + 1) * 512)
                K.op(K.act, (lambda cs=cs: nc.scalar.activation(out=sqb.ap[:], in_=oT.ap[:, cs], func=AF.Square)), reads=[oT], writes=[sqb])
                K.op(K.pe, lambda: nc.tensor.matmul(ops.ap[:], C.ones.ap[:], sqb.ap[:], start=True, stop=True), reads=[C.ones, sqb], writes=[ops], sig=True)
                K.op(K.act, lambda: nc.scalar.activation(out=rst.ap[:], in_=ops.ap[:], func=AF.Sqrt, bias=C.epsb.ap[:, 0:1], scale=1.0 / 128), reads=[ops, C.epsb], writes=[rst])
                K.op(K.dve, lambda: nc.vector.reciprocal(out=rst.ap[:], in_=rst.ap[:]), reads=[rst], writes=[rst])
                K.op(K.dve, (lambda cs=cs: nc.vector.scalar_tensor_tensor(out=tmpn.ap[:], in0=oT.ap[:, cs], scalar=C.prm.ap[:, ohn:ohn + 1], in1=rst.ap[:],
                                                                         op0=ALU.mult, op1=ALU.mult)), reads=[oT, rst, C.prm], writes=[tmpn])
                b = bo[tt % 2]
                K.op(K.dve, (lambda cs=cs, b=b: nc.vector.tensor_tensor(out=b.ap[:], in0=tmpn.ap[:], in1=hgs.ap[:, cs], op=ALU.mult)), reads=[tmpn, hgs], writes=[b])
                K.dma(K.sp, ccT.ap[1024 + hd * 128:1024 + (hd + 1) * 128, cs], b.ap[:], reads=[b], writes=[ccT], owner=b)
        K.barrier()
        K.release([rm128, rm16, m01] + qs_ + lfs_ + kks_ + vis_ + hgss_ + bo)
        st.close()

    def lru_phase():
        st = contextlib.ExitStack()
        xbs = [K.sb(st, f"rxb{i}", [128, 2, T], F32) for i in range(2)]; gbs = [K.sb(st, f"rgb{i}", [128, 2, T], F32) for i in range(2)]
        yc = K.sb(st, "ryc", [128, 2, T], F32); ycb = K.sb(st, "rycb", [128, 2, T], BF16)
        r = K.sb(st, "rr", [128, 2, T], F32); ig = K.sb(st, "rig", [128, 2, T], F32)
        a = K.sb(st, "ra", [128, 2, T], F32); mu = K.sb(st, "rmu", [128, 2, T], F32)
        gt = K.sb(st, "rgt", [128, 2, T], F32)
        mo = K.sb(st, "rmo", [128, 2, T], BF16)
        wa = [K.sb(st, f"rwa{i}", [128, 2, 256], BF16) for i in range(2)]; wx = [K.sb(st, f"rwx{i}", [128, 2, 256], BF16) for i in range(2)]
        pss = [K.psum(st, f"rps{i}", [128, 512]) for i in range(8)]
        pi = 0
        ocw = [PRM[f"convw{t}"][0] for t in range(4)]; ocb = PRM["convb"][0]; oba = PRM["ba"][0]; obx = PRM["bx"][0]
        NB = LRU // 256

        def loads(nb):
            rows = slice(nb * 256, (nb + 1) * 256)
            K.dma(K.pool, wa[nb % 2].ap[:], w_ra.ap[nb], reads=[w_ra], writes=[wa[nb % 2]], owner=wa[nb % 2])
            K.dma(K.pool, wx[nb % 2].ap[:], w_rx.ap[nb], reads=[w_rx], writes=[wx[nb % 2]], owner=wx[nb % 2])
            K.dma(K.sp, xbs[nb % 2].ap[:], xbT.ap[rows, :].rearrange("(k p) t -> p k t", p=128), reads=[xbT], writes=[xbs[nb % 2]], owner=xbs[nb % 2])
            K.dma(K.sp, gbs[nb % 2].ap[:], gbT.ap[rows, :].rearrange("(k p) t -> p k t", p=128), reads=[gbT], writes=[gbs[nb % 2]], owner=gbs[nb % 2])

        loads(0)
        for nb in range(NB):
            rows = slice(nb * 256, (nb + 1) * 256)
            W1 = wa[nb % 2]; W2 = wx[nb % 2]; xb = xbs[nb % 2]; gb = gbs[nb % 2]
            if nb + 1 < NB:
                loads(nb + 1)
            for k in range(2):
                K.op(K.pool, (lambda k=k: nc.gpsimd.tensor_tensor(out=gt.ap[:, k, :], in0=gb.ap[:, k, :], in1=gb.ap[:, k, :], op=ALU.mult)), reads=[gb], writes=[gt])
                K.op(K.pool, (lambda k=k: nc.gpsimd.tensor_scalar(out=gt.ap[:, k, :], in0=gt.ap[:, k, :], scalar1=0.044715, scalar2=1.0, op0=ALU.mult, op1=ALU.add)),
                     reads=[gt], writes=[gt])
                K.op(K.pool, (lambda k=k: nc.gpsimd.tensor_tensor(out=gt.ap[:, k, :], in0=gt.ap[:, k, :], in1=gb.ap[:, k, :], op=ALU.mult)), reads=[gt, gb], writes=[gt])
            for k in range(2):
                ch = nb * 2 + k
                K.op(K.act, (lambda k=k, ch=ch: nc.scalar.activation(out=yc.ap[:, k, :], in_=xb.ap[:, k, :], func=AF.Identity,
                                                                     scale=C.prm.ap[:, ocw[0] + ch:ocw[0] + ch + 1], bias=C.prm.ap[:, ocb + ch:ocb + ch + 1])),
                     reads=[xb, C.prm], writes=[yc])
                for tap in range(1, 4):
                    K.op(K.dve, (lambda k=k, ch=ch, tap=tap: nc.vector.scalar_tensor_tensor(out=yc.ap[:, k, tap:], in0=xb.ap[:, k, 0:T - tap],
                                                                                            scalar=C.prm.ap[:, ocw[tap] + ch:ocw[tap] + ch + 1], in1=yc.ap[:, k, tap:],
                                                                                            op0=ALU.mult, op1=ALU.add)), reads=[xb, yc, C.prm], writes=[yc])
                K.op(K.act, (lambda k=k: nc.scalar.copy(out=ycb.ap[:, k, :], in_=yc.ap[:, k, :])), reads=[yc], writes=[ycb])
            for k in range(2):
                ch = nb * 2 + k
                for tt in range(4):
                    cs = slice(tt * 512, (tt + 1) * 512)
                    for (W, dstt, ob) in ((W1, r, oba), (W2, ig, obx)):
                        ps = pss[pi % 8]; pi += 1
                        for ic in range(2):
                            K.op(K.pe, (lambda W=W, ps=ps, ic=ic, k=k, cs=cs: nc.tensor.matmul(ps.ap[:], W.ap[:, ic, k * 128:(k + 1) * 128], ycb.ap[:, ic, cs],
                                                                                           start=(ic == 0), stop=(ic == 1))), reads=[W, ycb], writes=[ps], sig=(ic == 1))
                        K.op(K.act, (lambda ps=ps, dstt=dstt, ob=ob, k=k, cs=cs, ch=ch: nc.scalar.activation(out=dstt.ap[:, k, cs], in_=ps.ap[:], func=AF.Sigmoid,
                                                                                                        bias=C.prm.ap[:, ob + ch:ob + ch + 1])),
                             reads=[ps, C.prm], writes=[dstt])
            for k in range(2):
                K.op(K.act, (lambda k=k: nc.scalar.activation(out=gt.ap[:, k, :], in_=gt.ap[:, k, :], func=AF.Sigmoid, scale=1.5957691216057308)), reads=[gt], writes=[gt])
            for k in range(2):
                K.op(K.pool, (lambda k=k: nc.gpsimd.tensor_tensor(out=gt.ap[:, k, :], in0=gt.ap[:, k, :], in1=gb.ap[:, k, :], op=ALU.mult)), reads=[gt, gb], writes=[gt])
            for k in range(2):
                scp = C.drv.ap[:, 24 + nb * 2 + k:25 + nb * 2 + k]
                K.op(K.act, (lambda k=k, scp=scp: nc.scalar.activation(out=a.ap[:, k, :], in_=r.ap[:, k, :], func=AF.Exp, scale=scp)), reads=[r, C.drv], writes=[a])
            for k in range(2):
                K.op(K.act, (lambda k=k: nc.scalar.activation(out=mu.ap[:, k, :], in_=a.ap[:, k, :], func=AF.Square)), reads=[a], writes=[mu])
            for k in range(2):
                K.op(K.dve, (lambda k=k: nc.vector.tensor_scalar(out=mu.ap[:, k, :], in0=mu.ap[:, k, :], scalar1=1.0, scalar2=-1.0, op0=ALU.min, op1=ALU.mult)), reads=[mu], writes=[mu])
            for k in range(2):
                K.op(K.act, (lambda k=k: nc.scalar.activation(out=mu.ap[:, k, :], in_=mu.ap[:, k, :], func=AF.Sqrt, bias=1.0, scale=1.0)), reads=[mu], writes=[mu])
            for k in range(2):
                K.op(K.dve, (lambda k=k: nc.vector.memset(mu.ap[:, k, 0:1], 1.0)), reads=[], writes=[mu])
                K.op(K.dve, (lambda k=k: nc.vector.tensor_tensor(out=mu.ap[:, k, :], in0=mu.ap[:, k, :], in1=ig.ap[:, k, :], op=ALU.mult)), reads=[mu, ig], writes=[mu])
                K.op(K.dve, (lambda k=k: nc.vector.tensor_tensor(out=mu.ap[:, k, :], in0=mu.ap[:, k, :], in1=yc.ap[:, k, :], op=ALU.mult)), reads=[mu, yc], writes=[mu])
            for k in range(2):
                K.op(K.dve, (lambda k=k: nc.vector.tensor_tensor_scan(out=r.ap[:, k, :], data0=a.ap[:, k, :], data1=mu.ap[:, k, :], initial=0.0, op0=ALU.mult, op1=ALU.add)),
                     reads=[a, mu], writes=[r])
            for k in range(2):
                K.op(K.pool, (lambda k=k: nc.gpsimd.tensor_tensor(out=mo.ap[:, k, :], in0=gt.ap[:, k, :], in1=r.ap[:, k, :], op=ALU.mult)), reads=[gt, r], writes=[mo])
            K.dma(K.sp, mxT.ap[rows, :].rearrange("(k p) t -> p k t", p=128), mo.ap[:], reads=[mo], writes=[mxT], owner=mo)
        K.barrier()
        K.release(xbs + gbs + [mo] + wa + wx)
        st.close()

    token_phase("p0", 0, False, inproj_even)
    if stage >= 2:
        attention_phase()
    if stage >= 3:
        hgrn2_phase()
    if stage >= 4:
        token_phase("p1", 0, True, inproj_odd)
    if stage >= 5:
        lru_phase()
    if stage >= 6:
        token_phase("p2", 1, True, None)
    K.barrier()
    gs.close()
    es.close()
    return nc


def host_inputs(inp):
    f = lambda a: np.ascontiguousarray(np.asarray(a, dtype=np.float32))
    sh = {}
    wi = f(inp["w_in_even"][0])
    names = ["w_sq", "w_sk", "w_sv", "w_hq", "w_hf", "w_hi", "w_hg"]
    for i, nm in enumerate(names):
        sh[nm] = _wl(wi[:, i * 1024:(i + 1) * 1024], 512)
    sh["w_oe"] = _wl(f(inp["w_out_even"][0]), 512)
    sh["w_io"] = _wl(f(inp["w_in_odd"][0]), 512)
    sh["w_oo"] = _wl(f(inp["w_out_odd"][0]), 256)
    for i in range(2):
        gu = f(inp["w_gate_up"][i])
        g = gu[:, :DFF].reshape(D, FC // 2, 256); u = gu[:, DFF:].reshape(D, FC // 2, 256)
        sh[f"w_gu{i}"] = _wl(np.concatenate([g, u], axis=2).reshape(D, FC * 256), 512)
        sh[f"w_dn{i}"] = _wl(f(inp["w_down"][i]), 128)
        sh[f"w_pg{i}"] = _wl(f(inp["w_ple_gate"][i]), 512)
        sh[f"w_pu{i}"] = _wl(f(inp["w_ple_up"][i]), 2048)
    for nm, key in (("w_ra", "rg_wa"), ("w_rx", "rg_wx")):
        w = f(inp[key][0])
        sh[nm] = np.ascontiguousarray(w.reshape(11, 2, 128, 256).transpose(0, 2, 1, 3))
    prm = np.zeros((128, NPRM), np.float32)
    def put(name, arr):
        off, n = PRM[name]; prm[:, off:off + n] = arr
    for i in range(2):
        for nm in ("mix_pre_g", "mix_post_g", "ffn_pre_g", "ffn_post_g", "ple_norm_g"):
            put(f"{nm}{i}", _cols(f(inp[nm][i])))
    put("lb0", _cols(f(inp["hg_lb_logits"][0]))); put("lb1", _cols(f(inp["hg_lb_logits"][1])))
    put("hgn", f(inp["hg_norm_g"][0]).reshape(128, 1))
    for tap in range(4):
        put(f"convw{tap}", _cols(f(inp["conv_w"][0, tap])))
    put("convb", _cols(f(inp["conv_b"][0]))); put("ba", _cols(f(inp["rg_ba"][0]).reshape(-1))); put("bx", _cols(f(inp["rg_bx"][0]).reshape(-1)))
    put("lam", _cols(f(inp["rg_lambda"][0])))
    sh["prm"] = prm
    for k, v in _consts().items():
        sh["c_" + k] = v
    return sh


def kernel(**inp):
    sh = host_inputs(inp)
    x = np.asarray(inp["x"], np.float32); p = np.asarray(inp["p"], np.float32)
    nc = build()
    in_maps = []
    for b in range(8):
        m = dict(sh)
        m["xT"] = np.ascontiguousarray(x[b].T)
        m["pT"] = np.ascontiguousarray(p[:, b].transpose(0, 2, 1))
        in_maps.append(m)
    res = run_bass_kernel_spmd(nc, in_maps, core_ids=list(range(8)))
    out = np.stack([np.ascontiguousarray(res.results[b]["outT"].T) for b in range(8)], axis=0)
    return out.astype(np.float32)
```

```python
import contextlib
import numpy as np
import concourse.bass as bass
import concourse.mybir as mybir
from concourse.bass_utils import run_bass_kernel_spmd

F32 = mybir.dt.float32
F32R = mybir.dt.float32r
BF16 = mybir.dt.bfloat16
AF = mybir.ActivationFunctionType
ALU = mybir.AluOpType

D = 2048
T = 2048
TT = 512
NTT = T // TT
KC = D // 128
DFF = 5632
FC = DFF // 128
LRU = 2816
LC = LRU // 128
PLE = 256
EPS = 1e-6
WSLOT = 8192


def _wl(w, cb):
    K, N = w.shape
    return np.ascontiguousarray(w.reshape(K // 128, 128, N // cb, cb).transpose(2, 1, 0, 3))


def _cols(v):
    return np.ascontiguousarray(v.reshape(-1, 128).T)


PRM = {}


def _prm_layout():
    off = 0
    def add(name, n):
        nonlocal off
        PRM[name] = (off, n)
        off += n
    for i in range(2):
        for nm in ("mix_pre_g", "mix_post_g", "ffn_pre_g", "ffn_post_g", "ple_norm_g"):
            add(f"{nm}{i}", 16)
    add("lb0", 8); add("lb1", 8); add("hgn", 1)
    for tap in range(4):
        add(f"convw{tap}", LC)
    add("convb", LC); add("ba", LC); add("bx", LC); add("lam", LC)
    return off


NPRM = _prm_layout()


def _consts():
    c = {}
    s = np.arange(128)[:, None]; t = np.arange(128)[None, :]
    c["ident"] = np.eye(128, dtype=np.float32)
    c["ones"] = np.ones((128, 128), np.float32)
    c["zeros"] = np.zeros((128, 512), np.float32)
    c["maskneg"] = np.where(s >= t, -30000.0, 0.0).astype(np.float32)
    c["mask01"] = (s <= t).astype(np.float32)
    c["negU"] = np.where(s >= t, -1.0, 0.0).astype(np.float32)
    c["negOnes"] = -np.ones((128, 128), np.float32)
    tt = np.arange(T)
    c["rm128"] = np.broadcast_to((tt % 128 != 0).astype(np.float32), (128, T)).copy()
    c["rm16"] = np.broadcast_to((tt % 16 != 0).astype(np.float32), (128, T)).copy()
    return c


class Tl:
    def __init__(self, name, ap):
        self.name = name; self.ap = ap
        self.w = {}; self.r = {}
        self.dsem = None; self.dcnt = 0
        self.dram = False

    def __getitem__(self, k):
        return self.ap[k]


class Eng:
    def __init__(self, name, eng, sem):
        self.name = name; self.eng = eng; self.sem = sem; self.cnt = 0
        self.waited = {}; self.pend = []


class Kern:
    def __init__(self, nc, es):
        self.nc = nc; self.es = es
        self.pe = Eng("pe", nc.tensor, es.enter_context(nc.semaphore("s_pe")))
        self.act = Eng("act", nc.scalar, es.enter_context(nc.semaphore("s_act")))
        self.dve = Eng("dve", nc.vector, es.enter_context(nc.semaphore("s_dve")))
        self.pool = Eng("pool", nc.gpsimd, es.enter_context(nc.semaphore("s_pool")))
        self.sp = Eng("sp", nc.sync, es.enter_context(nc.semaphore("s_sp")))
        self.engs = [self.pe, self.act, self.dve, self.pool, self.sp]
        self.dsems = []
        self.nsem = 5

    def sb(self, stack, name, shape, dt):
        return Tl(name, stack.enter_context(self.nc.sbuf_tensor(name, list(shape), dt)))

    def psum(self, stack, name, shape, dt=F32):
        return Tl(name, stack.enter_context(self.nc.psum_tensor(name, list(shape), dt)))

    def dram(self, name, shape, dt, kind="Internal"):
        t = Tl(name, self.nc.dram_tensor(name, list(shape), dt, kind=kind).ap())
        t.dram = True
        return t

    def _wait(self, E, ev, selfskip=False):
        for sid, (sem, val) in ev.items():
            if selfskip and sid == id(E.sem):
                continue
            if E is self.pe and sid == id(E.sem):
                continue
            if E.waited.get(sid, 0) < val:
                E.eng.wait_ge(sem, val)
                E.waited[sid] = val

    def _deps(self, E, reads, writes, selfskip=False):
        ev = {}
        def mrg(d):
            for sid, (sem, val) in d.items():
                if sid not in ev or ev[sid][1] < val:
                    ev[sid] = (sem, val)
        for t in reads:
            mrg(t.w)
        for t in writes:
            mrg(t.w); mrg(t.r)
        self._wait(E, ev, selfskip)

    def _record(self, ev, reads, writes):
        sid, sem, val = ev
        for t in writes:
            t.w = {sid: (sem, val)}; t.r = {}
        for t in reads:
            t.r[sid] = (sem, val)

    def op(self, E, fn, reads=(), writes=(), sig=True, selfskip=False, selfwait=None):
        self._deps(E, reads, writes, selfskip)
        if selfwait is not None and E.waited.get(id(E.sem), 0) < selfwait:
            E.eng.wait_ge(E.sem, selfwait)
            E.waited[id(E.sem)] = selfwait
        inst = fn()
        if sig:
            E.cnt += 1
            inst.then_inc(E.sem, 1)
            ev = (id(E.sem), E.sem, E.cnt)
            self._record(ev, reads, writes)
            for (r, w) in E.pend:
                self._record(ev, r, w)
            E.pend = []
        else:
            ev = (id(E.sem), E.sem, E.cnt + 1)
            self._record(ev, reads, writes)
        return inst

    def dma(self, Q, out_ap, in_ap, reads, writes, owner):
        reads = [t for t in reads if not t.dram]
        writes = [t for t in writes if not t.dram]
        self._deps(Q, reads, writes)
        if owner.dsem is None:
            owner.dsem = self.es.enter_context(self.nc.semaphore("d_" + owner.name))
            self.dsems.append(owner); self.nsem += 1
        owner.dcnt += 16
        Q.eng.dma_start(out=out_ap, in_=in_ap).then_inc(owner.dsem, 16)
        ev = (id(owner.dsem), owner.dsem, owner.dcnt)
        sid, sem, val = ev
        for t in writes:
            t.w = {sid: (sem, val)}; t.r = {}
        for t in reads:
            t.r[sid] = (sem, val)

    def barrier(self):
        ev = {}
        for E in self.engs:
            if E.cnt:
                ev[id(E.sem)] = (E.sem, E.cnt)
        for t in self.dsems:
            ev[id(t.dsem)] = (t.dsem, t.dcnt)
        for E in self.engs:
            self._wait(E, ev)

    def release(self, tiles):
        self.dsems = [t for t in self.dsems if t not in tiles]


class Ctx:
    pass


def wview(slot, kc, cb):
    return slot.ap[:, 0:kc * cb].rearrange("p (k c) -> p k c", c=cb)


def load_w(K, C, wl_dram, nb, kc, cb):
    slot = C.wslots[C.wi % len(C.wslots)]; C.wi += 1
    K.dma(K.pool, wview(slot, kc, cb), wl_dram.ap[nb], reads=[wl_dram], writes=[slot], owner=slot)
    return slot


def mm_group(K, C, ps_t, ps_ap, pairs, reads):
    n = len(pairs)
    for i, (l, r) in enumerate(pairs):
        K.op(K.pe, (lambda l=l, r=r, i=i: K.nc.tensor.matmul(ps_ap, l, r, start=(i == 0), stop=(i == n - 1))),
             reads=reads, writes=[ps_t], sig=(i == n - 1))


def next_ps(C):
    t = C.ps[C.pi % len(C.ps)]; C.pi += 1
    return t


def stats_begin(C, n, lag):
    C.statk = 0; C.statn = n; C.pend = []; C.lag = lag


def ones_mm(K, C):
    sq = C.pend.pop(0)
    k = C.statk; C.statk += 1
    K.op(K.pe, (lambda: K.nc.tensor.matmul(C.stat.ap[:], C.ones.ap[:], sq.ap[:], start=(k == 0), stop=(k == C.statn - 1))),
         reads=[sq, C.ones], writes=[C.stat], sig=True)


def sq_push(K, C, t, ap, eng):
    sq = C.sq[C.sqi % len(C.sq)]; C.sqi += 1
    if eng == "act":
        K.op(K.act, (lambda: K.nc.scalar.activation(out=sq.ap[:], in_=ap, func=AF.Square)), reads=[t], writes=[sq])
    else:
        K.op(K.dve, (lambda: K.nc.vector.tensor_tensor(out=sq.ap[:], in0=ap, in1=ap, op=ALU.mult)), reads=[t], writes=[sq])
    C.pend.append(sq)
    if len(C.pend) > C.lag:
        ones_mm(K, C)


def stats_finish(K, C, scale, out_rstd):
    while C.pend:
        ones_mm(K, C)
    assert C.statk == C.statn
    K.op(K.act, lambda: K.nc.scalar.activation(out=out_rstd.ap[:], in_=C.stat.ap[:], func=AF.Sqrt, bias=C.epsb.ap[:, 0:1], scale=scale),
         reads=[C.stat, C.epsb], writes=[out_rstd])
    K.op(K.dve, lambda: K.nc.vector.reciprocal(out=out_rstd.ap[:], in_=out_rstd.ap[:]), reads=[out_rstd], writes=[out_rstd])


def prm(C, name, j=None):
    off, n = PRM[name]
    if j is None:
        return C.prm.ap[:, off:off + n]
    return C.prm.ap[:, off + j:off + j + 1]


def norm_bf16(K, C, src, gname, dst, have_stats=False):
    if not have_stats:
        stats_begin(C, KC, 1)
        for kc in range(KC):
            sq_push(K, C, src, src.ap[:, kc, :], "act")
    stats_finish(K, C, 1.0 / D, C.rstd)
    for kc in range(KC):
        K.op(K.dve, (lambda kc=kc: K.nc.vector.scalar_tensor_tensor(out=dst.ap[:, kc, :], in0=src.ap[:, kc, :], scalar=prm(C, gname, kc),
                                                                   in1=C.rstd.ap[:], op0=ALU.mult, op1=ALU.mult)),
             reads=[src, C.rstd, C.prm], writes=[dst], selfskip=(kc > 0))


def post_norm_res(K, C, y, gname, h, follow=None, nT=None):
    stats_finish(K, C, 1.0 / D, C.rstd2)
    if follow == "norm":
        stats_begin(C, KC, 1)
    cnt_scale = {}

    def scale(kc):
        K.op(K.dve, (lambda: K.nc.vector.scalar_tensor_tensor(out=y.ap[:, kc, :], in0=y.ap[:, kc, :], scalar=prm(C, gname, kc),
                                                             in1=C.rstd2.ap[:], op0=ALU.mult, op1=ALU.mult)),
             reads=[y, C.rstd2, C.prm], writes=[y], selfskip=(kc > 0))
        cnt_scale[kc] = K.dve.cnt
    scale(0)
    for kc in range(KC):
        if kc + 1 < KC:
            scale(kc + 1)
        K.op(K.dve, (lambda kc=kc: K.nc.vector.tensor_tensor(out=h.ap[:, kc, :], in0=h.ap[:, kc, :], in1=y.ap[:, kc, :], op=ALU.add)),
             reads=[y, h], writes=[h], selfskip=True, selfwait=cnt_scale[kc])
        if follow == "norm":
            sq_push(K, C, h, h.ap[:, kc, :], "act")
        elif follow == "copy":
            K.op(K.act, (lambda kc=kc: K.nc.scalar.copy(out=nT.ap[:, kc, :], in_=h.ap[:, kc, :])), reads=[h], writes=[nT])


def linear_fm(K, C, inT, kcn, wl_dram, ncols, cb, epi):
    nblk = ncols // cb
    for nb in range(nblk):
        slot = load_w(K, C, wl_dram, nb, kcn, cb)
        wv = wview(slot, kcn, cb)
        for ci in range(cb // 128):
            ps = next_ps(C)
            mm_group(K, C, ps, ps.ap[:], [(wv[:, kc, ci * 128:(ci + 1) * 128], inT.ap[:, kc, :]) for kc in range(kcn)], reads=[slot, inT])
            epi(nb * (cb // 128) + ci, ps)


def stage_out(C, dt):
    lst = C.stg32 if dt == F32 else C.stg16
    t = lst[C.stgi[dt] % len(lst)]; C.stgi[dt] += 1
    return t


def build(stage=99, dbg=()):
    nc = bass.Bass("TRN2", target_bir_lowering=False)
    es = contextlib.ExitStack()
    K = Kern(nc, es)
    C = Ctx()
    dr = {}

    def ein(name, shape, dt=F32):
        dr[name] = K.dram(name, shape, dt, kind="ExternalInput"); return dr[name]

    def scr(name, shape, dt, out=False):
        dr[name] = K.dram(name, shape, dt, kind=("ExternalOutput" if (out or name in dbg) else "Internal")); return dr[name]

    xT = ein("xT", [D, T]); pT = ein("pT", [2, PLE, T]); prm_d = ein("prm", [128, NPRM])
    cst = {k: ein("c_" + k, list(v.shape)) for k, v in _consts().items()}
    w_sq = ein("w_sq", [2, 128, KC, 512]); w_sk = ein("w_sk", [2, 128, KC, 512]); w_sv = ein("w_sv", [2, 128, KC, 512])
    w_hq = ein("w_hq", [2, 128, KC, 512]); w_hf = ein("w_hf", [2, 128, KC, 512]); w_hi = ein("w_hi", [2, 128, KC, 512])
    w_hg = ein("w_hg", [2, 128, KC, 512])
    w_oe = ein("w_oe", [4, 128, KC, 512])
    w_io = ein("w_io", [2 * LRU // 512, 128, KC, 512])
    w_oo = ein("w_oo", [D // 256, 128, LC, 256])
    w_gu = [ein(f"w_gu{i}", [FC // 2, 128, KC, 512]) for i in range(2)]
    w_dn = [ein(f"w_dn{i}", [D // 128, 128, FC, 128]) for i in range(2)]
    w_pg = [ein(f"w_pg{i}", [4, 128, KC, 512]) for i in range(2)]
    w_pu = [ein(f"w_pu{i}", [1, 128, 2, 2048]) for i in range(2)]
    w_ra = ein("w_ra", [LRU // 256, 128, 2, 256]); w_rx = ein("w_rx", [LRU // 256, 128, 2, 256])

    outT = scr("outT", [D, T], F32, out=True)
    hT = scr("hT", [D, T], F32)
    qT = scr("qT", [1024, T], BF16); kT = scr("kT", [1024, T], BF16); vtok = scr("vtok", [8, T, 128], BF16)
    hqT = scr("hqT", [1024, T], F32); lfT = scr("lfT", [1024, T], F32); kkT = scr("kkT", [1024, T], F32)
    hgT = scr("hgT", [1024, T], BF16); hitok = scr("hitok", [8, T, 128], BF16)
    ccT = scr("ccT", [D, T], BF16)
    gbT = scr("gbT", [LRU, T], F32); xbT = scr("xbT", [LRU, T], F32)
    mxT = scr("mxT", [LRU, T], BF16)

    dbgt = {nm: scr(nm, [D, T], F32) for nm in ("dbg_m", "dbg_hmix", "dbg_hffn") if nm in dbg}
    def dbg_store(nm, tile, t0, tag):
        if nm in dbgt and tag == "p1":
            K.dma(K.sp, dbgt[nm].ap.rearrange("(k p) t -> p k t", p=128)[:, :, t0:t0 + TT], tile.ap[:], reads=[tile], writes=[dbgt[nm]], owner=tile)

    def kcv(d, t0):
        return d.ap.rearrange("(k p) t -> p k t", p=128)[:, :, t0:t0 + TT]

    gs = contextlib.ExitStack()
    C.prm = K.sb(gs, "prm_sb", [128, NPRM], F32)
    C.ones = K.sb(gs, "ones_sb", [128, 128], BF16)
    C.ident = K.sb(gs, "ident_sb", [128, 128], BF16)
    C.epsb = K.sb(gs, "epsb", [128, 1], F32)
    C.drv = K.sb(gs, "drv", [128, 64], F32)
    K.dma(K.sp, C.prm.ap[:], prm_d.ap[:, :], reads=[prm_d], writes=[C.prm], owner=C.prm)
    K.dma(K.pool, C.ones.ap[:], cst["ones"].ap[:, :], reads=[], writes=[C.ones], owner=C.ones)
    K.dma(K.pool, C.ident.ap[:], cst["ident"].ap[:, :], reads=[], writes=[C.ident], owner=C.ident)
    K.op(K.dve, lambda: nc.vector.memset(C.epsb.ap[:], EPS), writes=[C.epsb])
    o0, _ = PRM["lb0"]; o1, _ = PRM["lb1"]
    K.op(K.dve, lambda: nc.vector.tensor_tensor(out=C.drv.ap[:, 0:8], in0=C.prm.ap[:, o0:o0 + 8], in1=C.prm.ap[:, o1:o1 + 8], op=ALU.subtract),
         reads=[C.prm], writes=[C.drv])
    K.op(K.act, lambda: nc.scalar.activation(out=C.drv.ap[:, 0:8], in_=C.drv.ap[:, 0:8], func=AF.Sigmoid), reads=[C.drv], writes=[C.drv])
    K.op(K.dve, lambda: nc.vector.tensor_scalar(out=C.drv.ap[:, 8:16], in0=C.drv.ap[:, 0:8], scalar1=-1.0, scalar2=1.0, op0=ALU.mult, op1=ALU.add),
         reads=[C.drv], writes=[C.drv])
    K.op(K.dve, lambda: nc.vector.tensor_scalar(out=C.drv.ap[:, 16:24], in0=C.drv.ap[:, 8:16], scalar1=-1.0, scalar2=None, op0=ALU.mult),
         reads=[C.drv], writes=[C.drv])
    ol, _ = PRM["lam"]
    K.op(K.act, lambda: nc.scalar.activation(out=C.drv.ap[:, 24:46], in_=C.prm.ap[:, ol:ol + LC], func=AF.Exp, scale=-1.0), reads=[C.prm], writes=[C.drv])
    K.op(K.act, lambda: nc.scalar.activation(out=C.drv.ap[:, 24:46], in_=C.drv.ap[:, 24:46], func=AF.Ln, bias=1.0), reads=[C.drv], writes=[C.drv])
    K.op(K.dve, lambda: nc.vector.tensor_scalar(out=C.drv.ap[:, 24:46], in0=C.drv.ap[:, 24:46], scalar1=-8.0, scalar2=None, op0=ALU.mult),
         reads=[C.drv], writes=[C.drv])

    def token_phase(tag, layer, do_mix_ffn, inproj):
        ps_stack = contextlib.ExitStack()
        C.wslots = [K.sb(ps_stack, f"ws{i}_{tag}", [128, WSLOT], BF16) for i in range(3)]; C.wi = 0
        C.ps = [K.psum(ps_stack, f"ps{i}_{tag}", [128, TT]) for i in range(7)]; C.pi = 0
        C.stat = K.psum(ps_stack, f"stat_{tag}", [128, TT])
        C.sq = [K.sb(ps_stack, f"sq{i}_{tag}", [128, TT], BF16) for i in range(4)]; C.sqi = 0
        C.rstd = K.sb(ps_stack, f"rstd_{tag}", [128, TT], F32)
        C.rstd2 = K.sb(ps_stack, f"rstd2_{tag}", [128, TT], F32)
        nst = 3 if do_mix_ffn else 9
        C.stg32 = [K.sb(ps_stack, f"st32_{i}_{tag}", [128, TT], F32) for i in range(nst)]
        C.stg16 = [K.sb(ps_stack, f"st16_{i}_{tag}", [128, TT], BF16) for i in range(nst)]
        C.stgi = {F32: 0, BF16: 0}
        h = K.sb(ps_stack, f"h_{tag}", [128, KC, TT], F32)
        nT = K.sb(ps_stack, f"nT_{tag}", [128, KC, TT], BF16)
        if do_mix_ffn:
            act = K.sb(ps_stack, f"act_{tag}", [128, FC, TT], BF16)
            y = K.sb(ps_stack, f"y_{tag}", [128, KC, TT], F32)
            pt = K.sb(ps_stack, f"pt_{tag}", [128, 2, TT], BF16)
            tmp = [K.sb(ps_stack, f"tmp{i}_{tag}", [128, TT], F32) for i in range(2)]
            wpu = K.sb(ps_stack, f"wpu_{tag}", [128, 2 * D], BF16)
            K.dma(K.pool, wview(wpu, 2, D), w_pu[layer].ap[0], reads=[w_pu[layer]], writes=[wpu], owner=wpu)
        h_src = xT if layer == 0 else hT
        for tt in range(NTT):
            t0 = tt * TT
            K.dma(K.sp, h.ap[:], kcv(h_src, t0), reads=[h_src], writes=[h], owner=h)
            if do_mix_ffn:
                if layer == 0:
                    cc, ckc, wo, wcb = ccT, KC, w_oe, 512
                else:
                    cc, ckc, wo, wcb = mxT, LC, w_oo, 256
                if tt == 0:
                    K.dma(K.sp, act.ap[:, 0:ckc, :], kcv(cc, t0), reads=[cc], writes=[act], owner=act)
                def epi_y(c, ps):
                    K.op(K.act, lambda: nc.scalar.copy(out=y.ap[:, c, :], in_=ps.ap[:]), reads=[ps], writes=[y])
                    sq_push(K, C, y, y.ap[:, c, :], "dve")
                stats_begin(C, KC, 2)
                linear_fm(K, C, act, ckc, wo, D, wcb, epi_y)
                dbg_store("dbg_m", y, t0, tag)
                post_norm_res(K, C, y, f"mix_post_g{layer}", h, follow="norm")
                dbg_store("dbg_hmix", h, t0, tag)
                norm_bf16(K, C, h, f"ffn_pre_g{layer}", nT, have_stats=True)
                for nb in range(FC // 2):
                    slot = load_w(K, C, w_gu[layer], nb, KC, 512)
                    wv = wview(slot, KC, 512)
                    for ci in range(2):
                        psg = next_ps(C); psu = next_ps(C)
                        mm_group(K, C, psg, psg.ap[:], [(wv[:, kc, ci * 128:(ci + 1) * 128], nT.ap[:, kc, :]) for kc in range(KC)], reads=[slot, nT])
                        mm_group(K, C, psu, psu.ap[:], [(wv[:, kc, 256 + ci * 128:256 + (ci + 1) * 128], nT.ap[:, kc, :]) for kc in range(KC)], reads=[slot, nT])
                        tm = tmp[(nb * 2 + ci) % 2]; fc = nb * 2 + ci
                        K.op(K.act, lambda: nc.scalar.activation(out=tm.ap[:], in_=psg.ap[:], func=AF.Silu), reads=[psg], writes=[tm])
                        K.op(K.dve, lambda: nc.vector.tensor_tensor(out=act.ap[:, fc, :], in0=tm.ap[:], in1=psu.ap[:], op=ALU.mult), reads=[tm, psu], writes=[act])
                stats_begin(C, KC, 2)
                linear_fm(K, C, act, FC, w_dn[layer], D, 128, epi_y)
                if tt + 1 < NTT:
                    K.dma(K.sp, act.ap[:, 0:ckc, :], kcv(cc, t0 + TT), reads=[cc], writes=[act], owner=act)
                post_norm_res(K, C, y, f"ffn_post_g{layer}", h, follow="copy", nT=nT)
                dbg_store("dbg_hffn", h, t0, tag)
                stats_begin(C, KC, 2)
                K.dma(K.pool, pt.ap[:], pT.ap[layer].rearrange("(k p) t -> p k t", p=128)[:, :, t0:t0 + TT], reads=[pT], writes=[pt], owner=pt)
                uslot = wpu
                uv = wview(wpu, 2, D)
                for nb in range(4):
                    slot = load_w(K, C, w_pg[layer], nb, KC, 512)
                    wv = wview(slot, KC, 512)
                    for ci in range(4):
                        c = nb * 4 + ci
                        psg = next_ps(C); pse = next_ps(C)
                        mm_group(K, C, psg, psg.ap[:], [(wv[:, kc, ci * 128:(ci + 1) * 128], nT.ap[:, kc, :]) for kc in range(KC)], reads=[slot, nT])
                        mm_group(K, C, pse, pse.ap[:], [(uv[:, k2, c * 128:(c + 1) * 128], pt.ap[:, k2, :]) for k2 in range(2)], reads=[uslot, pt])
                        tm = tmp[c % 2]
                        K.op(K.act, lambda: nc.scalar.activation(out=tm.ap[:], in_=psg.ap[:], func=AF.Sigmoid), reads=[psg], writes=[tm])
                        K.op(K.dve, lambda: nc.vector.tensor_tensor(out=y.ap[:, c, :], in0=tm.ap[:], in1=pse.ap[:], op=ALU.mult), reads=[tm, pse], writes=[y])
                        sq_push(K, C, y, y.ap[:, c, :], "dve")
                post_norm_res(K, C, y, f"ple_norm_g{layer}", h, follow=("norm" if inproj is not None else None))
                h_dst = hT if inproj is not None else outT
                K.dma(K.sp, kcv(h_dst, t0), h.ap[:], reads=[h], writes=[h_dst], owner=h)
            if inproj is not None:
                inproj(K, C, h, nT, t0, do_mix_ffn)
        K.barrier()
        K.release(C.wslots + C.stg32 + C.stg16 + [h, nT] + ([act, pt, wpu] if do_mix_ffn else []))
        ps_stack.close()

    def inproj_even(K, C, h, nT, t0, have_stats):
        norm_bf16(K, C, h, "mix_pre_g0", nT, have_stats=have_stats)
        def store(t, dst, c):
            K.dma(K.sp, dst.ap[c * 128:(c + 1) * 128, t0:t0 + TT], t.ap[:], reads=[t], writes=[dst], owner=t)
        def epi_q(c, ps):
            t = stage_out(C, BF16)
            K.op(K.act, lambda: nc.scalar.activation(out=t.ap[:], in_=ps.ap[:], func=AF.Copy, scale=128.0 ** -0.5), reads=[ps], writes=[t]); store(t, qT, c)
        def epi_k(c, ps):
            t = stage_out(C, BF16)
            K.op(K.act, lambda: nc.scalar.copy(out=t.ap[:], in_=ps.ap[:]), reads=[ps], writes=[t]); store(t, kT, c)
        def epi_hq(c, ps):
            t = stage_out(C, F32)
            K.op(K.act, lambda: nc.scalar.activation(out=t.ap[:], in_=ps.ap[:], func=AF.Silu), reads=[ps], writes=[t]); store(t, hqT, c)
        def epi_hg(c, ps):
            t = stage_out(C, BF16)
            K.op(K.act, lambda: nc.scalar.activation(out=t.ap[:], in_=ps.ap[:], func=AF.Silu), reads=[ps], writes=[t]); store(t, hgT, c)
        def epi_hf(c, ps):
            s = stage_out(C, F32); t1 = stage_out(C, F32); t2 = stage_out(C, F32)
            K.op(K.act, lambda: nc.scalar.activation(out=s.ap[:], in_=ps.ap[:], func=AF.Sigmoid), reads=[ps], writes=[s])
            K.op(K.act, lambda: nc.scalar.activation(out=t1.ap[:], in_=s.ap[:], func=AF.Ln, bias=C.drv.ap[:, c:c + 1], scale=C.drv.ap[:, 8 + c:9 + c]),
                 reads=[s, C.drv], writes=[t1]); store(t1, lfT, c)
            K.op(K.dve, lambda: nc.vector.tensor_scalar(out=t2.ap[:], in0=s.ap[:], scalar1=C.drv.ap[:, 16 + c:17 + c], scalar2=C.drv.ap[:, 8 + c:9 + c],
                                                        op0=ALU.mult, op1=ALU.add), reads=[s, C.drv], writes=[t2]); store(t2, kkT, c)
        linear_fm(K, C, nT, KC, w_sq, 1024, 512, epi_q)
        linear_fm(K, C, nT, KC, w_sk, 1024, 512, epi_k)
        linear_fm(K, C, nT, KC, w_hq, 1024, 512, epi_hq)
        linear_fm(K, C, nT, KC, w_hf, 1024, 512, epi_hf)
        linear_fm(K, C, nT, KC, w_hg, 1024, 512, epi_hg)
        for (wl, dst) in ((w_sv, vtok), (w_hi, hitok)):
            for nb in range(2):
                slot = load_w(K, C, wl, nb, KC, 512); wv = wview(slot, KC, 512)
                for tb in range(TT // 128):
                    ps = next_ps(C)
                    mm_group(K, C, ps, ps.ap[:], [(nT.ap[:, kc, tb * 128:(tb + 1) * 128], wv[:, kc, :]) for kc in range(KC)], reads=[slot, nT])
                    t = stage_out(C, BF16)
                    K.op(K.act, lambda: nc.scalar.copy(out=t.ap[:], in_=ps.ap[:]), reads=[ps], writes=[t])
                    K.dma(K.sp, dst.ap[nb * 4:(nb + 1) * 4, t0 + tb * 128:t0 + (tb + 1) * 128, :].rearrange("h t d -> t h d"),
                          t.ap[:].rearrange("t (h d) -> t h d", d=128), reads=[t], writes=[dst], owner=t)

    def inproj_odd(K, C, h, nT, t0, have_stats):
        norm_bf16(K, C, h, "mix_pre_g1", nT, have_stats=have_stats)
        def epi(c, ps):
            t = stage_out(C, F32)
            K.op(K.act, lambda: nc.scalar.copy(out=t.ap[:], in_=ps.ap[:]), reads=[ps], writes=[t])
            dst, cc = (gbT, c) if c < LC else (xbT, c - LC)
            K.dma(K.sp, dst.ap[cc * 128:(cc + 1) * 128, t0:t0 + TT], t.ap[:], reads=[t], writes=[dst], owner=t)
        linear_fm(K, C, nT, KC, w_io, 2 * LRU, 512, epi)

    def attention_phase():
        st = contextlib.ExitStack()
        NS = 4
        cm = {}
        for nm, dt in (("maskneg", BF16), ("negU", F32R), ("negOnes", F32R)):
            cm[nm] = K.sb(st, "c_" + nm + "_sb", [128, 128], dt)
            K.dma(K.pool, cm[nm].ap[:], cst[nm].ap[:, :], reads=[], writes=[cm[nm]], owner=cm[nm])
        zer = K.sb(st, "zer_sb", [128, 512], BF16)
        K.dma(K.pool, zer.ap[:], cst["zeros"].ap[:, :], reads=[], writes=[zer], owner=zer)
        zer32 = K.sb(st, "zer32_sb", [128, 512], F32)
        K.dma(K.sp, zer32.ap[:], cst["zeros"].ap[:, :], reads=[], writes=[zer32], owner=zer32)
        S = []
        for s in range(NS):
            o = Ctx()
            o.q = K.sb(st, f"aq{s}", [128, T], BF16); o.k = K.sb(st, f"ak{s}", [128, T], BF16); o.v = K.sb(st, f"av{s}", [128, 16, 128], BF16)
            o.e = K.sb(st, f"ae{s}", [128, 512], F32)
            o.sp = [K.sb(st, f"asp{s}_{i}", [128, 512], F32R) for i in range(2)]
            o.A = K.sb(st, f"aA{s}", [128, 512], F32R)
            o.w = [K.sb(st, f"aw{s}_{i}", [128, 512], BF16) for i in range(2)]
            o.ob = K.sb(st, f"aob{s}", [128, 512], BF16)
            o.zps = K.psum(st, f"azps{s}", [128, 512]); o.ops = K.psum(st, f"aops{s}", [128, 512])
            S.append(o)
        for hg in range(8 // NS):
            for s, o in enumerate(S):
                hd = hg * NS + s
                K.dma(K.sp, o.q.ap[:], qT.ap[hd * 128:(hd + 1) * 128, :], reads=[qT], writes=[o.q], owner=o.q)
                K.dma(K.sp, o.k.ap[:], kT.ap[hd * 128:(hd + 1) * 128, :], reads=[kT], writes=[o.k], owner=o.k)
                K.dma(K.sp, o.v.ap[:], vtok.ap[hd].rearrange("(sb p) d -> p sb d", p=128), reads=[vtok], writes=[o.v], owner=o.v)
            step = 0
            for tq in range(4):
                nblk = 4 * tq + 4
                for o in S:
                    K.op(K.pe, (lambda o=o: nc.tensor.matmul(o.ops.ap[:], zer.ap[:, 0:128], zer.ap[:], start=True, stop=False)), reads=[zer], writes=[o.ops], sig=False)
                    K.op(K.dve, (lambda o=o: nc.vector.tensor_copy(out=o.A.ap[:], in_=zer32.ap[:])), reads=[zer32], writes=[o.A])
                for bi in range(nblk):
                    sb = nblk - 1 - bi
                    c0 = max(0, 128 * sb - 512 * tq); diag = 128 * sb >= 512 * tq
                    q0 = 512 * tq + c0
                    par = step % 2; step += 1
                    for o in S:
                        K.op(K.pe, (lambda o=o: nc.tensor.matmul(o.zps.ap[:, c0:512], o.k.ap[:, sb * 128:(sb + 1) * 128], o.q.ap[:, q0:512 * tq + 512],
                                                                 start=True, stop=False)), reads=[o.k, o.q], writes=[o.zps], sig=not diag)
                        if diag:
                            K.op(K.pe, (lambda o=o: nc.tensor.matmul(o.zps.ap[:, c0:c0 + 128], C.ident.ap[:], cm["maskneg"].ap[:], start=False, stop=False)),
                                 reads=[C.ident, cm["maskneg"]], writes=[o.zps], sig=True)
                    for o in S:
                        K.op(K.act, (lambda o=o: nc.scalar.activation(out=o.e.ap[:, c0:512], in_=o.zps.ap[:, c0:512], func=AF.Exp)), reads=[o.zps], writes=[o.e])
                        K.op(K.act, (lambda o=o: nc.scalar.activation(out=o.sp[par].ap[:, c0:512], in_=o.e.ap[:, c0:512], func=AF.Ln, bias=1.0)),
                             reads=[o.e], writes=[o.sp[par]])
                    for o in S:
                        last = (bi == 0)
                        K.op(K.pe, (lambda o=o: nc.tensor.matmul(o.zps.ap[:, c0:512], cm["negU"].ap[:], o.sp[par].ap[:, c0:512], start=False, stop=last)),
                             reads=[cm["negU"], o.sp[par]], writes=[o.zps], sig=last)
                        if not last:
                            K.op(K.pe, (lambda o=o: nc.tensor.matmul(o.zps.ap[:, c0:512], cm["negOnes"].ap[:], o.A.ap[:, c0:512], start=False, stop=True)),
                                 reads=[cm["negOnes"], o.A], writes=[o.zps], sig=True)
                    for o in S:
                        K.op(K.act, (lambda o=o: nc.scalar.activation(out=o.w[par].ap[:, c0:512], in_=o.zps.ap[:, c0:512], func=AF.Exp)),
                             reads=[o.zps], writes=[o.w[par]])
                        if sb > 0:
                            K.op(K.dve, (lambda o=o: nc.vector.tensor_tensor(out=o.A.ap[:, c0:512], in0=o.A.ap[:, c0:512].bitcast(F32), in1=o.sp[par].ap[:, c0:512].bitcast(F32), op=ALU.add)),
                                 reads=[o.A, o.sp[par]], writes=[o.A])
                    for o in S:
                        K.op(K.pe, (lambda o=o: nc.tensor.matmul(o.ops.ap[:, c0:512], o.v.ap[:, sb, :], o.w[par].ap[:, c0:512], start=False, stop=(sb == 0))),
                             reads=[o.v, o.w[par]], writes=[o.ops], sig=(sb == 0))
                for s, o in enumerate(S):
                    hd = hg * NS + s
                    K.op(K.dve, (lambda o=o: nc.vector.tensor_copy(out=o.ob.ap[:], in_=o.ops.ap[:])), reads=[o.ops], writes=[o.ob])
                    K.dma(K.sp, ccT.ap[hd * 128:(hd + 1) * 128, tq * 512:(tq + 1) * 512], o.ob.ap[:], reads=[o.ob], writes=[ccT], owner=o.ob)
        K.barrier()
        K.release([o.q for o in S] + [o.k for o in S] + [o.v for o in S] + [o.ob for o in S] + list(cm.values()) + [zer, zer32])
        st.close()

    def hgrn2_phase():
        st = contextlib.ExitStack()
        rm128 = K.sb(st, "rm128", [128, T], F32); rm16 = K.sb(st, "rm16", [128, T], F32)
        m01 = K.sb(st, "m01", [128, 128], F32)
        K.dma(K.sp, rm128.ap[:], cst["rm128"].ap[:, :], reads=[], writes=[rm128], owner=rm128)
        K.dma(K.sp, rm16.ap[:], cst["rm16"].ap[:, :], reads=[], writes=[rm16], owner=rm16)
        K.dma(K.sp, m01.ap[:], cst["mask01"].ap[:, :], reads=[], writes=[m01], owner=m01)
        qs_ = [K.sb(st, f"gq{i}", [128, T], F32) for i in range(2)]; lfs_ = [K.sb(st, f"glf{i}", [128, T], F32) for i in range(2)]
        kks_ = [K.sb(st, f"gkk{i}", [128, T], F32) for i in range(2)]
        L = K.sb(st, "gL", [128, T], F32); L16 = K.sb(st, "gL16", [128, T], F32)
        Es = [K.sb(st, f"gE{i}", [128, T], F32) for i in range(2)]; Dts = [K.sb(st, f"gD{i}", [128, T], F32) for i in range(2)]
        Q0 = K.sb(st, "gQ0", [128, T], BF16); Kh = K.sb(st, "gKh", [128, T], BF16); Qs = K.sb(st, "gQs", [128, T], BF16)
        Ks = [K.sb(st, f"gKs{i}", [128, T], BF16) for i in range(2)]
        vis_ = [K.sb(st, f"gvi{i}", [128, 16, 128], BF16) for i in range(2)]; Kht = K.sb(st, "gKht", [128, 16, 128], BF16)
        scm = K.sb(st, "gscm", [128, 16, 128], BF16)
        oT = K.sb(st, "goT", [128, T], F32); hgss_ = [K.sb(st, f"ghgs{i}", [128, T], BF16) for i in range(2)]
        dec = K.sb(st, "gdec", [128, 16], F32)
        S32 = K.sb(st, "gS32", [128, 128], F32); Sb = K.sb(st, "gSb", [128, 128], BF16)
        sqb = K.sb(st, "gsq", [128, 512], BF16); rst = K.sb(st, "grst", [128, 512], F32); tmpn = K.sb(st, "gtmp", [128, 512], F32)
        bo = [K.sb(st, f"gbo{i}", [128, 512], BF16) for i in range(2)]
        scps = [K.psum(st, f"gsc{i}", [128, 512]) for i in range(4)]
        tps = [K.psum(st, f"gtp{i}", [128, 1024], BF16) for i in range(2)]
        ops = K.psum(st, "gops", [128, 512]); sps = K.psum(st, "gsps", [128, 512])
        v3 = lambda t: t.ap[:].rearrange("p (n c) -> p n c", c=128)
        def g_loads(hd):
            rows = slice(hd * 128, (hd + 1) * 128); b = hd % 2
            K.dma(K.sp, lfs_[b].ap[:], lfT.ap[rows, :], reads=[lfT], writes=[lfs_[b]], owner=lfs_[b])
            K.dma(K.sp, qs_[b].ap[:], hqT.ap[rows, :], reads=[hqT], writes=[qs_[b]], owner=qs_[b])
            K.dma(K.sp, kks_[b].ap[:], kkT.ap[rows, :], reads=[kkT], writes=[kks_[b]], owner=kks_[b])
            K.dma(K.sp, vis_[b].ap[:], hitok.ap[hd].rearrange("(sb p) d -> p sb d", p=128), reads=[hitok], writes=[vis_[b]], owner=vis_[b])
            K.dma(K.sp, hgss_[b].ap[:], hgT.ap[rows, :], reads=[hgT], writes=[hgss_[b]], owner=hgss_[b])
        g_loads(0)
        ej = [0]
        def nxtE():
            ej[0] += 1
            return Es[ej[0] % 2], Dts[ej[0] % 2]
        for hd in range(8):
            rows = slice(hd * 128, (hd + 1) * 128)
            q = qs_[hd % 2]; lf = lfs_[hd % 2]; kk = kks_[hd % 2]; vi = vis_[hd % 2]; hgs = hgss_[hd % 2]
            if hd + 1 < 8:
                g_loads(hd + 1)
            K.op(K.dve, lambda: nc.vector.tensor_tensor_scan(out=L.ap[:], data0=rm128.ap[:], data1=lf.ap[:], initial=0.0, op0=ALU.mult, op1=ALU.add),
                 reads=[rm128, lf], writes=[L])
            K.op(K.dve, lambda: nc.vector.tensor_tensor_scan(out=L16.ap[:], data0=rm16.ap[:], data1=lf.ap[:], initial=0.0, op0=ALU.mult, op1=ALU.add),
                 reads=[rm16, lf], writes=[L16])
            E, Dt = nxtE()
            K.op(K.act, lambda: nc.scalar.activation(out=E.ap[:], in_=L.ap[:], func=AF.Exp), reads=[L], writes=[E])
            K.op(K.dve, lambda: nc.vector.tensor_tensor(out=Q0.ap[:], in0=q.ap[:], in1=E.ap[:], op=ALU.mult), reads=[q, E], writes=[Q0])
            K.op(K.act, lambda: nc.scalar.activation(out=dec.ap[:], in_=v3(L)[:, :, 127], func=AF.Exp), reads=[L], writes=[dec])
            E, Dt = nxtE()
            K.op(K.dve, lambda: nc.vector.tensor_tensor(out=v3(Dt), in0=v3(L)[:, :, 127:128].to_broadcast([128, 16, 128]), in1=v3(L), op=ALU.subtract),
                 reads=[L], writes=[Dt])
            K.op(K.act, lambda: nc.scalar.activation(out=E.ap[:], in_=Dt.ap[:], func=AF.Exp), reads=[Dt], writes=[E])
            K.op(K.dve, lambda: nc.vector.tensor_tensor(out=Kh.ap[:], in0=kk.ap[:], in1=E.ap[:], op=ALU.mult), reads=[kk, E], writes=[Kh])
            E, Dt = nxtE()
            K.op(K.act, lambda: nc.scalar.activation(out=E.ap[:], in_=L16.ap[:], func=AF.Exp), reads=[L16], writes=[E])
            K.op(K.dve, lambda: nc.vector.tensor_tensor(out=Qs.ap[:], in0=q.ap[:], in1=E.ap[:], op=ALU.mult), reads=[q, E], writes=[Qs])
            for n in range(16):
                tp = tps[n // 8]
                K.op(K.pe, (lambda n=n, tp=tp: nc.tensor.transpose(tp.ap[:, (n % 8) * 128:(n % 8 + 1) * 128], Kh.ap[:, n * 128:(n + 1) * 128], C.ident.ap[:])),
                     reads=[Kh, C.ident], writes=[tp], sig=(n % 8 == 7))
            for i2 in range(2):
                K.op(K.act, (lambda i2=i2: nc.scalar.copy(out=Kht.ap[:, i2 * 8:(i2 + 1) * 8, :], in_=tps[i2].ap[:].rearrange("p (n c) -> p n c", c=128))),
                     reads=[tps[i2]], writes=[Kht])
            EB = [None] * 8
            def sub_exp(i):
                E, Dt = nxtE()
                EB[i] = E
                if i == 0:
                    K.op(K.dve, lambda: nc.vector.tensor_scalar(out=Dt.ap[:], in0=L.ap[:], scalar1=-1.0, scalar2=None, op0=ALU.mult), reads=[L], writes=[Dt])
                else:
                    K.op(K.dve, (lambda: nc.vector.tensor_tensor(out=v3(Dt), in0=v3(L)[:, :, 16 * i - 1:16 * i].to_broadcast([128, 16, 128]), in1=v3(L),
                                                                 op=ALU.subtract)), reads=[L], writes=[Dt])
                K.op(K.act, lambda: nc.scalar.activation(out=E.ap[:], in_=Dt.ap[:], func=AF.Exp), reads=[Dt], writes=[E])
            sub_exp(0)
            for i in range(8):
                ks = Ks[i % 2]
                if i + 1 < 8:
                    sub_exp(i + 1)
                E = EB[i]
                K.op(K.dve, (lambda ks=ks, E=E: nc.vector.scalar_tensor_tensor(out=ks.ap[:], in0=E.ap[:], scalar=1e30, in1=kk.ap[:], op0=ALU.min, op1=ALU.mult)),
                     reads=[E, kk], writes=[ks])
                for n in range(16):
                    sc = scps[n // 4]; cb0 = (n % 4) * 128 + 16 * i
                    K.op(K.pe, (lambda n=n, sc=sc, cb0=cb0, ks=ks, i=i: nc.tensor.matmul(sc.ap[:, cb0:cb0 + 16], ks.ap[:, n * 128:(n + 1) * 128],
                                                                                    Qs.ap[:, n * 128 + 16 * i:n * 128 + 16 * i + 16], start=True, stop=True)),
                         reads=[ks, Qs], writes=[sc], sig=(n == 15))
            for n4 in range(4):
                K.op(K.dve, (lambda n4=n4: nc.vector.tensor_tensor(out=scm.ap[:, n4 * 4:(n4 + 1) * 4, :], in0=scps[n4].ap[:].rearrange("p (n c) -> p n c", c=128),
                                                                   in1=m01.ap[:].rearrange("p (o c) -> p o c", o=1).to_broadcast([128, 4, 128]), op=ALU.mult)),
                     reads=[scps[n4], m01], writes=[scm])
            for n in range(16):
                oc = ops.ap[:, (n % 4) * 128:(n % 4 + 1) * 128]
                if n > 0:
                    K.op(K.pe, (lambda n=n, oc=oc: nc.tensor.matmul(oc, Sb.ap[:], Q0.ap[:, n * 128:(n + 1) * 128], start=True, stop=False)),
                         reads=[Sb, Q0], writes=[ops], sig=False)
                K.op(K.pe, (lambda n=n, oc=oc: nc.tensor.matmul(oc, vi.ap[:, n, :], scm.ap[:, n, :], start=(n == 0), stop=True)),
                     reads=[vi, scm], writes=[ops], sig=True)
                K.op(K.act, (lambda n=n, oc=oc: nc.scalar.copy(out=oT.ap[:, n * 128:(n + 1) * 128], in_=oc)), reads=[ops], writes=[oT])
                if n < 15:
                    sp_ = sps.ap[:, (n % 4) * 128:(n % 4 + 1) * 128]
                    K.op(K.pe, (lambda n=n, sp_=sp_: nc.tensor.matmul(sp_, Kht.ap[:, n, :], vi.ap[:, n, :], start=True, stop=True)),
                         reads=[Kht, vi], writes=[sps], sig=True)
                    if n == 0:
                        K.op(K.dve, (lambda sp_=sp_: nc.vector.tensor_copy(out=S32.ap[:], in_=sp_)), reads=[sps], writes=[S32])
                    else:
                        K.op(K.dve, (lambda n=n, sp_=sp_: nc.vector.scalar_tensor_tensor(out=S32.ap[:], in0=S32.ap[:], scalar=dec.ap[:, n:n + 1], in1=sp_,
                                                                                         op0=ALU.mult, op1=ALU.add)), reads=[S32, dec, sps], writes=[S32])
                    K.op(K.act, lambda: nc.scalar.copy(out=Sb.ap[:], in_=S32.ap[:]), reads=[S32], writes=[Sb])
            ohn, _ = PRM["hgn"]
            for tt in range(4):
                cs = slice(tt * 512, (tt + 1) * 512)
                K.op(K.act, (lambda cs=cs: nc.scalar.activation(out=sqb.ap[:], in_=oT.ap[:, cs], func=AF.Square)), reads=[oT], writes=[sqb])
                K.op(K.pe, lambda: nc.tensor.matmul(ops.ap[:], C.ones.ap[:], sqb.ap[:], start=True, stop=True), reads=[C.ones, sqb], writes=[ops], sig=True)
                K.op(K.act, lambda: nc.scalar.activation(out=rst.ap[:], in_=ops.ap[:], func=AF.Ln, bias=C.epsb.ap[:, 0:1], scale=1.0 / 128), reads=[ops, C.epsb], writes=[rst])
                K.op(K.act, lambda: nc.scalar.activation(out=rst.ap[:], in_=rst.ap[:], func=AF.Exp, scale=-0.5), reads=[rst], writes=[rst])
                K.op(K.dve, (lambda cs=cs: nc.vector.scalar_tensor_tensor(out=tmpn.ap[:], in0=oT.ap[:, cs], scalar=C.prm.ap[:, ohn:ohn + 1], in1=rst.ap[:],
                                                                         op0=ALU.mult, op1=ALU.mult)), reads=[oT, rst, C.prm], writes=[tmpn])
                b = bo[tt % 2]
                K.op(K.dve, (lambda cs=cs, b=b: nc.vector.tensor_tensor(out=b.ap[:], in0=tmpn.ap[:], in1=hgs.ap[:, cs], op=ALU.mult)), reads=[tmpn, hgs], writes=[b])
                K.dma(K.sp, ccT.ap[1024 + hd * 128:1024 + (hd + 1) * 128, cs], b.ap[:], reads=[b], writes=[ccT], owner=b)
        K.barrier()
        K.release([rm128, rm16, m01] + qs_ + lfs_ + kks_ + vis_ + hgss_ + bo)
        st.close()

    def lru_phase():
        st = contextlib.ExitStack()
        xbs = [K.sb(st, f"rxb{i}", [128, 2, T], F32) for i in range(2)]; gbs = [K.sb(st, f"rgb{i}", [128, 2, T], F32) for i in range(2)]
        yc = K.sb(st, "ryc", [128, 2, T], F32); ycb = K.sb(st, "rycb", [128, 2, T], BF16)
        r = K.sb(st, "rr", [128, 2, T], F32); ig = K.sb(st, "rig", [128, 2, T], F32)
        a = K.sb(st, "ra", [128, 2, T], F32); mu = K.sb(st, "rmu", [128, 2, T], F32)
        gt = K.sb(st, "rgt", [128, 2, T], F32)
        mo = K.sb(st, "rmo", [128, 2, T], BF16)
        wa = [K.sb(st, f"rwa{i}", [128, 2, 256], BF16) for i in range(2)]; wx = [K.sb(st, f"rwx{i}", [128, 2, 256], BF16) for i in range(2)]
        pss = [K.psum(st, f"rps{i}", [128, 512]) for i in range(8)]
        pi = 0
        ocw = [PRM[f"convw{t}"][0] for t in range(4)]; ocb = PRM["convb"][0]; oba = PRM["ba"][0]; obx = PRM["bx"][0]
        NB = LRU // 256

        def loads(nb):
            rows = slice(nb * 256, (nb + 1) * 256)
            K.dma(K.pool, wa[nb % 2].ap[:], w_ra.ap[nb], reads=[w_ra], writes=[wa[nb % 2]], owner=wa[nb % 2])
            K.dma(K.pool, wx[nb % 2].ap[:], w_rx.ap[nb], reads=[w_rx], writes=[wx[nb % 2]], owner=wx[nb % 2])
            K.dma(K.sp, xbs[nb % 2].ap[:], xbT.ap[rows, :].rearrange("(k p) t -> p k t", p=128), reads=[xbT], writes=[xbs[nb % 2]], owner=xbs[nb % 2])
            K.dma(K.sp, gbs[nb % 2].ap[:], gbT.ap[rows, :].rearrange("(k p) t -> p k t", p=128), reads=[gbT], writes=[gbs[nb % 2]], owner=gbs[nb % 2])

        loads(0)
        for nb in range(NB):
            rows = slice(nb * 256, (nb + 1) * 256)
            W1 = wa[nb % 2]; W2 = wx[nb % 2]; xb = xbs[nb % 2]; gb = gbs[nb % 2]
            if nb + 1 < NB:
                loads(nb + 1)
            for k in range(2):
                K.op(K.pool, (lambda k=k: nc.gpsimd.tensor_tensor(out=gt.ap[:, k, :], in0=gb.ap[:, k, :], in1=gb.ap[:, k, :], op=ALU.mult)), reads=[gb], writes=[gt])
                K.op(K.pool, (lambda k=k: nc.gpsimd.tensor_scalar(out=gt.ap[:, k, :], in0=gt.ap[:, k, :], scalar1=0.044715, scalar2=1.0, op0=ALU.mult, op1=ALU.add)),
                     reads=[gt], writes=[gt])
                K.op(K.pool, (lambda k=k: nc.gpsimd.tensor_tensor(out=gt.ap[:, k, :], in0=gt.ap[:, k, :], in1=gb.ap[:, k, :], op=ALU.mult)), reads=[gt, gb], writes=[gt])
            for k in range(2):
                ch = nb * 2 + k
                K.op(K.act, (lambda k=k, ch=ch: nc.scalar.activation(out=yc.ap[:, k, :], in_=xb.ap[:, k, :], func=AF.Identity,
                                                                     scale=C.prm.ap[:, ocw[0] + ch:ocw[0] + ch + 1], bias=C.prm.ap[:, ocb + ch:ocb + ch + 1])),
                     reads=[xb, C.prm], writes=[yc])
                for tap in range(1, 4):
                    K.op(K.dve, (lambda k=k, ch=ch, tap=tap: nc.vector.scalar_tensor_tensor(out=yc.ap[:, k, tap:], in0=xb.ap[:, k, 0:T - tap],
                                                                                            scalar=C.prm.ap[:, ocw[tap] + ch:ocw[tap] + ch + 1], in1=yc.ap[:, k, tap:],
                                                                                            op0=ALU.mult, op1=ALU.add)), reads=[xb, yc, C.prm], writes=[yc])
                K.op(K.act, (lambda k=k: nc.scalar.copy(out=ycb.ap[:, k, :], in_=yc.ap[:, k, :])), reads=[yc], writes=[ycb])
            for k in range(2):
                ch = nb * 2 + k
                for tt in range(4):
                    cs = slice(tt * 512, (tt + 1) * 512)
                    for (W, dstt, ob) in ((W1, r, oba), (W2, ig, obx)):
                        ps = pss[pi % 8]; pi += 1
                        for ic in range(2):
                            K.op(K.pe, (lambda W=W, ps=ps, ic=ic, k=k, cs=cs: nc.tensor.matmul(ps.ap[:], W.ap[:, ic, k * 128:(k + 1) * 128], ycb.ap[:, ic, cs],
                                                                                           start=(ic == 0), stop=(ic == 1))), reads=[W, ycb], writes=[ps], sig=(ic == 1))
                        K.op(K.act, (lambda ps=ps, dstt=dstt, ob=ob, k=k, cs=cs, ch=ch: nc.scalar.activation(out=dstt.ap[:, k, cs], in_=ps.ap[:], func=AF.Sigmoid,
                                                                                                        bias=C.prm.ap[:, ob + ch:ob + ch + 1])),
                             reads=[ps, C.prm], writes=[dstt])
            for k in range(2):
                K.op(K.act, (lambda k=k: nc.scalar.activation(out=gt.ap[:, k, :], in_=gt.ap[:, k, :], func=AF.Sigmoid, scale=1.5957691216057308)), reads=[gt], writes=[gt])
            for k in range(2):
                K.op(K.pool, (lambda k=k: nc.gpsimd.tensor_tensor(out=gt.ap[:, k, :], in0=gt.ap[:, k, :], in1=gb.ap[:, k, :], op=ALU.mult)), reads=[gt, gb], writes=[gt])
            for k in range(2):
                scp = C.drv.ap[:, 24 + nb * 2 + k:25 + nb * 2 + k]
                K.op(K.act, (lambda k=k, scp=scp: nc.scalar.activation(out=a.ap[:, k, :], in_=r.ap[:, k, :], func=AF.Exp, scale=scp)), reads=[r, C.drv], writes=[a])
            for k in range(2):
                K.op(K.act, (lambda k=k: nc.scalar.activation(out=mu.ap[:, k, :], in_=a.ap[:, k, :], func=AF.Square)), reads=[a], writes=[mu])
            for k in range(2):
                K.op(K.dve, (lambda k=k: nc.vector.tensor_scalar(out=mu.ap[:, k, :], in0=mu.ap[:, k, :], scalar1=1.0, scalar2=-1.0, op0=ALU.min, op1=ALU.mult)), reads=[mu], writes=[mu])
            for k in range(2):
                K.op(K.act, (lambda k=k: nc.scalar.activation(out=mu.ap[:, k, :], in_=mu.ap[:, k, :], func=AF.Sqrt, bias=1.0, scale=1.0)), reads=[mu], writes=[mu])
            for k in range(2):
                K.op(K.dve, (lambda k=k: nc.vector.memset(mu.ap[:, k, 0:1], 1.0)), reads=[], writes=[mu])
                K.op(K.dve, (lambda k=k: nc.vector.tensor_tensor(out=mu.ap[:, k, :], in0=mu.ap[:, k, :], in1=ig.ap[:, k, :], op=ALU.mult)), reads=[mu, ig], writes=[mu])
                K.op(K.dve, (lambda k=k: nc.vector.tensor_tensor(out=mu.ap[:, k, :], in0=mu.ap[:, k, :], in1=yc.ap[:, k, :], op=ALU.mult)), reads=[mu, yc], writes=[mu])
            for k in range(2):
                K.op(K.dve, (lambda k=k: nc.vector.tensor_tensor_scan(out=r.ap[:, k, :], data0=a.ap[:, k, :], data1=mu.ap[:, k, :], initial=0.0, op0=ALU.mult, op1=ALU.add)),
                     reads=[a, mu], writes=[r])
            for k in range(2):
                K.op(K.pool, (lambda k=k: nc.gpsimd.tensor_tensor(out=mo.ap[:, k, :], in0=gt.ap[:, k, :], in1=r.ap[:, k, :], op=ALU.mult)), reads=[gt, r], writes=[mo])
            K.dma(K.sp, mxT.ap[rows, :].rearrange("(k p) t -> p k t", p=128), mo.ap[:], reads=[mo], writes=[mxT], owner=mo)
        K.barrier()
        K.release(xbs + gbs + [mo] + wa + wx)
        st.close()

    token_phase("p0", 0, False, inproj_even)
    if stage >= 2:
        attention_phase()
    if stage >= 3:
        hgrn2_phase()
    if stage >= 4:
        token_phase("p1", 0, True, inproj_odd)
    if stage >= 5:
        lru_phase()
    if stage >= 6:
        token_phase("p2", 1, True, None)
    K.barrier()
    gs.close()
    es.close()
    return nc


def host_inputs(inp):
    f = lambda a: np.ascontiguousarray(np.asarray(a, dtype=np.float32))
    sh = {}
    wi = f(inp["w_in_even"][0])
    names = ["w_sq", "w_sk", "w_sv", "w_hq", "w_hf", "w_hi", "w_hg"]
    for i, nm in enumerate(names):
        sh[nm] = _wl(wi[:, i * 1024:(i + 1) * 1024], 512)
    sh["w_oe"] = _wl(f(inp["w_out_even"][0]), 512)
    sh["w_io"] = _wl(f(inp["w_in_odd"][0]), 512)
    sh["w_oo"] = _wl(f(inp["w_out_odd"][0]), 256)
    for i in range(2):
        gu = f(inp["w_gate_up"][i])
        g = gu[:, :DFF].reshape(D, FC // 2, 256); u = gu[:, DFF:].reshape(D, FC // 2, 256)
        sh[f"w_gu{i}"] = _wl(np.concatenate([g, u], axis=2).reshape(D, FC * 256), 512)
        sh[f"w_dn{i}"] = _wl(f(inp["w_down"][i]), 128)
        sh[f"w_pg{i}"] = _wl(f(inp["w_ple_gate"][i]), 512)
        sh[f"w_pu{i}"] = _wl(f(inp["w_ple_up"][i]), 2048)
    for nm, key in (("w_ra", "rg_wa"), ("w_rx", "rg_wx")):
        w = f(inp[key][0])
        sh[nm] = np.ascontiguousarray(w.reshape(11, 2, 128, 256).transpose(0, 2, 1, 3))
    prm = np.zeros((128, NPRM), np.float32)
    def put(name, arr):
        off, n = PRM[name]; prm[:, off:off + n] = arr
    for i in range(2):
        for nm in ("mix_pre_g", "mix_post_g", "ffn_pre_g", "ffn_post_g", "ple_norm_g"):
            put(f"{nm}{i}", _cols(f(inp[nm][i])))
    put("lb0", _cols(f(inp["hg_lb_logits"][0]))); put("lb1", _cols(f(inp["hg_lb_logits"][1])))
    put("hgn", f(inp["hg_norm_g"][0]).reshape(128, 1))
    for tap in range(4):
        put(f"convw{tap}", _cols(f(inp["conv_w"][0, tap])))
    put("convb", _cols(f(inp["conv_b"][0]))); put("ba", _cols(f(inp["rg_ba"][0]).reshape(-1))); put("bx", _cols(f(inp["rg_bx"][0]).reshape(-1)))
    put("lam", _cols(f(inp["rg_lambda"][0])))
    sh["prm"] = prm
    for k, v in _consts().items():
        sh["c_" + k] = v
    return sh


def kernel(**inp):
    sh = host_inputs(inp)
    x = np.asarray(inp["x"], np.float32); p = np.asarray(inp["p"], np.float32)
    nc = build()
    in_maps = []
    for b in range(8):
        m = dict(sh)
        m["xT"] = np.ascontiguousarray(x[b].T)
        m["pT"] = np.ascontiguousarray(p[:, b].transpose(0, 2, 1))
        in_maps.append(m)
    res = run_bass_kernel_spmd(nc, in_maps, core_ids=list(range(8)))
    out = np.stack([np.ascontiguousarray(res.results[b]["outT"].T) for b in range(8)], axis=0)
    return out.astype(np.float32)
```
